# Optimizing a Trainium2 kernel written in Bass

```python
import math
import jax, jax.numpy as jnp
from jax import lax
import numpy as np

D_MODEL = 2048
BATCH = 16
SEQ = 256
DEPTH = 4
DEC_BATCH = 4
DEC_SEQ = 4096
PAST_LEN = 256

GRID_W = 64
D_MIX = D_MODEL
D_A = D_MIX // 2
HEAD_A = 64
N_HEADS_A = D_A // HEAD_A
D_B = D_MIX - D_A
N_GROUPS_B = 8
HEAD_B = D_B // N_GROUPS_B
CHUNK = 128
DECAY_LORA = 64
A_LORA = 64
GATE_LORA = 160
C_SHIFT = 3 * D_A + DECAY_LORA + A_LORA + GATE_LORA
P_IN = C_SHIFT + 2 * D_B
D_FF = -(-8 * D_MODEL // (3 * 256)) * 256
RMS_EPS = 1e-6
GN_EPS = HEAD_A * 1e-5
LN_EPS = 1e-5
F32 = jnp.float32

kernel_name = 'hymba_rwkv7_gmlp_flow_step'


def _rmsnorm(x, g):
    xf = x.astype(F32)
    y = xf * lax.rsqrt(jnp.mean(xf * xf, axis=-1, keepdims=True) + RMS_EPS)
    return (y * g).astype(x.dtype)


def _shift_context(p, mu):
    prev = jnp.pad(p[:, :-1], ((0, 0), (1, 0), (0, 0)))
    nxt = jnp.pad(p[:, 1:], ((0, 0), (0, 1), (0, 0)))
    return p + mu[0] * (prev - p) + mu[1] * (nxt - p)


def _shift_grid(p, mu):
    B, T, C = p.shape
    rows = T // GRID_W
    g = p.reshape(B, rows, GRID_W, C)
    left = jnp.pad(g[:, :, :-1], ((0, 0), (0, 0), (1, 0), (0, 0)))
    right = jnp.pad(g[:, :, 1:], ((0, 0), (0, 0), (0, 1), (0, 0)))
    up = jnp.pad(g[:, :-1], ((0, 0), (1, 0), (0, 0), (0, 0)))
    down = jnp.pad(g[:, 1:], ((0, 0), (0, 1), (0, 0), (0, 0)))
    out = g + mu[0] * (left - g) + mu[1] * (right - g) + mu[2] * (up - g) + mu[3] * (down - g)
    return out.reshape(B, T, C)


def _delta_scan(r, w, k, v, aa, bb, s0, reverse):
    xs = tuple(jnp.moveaxis(t, 1, 0) for t in (r, w, k, v, aa, bb))

    def step(S, inp):
        r_t, w_t, k_t, v_t, a_t, b_t = inp
        sa = jnp.einsum('bhvk,bhk->bhv', S, a_t)
        S = S * w_t[:, :, None, :] + sa[..., None] * b_t[:, :, None, :] + v_t[..., None] * k_t[:, :, None, :]
        return S, jnp.einsum('bhvk,bhk->bhv', S, r_t)

    s_fin, ys = lax.scan(step, s0, xs, reverse=reverse)
    return jnp.moveaxis(ys, 0, 1), s_fin


def _group_norm(y, w, b):
    mean = jnp.mean(y, axis=-1, keepdims=True)
    yc = y - mean
    var = jnp.mean(yc * yc, axis=-1, keepdims=True)
    return yc * lax.rsqrt(var + GN_EPS) * w + b


def _rwkv7_bidir(p, s0, lp):
    dtype = p.dtype
    p = p.astype(F32)
    B, T, _ = p.shape
    r, k, v, wd, ad, gd = jnp.split(
        p, [D_A, 2 * D_A, 3 * D_A, 3 * D_A + DECAY_LORA, 3 * D_A + DECAY_LORA + A_LORA], axis=-1)
    heads = lambda t: t.reshape(B, T, N_HEADS_A, HEAD_A)
    kk = heads(k * lp['k_k'])
    kk = kk * lax.rsqrt(jnp.maximum(jnp.sum(kk * kk, axis=-1, keepdims=True), 1e-24))
    gate = jax.nn.sigmoid(gd) @ lp['g2']
    rh, vh = heads(r), heads(v)
    wt = jnp.tanh(wd)
    out = jnp.zeros((B, T, N_HEADS_A, HEAD_A), F32)
    finals = []
    for d in range(2):
        w_log = -jax.nn.softplus(-(lp['w0'][d] + wt @ lp['w2'][d])) - 0.5
        decay = heads(jnp.exp(-jnp.exp(w_log)))
        a = jax.nn.sigmoid(lp['a0'][d] + ad @ lp['a2'][d])
        kd = heads(k * (1.0 + (a - 1.0) * lp['k_a']))
        y, s_fin = _delta_scan(rh, decay, kd, vh, -kk, kk * heads(a),
                               s0[:, d].astype(F32), reverse=(d == 1))
        out = out + _group_norm(y, lp['gn_w'], lp['gn_b']) \
            + jnp.sum(rh * kd * lp['r_k'], axis=-1, keepdims=True) * vh
        finals.append(s_fin)
    o = (out.reshape(B, T, D_A) * gate).astype(dtype)
    return o, jnp.stack(finals, axis=1)


def _chunk_gmlp(p, lp):
    dtype = p.dtype
    B, T, _ = p.shape
    u, v = jnp.split(p, 2, axis=-1)
    v = v.astype(F32).reshape(B, T // CHUNK, CHUNK, N_GROUPS_B, HEAD_B)
    mean = jnp.mean(v, axis=-1, keepdims=True)
    vc = v - mean
    v = vc * lax.rsqrt(jnp.mean(vc * vc, axis=-1, keepdims=True) + LN_EPS) * lp['gmlp_ln_g'] + lp['gmlp_ln_b']
    s = jnp.einsum('gij,bnjgd->bnigd', lp['w_spatial'], v) + lp['b_spatial'].T[:, :, None]
    return (u * s.reshape(B, T, D_B)).astype(dtype)


def _layer(x, mod, s0, shift_fn, lp):
    sh1, sc1, g1, sh2, sc2, g2 = jnp.split(mod, 6, axis=-1)
    h = _rmsnorm(x, lp['norm1_g']) * (1.0 + sc1) + sh1
    p = h @ lp['w_in']
    o_a, s_fin = _rwkv7_bidir(shift_fn(p[..., :C_SHIFT], lp['mu_shift']), s0, lp)
    o_b = _chunk_gmlp(jax.nn.gelu(p[..., C_SHIFT:]), lp)
    x = x + g1 * (jnp.concatenate([o_a, o_b], axis=-1) @ lp['w_out'])
    h = _rmsnorm(x, lp['norm2_g']) * (1.0 + sc2) + sh2
    ff = (jax.nn.silu(h @ lp['w_ffn_gate']) * (h @ lp['w_ffn_up'])) @ lp['w_ffn_down']
    return x + g2 * ff, s_fin


def setup_inputs(seed: int = 0) -> dict:
    key = jax.random.key(seed)
    ks = jax.random.split(key, 32)
    nrm = lambda k, shape, s: jax.random.normal(k, shape, F32) * s
    L = DEPTH
    return {
        'x_prompt': nrm(ks[0], (BATCH, SEQ, D_MODEL), 1.0),
        'x_sample': nrm(ks[1], (DEC_BATCH, DEC_SEQ, D_MODEL), 1.0),
        'state_rwkv': nrm(ks[2], (DEC_BATCH, DEPTH, 2, N_HEADS_A, HEAD_A, HEAD_A), 1.0),
        'c': nrm(ks[3], (DEC_BATCH, D_MODEL), 1.0),
        'c_ctx': nrm(ks[4], (D_MODEL,), 1.0),
        'w_mod': nrm(ks[5], (L, D_MODEL, 6 * D_MODEL), 0.5 * D_MODEL ** -0.5),
        'b_mod': nrm(ks[6], (L, 6 * D_MODEL), 0.02),
        'norm1_g': 1.0 + nrm(ks[7], (L, D_MODEL), 0.05),
        'w_in': nrm(ks[8], (L, D_MODEL, P_IN), D_MODEL ** -0.5),
        'mu_shift': jax.random.uniform(ks[9], (L, 4, C_SHIFT), F32, 0.0, 0.5),
        'w0': jax.random.uniform(ks[10], (L, 2, D_A), F32, -4.0, 1.0),
        'w2': nrm(ks[11], (L, 2, DECAY_LORA, D_A), 0.5 * DECAY_LORA ** -0.5),
        'a0': nrm(ks[12], (L, 2, D_A), 0.5),
        'a2': nrm(ks[13], (L, 2, A_LORA, D_A), 0.5 * A_LORA ** -0.5),
        'g2': nrm(ks[14], (L, GATE_LORA, D_A), GATE_LORA ** -0.5),
        'k_k': 0.85 + nrm(ks[15], (L, D_A), 0.05),
        'k_a': 1.0 + nrm(ks[16], (L, D_A), 0.05),
        'r_k': nrm(ks[17], (L, N_HEADS_A, HEAD_A), 0.1),
        'gn_w': 1.0 + nrm(ks[18], (L, N_HEADS_A, HEAD_A), 0.05),
        'gn_b': nrm(ks[19], (L, N_HEADS_A, HEAD_A), 0.02),
        'gmlp_ln_g': 1.0 + nrm(ks[20], (L, N_GROUPS_B, HEAD_B), 0.05),
        'gmlp_ln_b': nrm(ks[21], (L, N_GROUPS_B, HEAD_B), 0.02),
        'w_spatial': nrm(ks[22], (L, N_GROUPS_B, CHUNK, CHUNK), CHUNK ** -0.5),
        'b_spatial': 1.0 + nrm(ks[23], (L, N_GROUPS_B, CHUNK), 0.05),
        'w_out': nrm(ks[24], (L, D_MIX, D_MODEL), D_MIX ** -0.5),
        'norm2_g': 1.0 + nrm(ks[25], (L, D_MODEL), 0.05),
        'w_ffn_gate': nrm(ks[26], (L, D_MODEL, D_FF), D_MODEL ** -0.5),
        'w_ffn_up': nrm(ks[27], (L, D_MODEL, D_FF), D_MODEL ** -0.5),
        'w_ffn_down': nrm(ks[28], (L, D_FF, D_MODEL), D_FF ** -0.5),
        'final_norm_g': 1.0 + nrm(ks[29], (D_MODEL,), 0.05),
    }


def reference(x_prompt, x_sample, state_rwkv, c, c_ctx, w_mod, b_mod, norm1_g, w_in, mu_shift,
              w0, w2, a0, a2, g2, k_k, k_a, r_k, gn_w, gn_b, gmlp_ln_g, gmlp_ln_b, w_spatial,
              b_spatial, w_out, norm2_g, w_ffn_gate, w_ffn_up, w_ffn_down, final_norm_g):
    x_ctx = x_prompt
    x_lat = x_sample
    s0_ctx = jnp.zeros((x_prompt.shape[0], 2, N_HEADS_A, HEAD_A, HEAD_A), F32)
    ctx_states = []
    for l in range(DEPTH):
        lp = {
            'norm1_g': norm1_g[l], 'w_in': w_in[l], 'mu_shift': mu_shift[l],
            'w0': w0[l], 'w2': w2[l], 'a0': a0[l], 'a2': a2[l], 'g2': g2[l],
            'k_k': k_k[l], 'k_a': k_a[l], 'r_k': r_k[l], 'gn_w': gn_w[l], 'gn_b': gn_b[l],
            'gmlp_ln_g': gmlp_ln_g[l], 'gmlp_ln_b': gmlp_ln_b[l],
            'w_spatial': w_spatial[l], 'b_spatial': b_spatial[l], 'w_out': w_out[l],
            'norm2_g': norm2_g[l], 'w_ffn_gate': w_ffn_gate[l], 'w_ffn_up': w_ffn_up[l],
            'w_ffn_down': w_ffn_down[l],
        }
        mod_ctx = (jax.nn.silu(c_ctx) @ w_mod[l] + b_mod[l])[None, None, :]
        mod_lat = (jax.nn.silu(c) @ w_mod[l] + b_mod[l])[:, None, :]
        x_ctx, s_ctx = _layer(x_ctx, mod_ctx, s0_ctx, _shift_context, lp)
        ctx_states.append(s_ctx)
        x_lat, _ = _layer(x_lat, mod_lat, state_rwkv[:, l], _shift_grid, lp)
    new_state_rwkv = jnp.stack(ctx_states, axis=1).astype(x_prompt.dtype)
    y_prompt = _rmsnorm(x_ctx, final_norm_g)
    y_sample = _rmsnorm(x_lat, final_norm_g)
    return (y_prompt, y_sample, new_state_rwkv)
```

```python
import contextlib
import numpy as np
import concourse.bass as bass
import concourse.mybir as mybir
from concourse.bass_utils import run_bass_kernel_spmd

F32 = mybir.dt.float32
BF16 = mybir.dt.bfloat16
AF = mybir.ActivationFunctionType
ALU = mybir.AluOpType
AX = mybir.AxisListType

D = 2048
NK = 16
CS = 3360
PIN = 5408
DFF = 5632
NF = 44
NH = 16
HD = 64
SEQP = 256
NPR = 2
RMS_EPS = 1e-6
GN_EPS = 64 * 1e-5
LN_EPS = 1e-5
CDEC = 0.6065306597126334

EPOCH = 12000
N_DMA_SEMS = 24


class Buf:
    __slots__ = ("name", "writers", "readers", "excl")

    def __init__(self, name="", excl=False):
        self.name = name
        self.writers = {}
        self.readers = {}
        self.excl = excl


class Prog:
    def __init__(self, nc):
        self.nc = nc
        self.es = contextlib.ExitStack()
        self.eng = {"pe": nc.tensor, "act": nc.scalar, "dve": nc.vector,
                    "pool": nc.gpsimd, "sp": nc.sync}
        self.sems = {e: [] for e in self.eng}
        self.cnt = {e: 0 for e in self.eng}
        self.seen = {e: {} for e in self.eng}
        self.dma_sems = {}
        self.dma_vals = {}
        self.dma_next = {}
        self.all_dma = {}
        self.n_ins = 0
        for q in ("sp", "pool", "cc"):
            n = N_DMA_SEMS if q != "cc" else 2
            self.dma_sems[q] = [self.es.enter_context(nc.semaphore(f"dq_{q}_{i}")) for i in range(n)]
            self.dma_vals[q] = [0] * n
            self.dma_next[q] = 0
        for e in self.eng:
            self._new_epoch(e)

    def _new_epoch(self, e):
        s = self.es.enter_context(self.nc.semaphore(f"s_{e}_{len(self.sems[e])}"))
        self.sems[e].append(s)
        self.cnt[e] = 0

    def _wait_list(self, e, events):
        need = {}
        for ev in events:
            if ev is None:
                continue
            if ev[0] == "eng":
                _, src, ep, c = ev
                if src == e and e == "pe":
                    continue
                key = ("eng", src, ep)
                sem = self.sems[src][ep]
            else:
                _, q, i, c = ev
                key = ("dma", q, i)
                sem = self.dma_sems[q][i]
            if self.seen[e].get(key, 0) >= c:
                continue
            if key not in need or need[key][1] < c:
                need[key] = (sem, c)
        for key, (sem, c) in need.items():
            self.seen[e][key] = c
        return list(need.values())

    @staticmethod
    def _deps(reads, writes):
        evs = []
        for b in reads:
            evs.extend(b.writers.values())
        for b in writes:
            evs.extend(b.writers.values())
            evs.extend(b.readers.values())
        return evs

    @staticmethod
    def _record(ev, key, reads, writes):
        for b in writes:
            b.writers = {key: ev}
            b.readers = {}
        for b in reads:
            b.readers[key] = ev

    def op(self, e, fn, reads=(), writes=(), inc=True, rec_reads=None):
        if any(b.excl for b in reads):
            writes = list(writes) + [b for b in reads if b.excl]
            reads = [b for b in reads if not b.excl]
        waits = self._wait_list(e, self._deps(reads, writes))
        eng = self.eng[e]
        for sem, c in waits[:-1]:
            eng.wait_ge(sem, c)
        ins = fn()
        if waits:
            ins._wait_ge(waits[-1][0], waits[-1][1])
        self.n_ins += 1
        if not inc:
            return None
        if self.cnt[e] >= EPOCH:
            self._new_epoch(e)
        self.cnt[e] += 1
        ep = len(self.sems[e]) - 1
        ins.then_inc(self.sems[e][ep], 1)
        ev = ("eng", e, ep, self.cnt[e])
        self._record(ev, e, reads if rec_reads is None else rec_reads, writes)
        return ev

    def dma(self, q, out, in_, reads=(), writes=(), join=(), **kw):
        eng = self.eng[q]
        i = self.dma_next[q]
        self.dma_next[q] = (i + 1) % N_DMA_SEMS
        prev = ("dma", q, i, self.dma_vals[q][i]) if self.dma_vals[q][i] else None
        waits = self._wait_list(q, self._deps(reads, writes) + [prev])
        for sem, c in waits:
            eng.wait_ge(sem, c)
        self.dma_vals[q][i] += 16
        v = self.dma_vals[q][i]
        eng.dma_start(out=out, in_=in_, **kw).then_inc(self.dma_sems[q][i], 16)
        self.n_ins += 1
        ev = ("dma", q, i, v)
        self.all_dma[(q, i)] = ev
        self._record(ev, ("dma", q, i, v), reads, writes)
        for b in join:
            b.writers[("dma", q, i, v)] = ev
        return ev

    def coll(self, ins_ap, outs_ap, groups, reads=(), writes=()):
        q = "cc"
        eng = self.eng["pool"]
        i = self.dma_next[q]
        self.dma_next[q] = (i + 1) % len(self.dma_sems[q])
        prev = ("dma", q, i, self.dma_vals[q][i]) if self.dma_vals[q][i] else None
        for sem, c in self._wait_list("pool", self._deps(reads, writes) + [prev]):
            eng.wait_ge(sem, c)
        self.dma_vals[q][i] += 1
        v = self.dma_vals[q][i]
        eng.collective_compute("AllGather", ALU.bypass, replica_groups=groups, ins=[ins_ap], outs=[outs_ap]) \
            .then_inc(self.dma_sems[q][i], 1)
        self.n_ins += 1
        ev = ("dma", q, i, v)
        self.all_dma[(q, i)] = ev
        self._record(ev, ("dma", q, i, v), reads, writes)
        return ev

    def barrier(self):
        evs = []
        for e in self.eng:
            ep = len(self.sems[e]) - 1
            if self.cnt[e] > 0:
                evs.append(("eng", e, ep, self.cnt[e]))
            if ep > 0:
                evs.append(("eng", e, ep - 1, EPOCH))
        evs.extend(self.all_dma.values())
        for e in self.eng:
            for sem, c in self._wait_list(e, evs):
                self.eng[e].wait_ge(sem, c)


class _Stop(Exception):
    pass


def build(L, TS, dbg=False, stop=None, pair=False, ncores=8):
    nc = bass.Bass("TRN2", target_bir_lowering=False)
    P = Prog(nc)
    TT = TS + NPR * SEQP
    NT5 = TT // 512
    NCH = TT // 128
    NW = NT5
    c_ = CDEC

    def din(name, shape, dt=F32):
        return nc.dram_tensor(name, list(shape), dt, kind="ExternalInput").ap()

    def dscr(name, shape, dt=F32):
        kind = "ExternalOutput" if dbg else "Internal"
        return nc.dram_tensor(name, list(shape), dt, kind=kind).ap()

    xT_in = din("xT", [D, TT])
    cvec = din("cvec", [128, NK, 2])
    state0 = din("state0", [L, 2, NH, HD, HD])
    w_mod = din("w_mod", [L, D, 6 * D])
    bmodT = din("bmodT", [128, L, 96])
    n1T = din("n1T", [128, L, NK])
    n2T = din("n2T", [128, L, NK])
    nfT = din("nfT", [128, NK])
    w_in = din("w_in", [L, D, PIN])
    muT = din("muT", [128, L, 27, 4])
    w0T = din("w0T", [128, L, 2, 8])
    a0T = din("a0T", [128, L, 2, 8])
    w2 = din("w2", [L, 2, 64, 1024])
    a2 = din("a2", [L, 2, 64, 1024])
    g2 = din("g2", [L, 160, 1024])
    kkT = din("kkT", [128, L, 8])
    kaT = din("kaT", [128, L, 8])
    rkT = din("rkT", [128, L, 8])
    gnw = din("gnw", [L, 1024])
    gnb = din("gnb", [L, 1024])
    lng = din("lng", [L, 1024])
    lnb = din("lnb", [L, 1024])
    wsT = din("wsT", [L, 128, 8, 128])
    bsp = din("bsp", [L, 1, 1024])
    w_out = din("w_out", [L, D, D])
    w_g = din("w_g", [L, D, DFF])
    w_u = din("w_u", [L, D, DFF])
    w_d = din("w_d", [L, DFF, D])
    msel = din("msel", [128, 2]) if pair else None
    GROUPS = [[2 * i, 2 * i + 1] for i in range(ncores // 2)]

    yT = nc.dram_tensor("yT", [D, TT], F32, kind="ExternalOutput").ap()
    nst = nc.dram_tensor("nst", [NPR, L, 2, NH, HD, HD], F32, kind="ExternalOutput").ap()

    XRES = dscr("XRES", [D, TT])
    PT = dscr("PT", [27 * 128, TT])
    UTd = dscr("UTd", [1024, TT], BF16)
    VTd = dscr("VTd", [TT, 1024])
    OT = dscr("OT", [D, TT], BF16)
    SCd = dscr("SCd", [NCH, 2, 8, 128, 4, 128], BF16)
    VBd = dscr("VBd", [NCH, 8, 128, 128], BF16)
    BTd = dscr("BTd", [1024, TT])
    GTd = dscr("GTd", [1024, TT])
    YGd = dscr("YGd", [2, TT, 1024])
    if pair:
        HSEND = nc.dram_tensor("HSEND", [27 * 128, 64], F32).ap()
        HGATH = nc.dram_tensor("HGATH", [2 * 27 * 128, 64], F32).ap()
        SSEND = nc.dram_tensor("SSEND", [128, 512], F32).ap()
        SGATH = nc.dram_tensor("SGATH", [2 * 128, 512], F32).ap()
        b_hsend, b_hgath, b_ssend, b_sgath = Buf(), Buf(), Buf(), Buf()

    XB = [[Buf() for _ in range(NT5)] for _ in range(NK)]
    PTB = [[Buf() for _ in range(NT5)] for _ in range(27)]
    UTB = [[Buf() for _ in range(NT5)] for _ in range(8)]
    VTB = [[Buf() for _ in range(8)] for _ in range(NT5)]
    OTB = [[Buf() for _ in range(NT5)] for _ in range(2)]
    SCB = [[[Buf() for _ in range(8)] for _ in range(2)] for _ in range(NW)]
    VBB = [[Buf() for _ in range(8)] for _ in range(NW)]
    BTB = [[Buf() for _ in range(8)] for _ in range(NW)]
    GTB = [[Buf() for _ in range(8)] for _ in range(NW)]
    YGB = [[Buf() for _ in range(NCH)] for _ in range(2)]

    es = P.es

    uid = {"n": 0}

    def sb(name, shape, dt=F32, ctx=None):
        uid["n"] += 1
        return (ctx or es).enter_context(nc.sbuf_tensor(f"{name}_{uid['n']}", list(shape), dt))

    PB = [es.enter_context(nc.psum_tensor(f"pb{i}", [128, 512], F32)) for i in range(8)]
    PBB = [Buf(f"pb{i}", excl=True) for i in range(8)]

    ones_f = sb("ones_f", [128, 128]); b_ones = Buf()
    blk_f = sb("blk_f", [128, 128]); b_blk = Buf()
    id_f = sb("id_f", [128, 128]); b_idf = Buf()
    id_b = sb("id_b", [128, 128], BF16); b_idb = Buf()
    ones_row = sb("ones_row", [1, 128], BF16); b_onesrow = Buf()
    MASK = [sb(f"mask{d}", [128, 512]) for d in range(2)]; b_mask = [Buf(), Buf()]
    NMASK = [sb(f"nmask{d}", [128, 128]) for d in range(2)]; b_nmask = [Buf(), Buf()]
    RST = sb("rst", [128, 512]); b_rst = Buf()

    V_, G_, A_, S_ = nc.vector, nc.gpsimd, nc.scalar, nc.sync

    P.op("dve", lambda: V_.memset(ones_f[:], 1.0), writes=[b_ones])
    P.op("dve", lambda: V_.memset(ones_row[:], 1.0), writes=[b_onesrow])
    P.op("dve", lambda: V_.memset(blk_f[:], 0.0), writes=[b_blk])
    P.op("dve", lambda: V_.memset(blk_f[0:64, 0:64], 1.0), reads=[b_blk], writes=[b_blk])
    P.op("dve", lambda: V_.memset(blk_f[64:128, 64:128], 1.0), reads=[b_blk], writes=[b_blk])
    P.op("dve", lambda: V_.memset(id_f[:], 0.0), writes=[b_idf])
    P.op("pool", lambda: G_.affine_select(out=id_f[:], in_=id_f[:], pattern=[[-1, 128]], compare_op=ALU.not_equal,
                                          fill=1.0, base=0, channel_multiplier=1), reads=[b_idf], writes=[b_idf])
    P.op("dve", lambda: V_.tensor_copy(id_b[:], id_f[:]), reads=[b_idf], writes=[b_idb])

    def tri(dst, cm, coef, base, cmp, bufs):
        P.op("dve", lambda: V_.memset(dst, 1.0), writes=bufs)
        P.op("pool", lambda: G_.affine_select(out=dst, in_=dst, pattern=[[coef, 128]], compare_op=cmp,
                                              fill=0.0, base=base, channel_multiplier=cm), reads=bufs, writes=bufs)
    for q in range(4):
        strict = (q % 2 == 0)
        tri(MASK[0][:, q * 128:(q + 1) * 128], -1, 1, 0, ALU.is_gt if strict else ALU.is_ge, [b_mask[0]])
        tri(MASK[1][:, q * 128:(q + 1) * 128], 1, -1, 0, ALU.is_gt if strict else ALU.is_ge, [b_mask[1]])
    tri(NMASK[0][:], 1, -1, 0, ALU.is_gt, [b_nmask[0]])
    tri(NMASK[1][:], -1, 1, 0, ALU.is_gt, [b_nmask[1]])
    P.op("dve", lambda: V_.memset(RST[:], 1.0), writes=[b_rst])
    for c in range(4):
        P.op("dve", lambda c=c: V_.memset(RST[:, c * 128:c * 128 + 1], 0.0), reads=[b_rst], writes=[b_rst])

    MODT = sb("modt", [128, L, 96, 2]); b_modt = Buf()
    SCL = sb("scl", [128, L, 2, NK, 2]); b_scl = Buf()
    N1 = sb("n1", [128, L, NK]); N2 = sb("n2", [128, L, NK]); NFt = sb("nf", [128, NK]); b_n = Buf()
    MU = sb("mu", [128, L, 27, 6]); b_mu = Buf()
    W0 = sb("w0", [128, L, 2, 8]); A0 = sb("a0", [128, L, 2, 8])
    KKp = sb("kkp", [128, L, 8]); KAp = sb("kap", [128, L, 8]); RKp = sb("rkp", [128, L, 8]); b_par = Buf()
    BMT = sb("bmt", [128, L, 96])
    CV = sb("cv", [128, NK, 2]); CSb = sb("csb", [128, NK, 2], BF16); b_cv = Buf()
    MU4 = sb("mu4", [128, L, 27, 4])

    P.dma("sp", N1[:], n1T, writes=[b_n])
    P.dma("sp", N2[:], n2T, writes=[b_n])
    P.dma("sp", NFt[:], nfT, writes=[b_n])
    P.dma("sp", MU4[:], muT, writes=[b_mu])
    P.dma("sp", W0[:], w0T, writes=[b_par])
    P.dma("sp", A0[:], a0T, writes=[b_par])
    P.dma("sp", KKp[:], kkT, writes=[b_par])
    P.dma("sp", KAp[:], kaT, writes=[b_par])
    P.dma("sp", RKp[:], rkT, writes=[b_par])
    P.dma("sp", BMT[:], bmodT, writes=[b_modt])
    P.dma("sp", CV[:], cvec, writes=[b_cv])
    if pair:
        MSEL = sb("msel_t", [128, 2]); b_msel = Buf()
        P.dma("sp", MSEL[:], msel, writes=[b_msel])
    P.op("act", lambda: A_.activation(out=CSb[:], in_=CV[:], func=AF.Silu), reads=[b_cv], writes=[b_cv])
    P.op("dve", lambda: V_.tensor_copy(MU[:, :, :, 0:4], MU4[:]), reads=[b_mu], writes=[b_mu])
    P.op("dve", lambda: V_.reduce_sum(MU[:, :, :, 4], MU4[:], axis=AX.X), reads=[b_mu], writes=[b_mu])
    P.op("dve", lambda: V_.tensor_scalar(MU[:, :, :, 4], MU[:, :, :, 4], -1.0, 1.0, ALU.mult, ALU.add), reads=[b_mu], writes=[b_mu])
    P.op("dve", lambda: V_.tensor_tensor(MU[:, :, :, 5], MU4[:, :, :, 0], MU4[:, :, :, 1], ALU.add), reads=[b_mu], writes=[b_mu])
    P.op("dve", lambda: V_.tensor_scalar(MU[:, :, :, 5], MU[:, :, :, 5], -1.0, 1.0, ALU.mult, ALU.add), reads=[b_mu], writes=[b_mu])

    def mod_gen(l, WM, b_wm, pbk, pw):
        wv = w_mod[l].rearrange("(k p) n -> p k n", p=128)
        npiece = (6 * D) // pw
        nb = pw // 128
        for piece in range(npiece):
            t = piece % 2
            P.dma("pool", WM[t][:, :, 0:pw], wv[:, :, piece * pw:(piece + 1) * pw], writes=[b_wm[t]])
            for jb in range(nb):
                j = piece * nb + jb
                for k in range(NK):
                    last = (k == NK - 1) and (jb == nb - 1)
                    P.op("pe", lambda: nc.tensor.matmul(
                        PB[pbk][:, 2 * j:2 * j + 2], WM[t][:, k, jb * 128:(jb + 1) * 128], CSb[:, k, :],
                        start=(k == 0), stop=(k == NK - 1)),
                        reads=[b_wm[t], b_cv], writes=[PBB[pbk]], inc=last)
            yield
        P.op("dve", lambda: V_.tensor_tensor(
            MODT[:, l], PB[pbk][:, 0:192].rearrange("p (j v) -> p j v", v=2),
            BMT[:, l, :].unsqueeze(2).to_broadcast([128, 96, 2]), ALU.add),
            reads=[PBB[pbk], b_modt], writes=[b_modt])
        for wn, (Nt, off) in enumerate(((N1, 16), (N2, 64))):
            P.op("dve", lambda: V_.scalar_tensor_tensor(
                out=SCL[:, l, wn], in0=MODT[:, l, off:off + 16, :], scalar=1.0,
                in1=Nt[:, l, :].unsqueeze(2).to_broadcast([128, NK, 2]), op0=ALU.add, op1=ALU.mult),
                reads=[b_modt, b_n], writes=[b_scl])

    with contextlib.ExitStack() as cx:
        WM = [sb(f"wm{i}", [128, NK, 1024], BF16, cx) for i in range(2)]
        for _ in mod_gen(0, WM, [Buf(), Buf()], 0, 1024):
            pass
        P.barrier()
    if stop == "M":
        P.barrier(); P.es.close(); return nc

    def modv(l, off, k, v):
        return MODT[:, l, off + k, v:v + 1]

    def super_tiles():
        res = []
        t0 = 0
        while t0 < TS:
            nt = 1024 if TS - t0 >= 1024 else 512
            res.append((t0, nt, 0))
            t0 += nt
        res.append((TS, 512, 1))
        return res

    STS = super_tiles()
    rr = {"pb": 0}

    def rms_tile(cx_tiles, src_dram, srcB, t5, dst_ht, dst_col0, b_dst, scale_fn, bias_fn, ssbank, extra=None):
        XT_, b_xt, SQ_, b_sq, RS_, b_rs = cx_tiles
        P.dma("sp", XT_[:], src_dram.rearrange("(k p) t -> p k t", p=128)[:, :, t5 * 512:(t5 + 1) * 512],
              reads=[srcB[k][t5] for k in range(NK)], writes=[b_xt])
        for k in range(NK):
            s = k % 2
            P.op("act", lambda k=k, s=s: A_.activation(out=SQ_[s][:], in_=XT_[:, k, :], func=AF.Square),
                 reads=[b_xt], writes=[b_sq[s]])
            P.op("pe", lambda k=k, s=s: nc.tensor.matmul(PB[ssbank][:], ones_f[:], SQ_[s][:], start=(k == 0), stop=(k == NK - 1)),
                 reads=[b_sq[s], b_ones], writes=[PBB[ssbank]])
        P.op("dve", lambda: V_.tensor_scalar(RS_[:], PB[ssbank][:], 1.0 / D, RMS_EPS, ALU.mult, ALU.add),
             reads=[PBB[ssbank]], writes=[b_rs])
        P.op("act", lambda: A_.activation(out=RS_[:], in_=RS_[:], func=AF.Sqrt), reads=[b_rs], writes=[b_rs])
        P.op("dve", lambda: V_.reciprocal(RS_[:], RS_[:]), reads=[b_rs], writes=[b_rs])
        for k in range(NK):
            s = k % 2
            P.op("dve", lambda k=k, s=s: V_.tensor_tensor(SQ_[s][:], XT_[:, k, :], RS_[:], ALU.mult),
                 reads=[b_xt, b_rs, b_sq[s]], writes=[b_sq[s]])
            P.op("act", lambda k=k, s=s: A_.activation(out=dst_ht[:, k, dst_col0:dst_col0 + 512], in_=SQ_[s][:],
                                                        func=AF.Identity, scale=scale_fn(k), bias=bias_fn(k)),
                 reads=[b_sq[s], b_scl, b_modt, b_n], writes=[b_dst])

    def chk(ph):
        if stop == ph:
            raise _Stop()

    try:
      for l in range(L):
          XSRC = xT_in if l == 0 else XRES
          with contextlib.ExitStack() as cx:
              XT_ = sb("a_xt", [128, NK, 512], F32, cx); b_xt = Buf()
              SQ_ = [sb(f"a_sq{i}", [128, 512], F32, cx) for i in range(2)]; b_sq = [Buf(), Buf()]
              RS_ = sb("a_rs", [128, 512], F32, cx); b_rs = Buf()
              HT = sb("a_ht", [128, NK, 1024], BF16, cx); b_ht = Buf()
              WIN = [sb(f"a_win{i}", [128, NK, 512], BF16, cx) for i in range(2)]; b_win = [Buf(), Buf()]
              STG = [sb(f"a_stg{i}", [128, 512], F32, cx) for i in range(4)]; b_stg = [Buf() for _ in range(4)]
              STB = [sb(f"a_stb{i}", [128, 512], BF16, cx) for i in range(2)]; b_stb = [Buf() for _ in range(2)]
              wv = w_in[l].rearrange("(k p) n -> p k n", p=128)
              pieces = [(c0, 512) for c0 in range(0, 3072, 512)] + [(3072, 288)] + \
                       [(3360, 512), (3872, 512), (4384, 512), (4896, 512)]
              wc = 0
              sg = 0
              sgb = 0
              for (t0, NT, vs) in STS:
                  nsub = NT // 512
                  for sub in range(nsub):
                      t5 = t0 // 512 + sub
                      rms_tile((XT_, b_xt, SQ_, b_sq, RS_, b_rs), XSRC, XB, t5, HT, sub * 512, b_ht,
                               lambda k: SCL[:, l, 0, k, vs:vs + 1], lambda k: modv(l, 0, k, vs), 0)
                  for (c0, cw) in pieces:
                      wt = wc % 2
                      wc += 1
                      P.dma("pool", WIN[wt][:, :, 0:cw], wv[:, :, c0:c0 + cw], writes=[b_win[wt]])
                      if c0 < 4384:
                          nblk = (cw + 127) // 128
                          for bi in range(nblk):
                              bw = min(128, cw - bi * 128)
                              col = c0 + bi * 128
                              for sub in range(nsub):
                                  t5 = t0 // 512 + sub
                                  pb = 2 + rr["pb"] % 6
                                  rr["pb"] += 1
                                  for k in range(NK):
                                      P.op("pe", lambda k=k, pb=pb, bi=bi, bw=bw, wt=wt, sub=sub: nc.tensor.matmul(
                                          PB[pb][0:bw, :], WIN[wt][:, k, bi * 128:bi * 128 + bw], HT[:, k, sub * 512:(sub + 1) * 512],
                                          start=(k == 0), stop=(k == NK - 1)),
                                          reads=[b_win[wt], b_ht], writes=[PBB[pb]], inc=(k == NK - 1))
                                  if col < CS:
                                      jb = col // 128
                                      s = sg % 4
                                      sg += 1
                                      eng = "act" if (sg % 2) else "dve"
                                      if eng == "act":
                                          P.op("act", lambda s=s, pb=pb, bw=bw: A_.copy(STG[s][0:bw, :], PB[pb][0:bw, :]),
                                               reads=[PBB[pb]], writes=[b_stg[s]])
                                      else:
                                          P.op("dve", lambda s=s, pb=pb, bw=bw: V_.tensor_copy(STG[s][0:bw, :], PB[pb][0:bw, :]),
                                               reads=[PBB[pb]], writes=[b_stg[s]])
                                      P.dma("sp", PT[jb * 128:jb * 128 + bw, t5 * 512:(t5 + 1) * 512], STG[s][0:bw, :],
                                            reads=[b_stg[s]], writes=[PTB[jb][t5]])
                                  else:
                                      g = (col - CS) // 128
                                      s = sgb % 2
                                      sgb += 1
                                      P.op("act", lambda s=s, pb=pb: A_.activation(out=STB[s][:], in_=PB[pb][:], func=AF.Gelu_apprx_tanh),
                                           reads=[PBB[pb]], writes=[b_stb[s]])
                                      P.dma("sp", UTd[g * 128:(g + 1) * 128, t5 * 512:(t5 + 1) * 512], STB[s][:],
                                            reads=[b_stb[s]], writes=[UTB[g][t5]])
                      else:
                          vc0 = c0 - 4384
                          for sub in range(nsub):
                              t5 = t0 // 512 + sub
                              for ts in range(4):
                                  pb = 2 + rr["pb"] % 6
                                  rr["pb"] += 1
                                  for k in range(NK):
                                      P.op("pe", lambda k=k, pb=pb, wt=wt, sub=sub, ts=ts: nc.tensor.matmul(
                                          PB[pb][:], HT[:, k, sub * 512 + ts * 128:sub * 512 + (ts + 1) * 128], WIN[wt][:, k, :],
                                          start=(k == 0), stop=(k == NK - 1)),
                                          reads=[b_win[wt], b_ht], writes=[PBB[pb]], inc=(k == NK - 1))
                                  s = sg % 4
                                  sg += 1
                                  P.op("act", lambda s=s, pb=pb: A_.activation(out=STG[s][:], in_=PB[pb][:], func=AF.Gelu_apprx_tanh),
                                       reads=[PBB[pb]], writes=[b_stg[s]])
                                  tok0 = t5 * 512 + ts * 128
                                  P.dma("sp", VTd[tok0:tok0 + 128, vc0:vc0 + 512], STG[s][:],
                                        reads=[b_stg[s]], writes=[VTB[t5][ts * 2 + vc0 // 512]])
              P.barrier()
          if pair:
              P.dma("sp", HSEND, PT[:, TS - 64:TS], reads=[PTB[j][TS // 512 - 1] for j in range(27)], writes=[b_hsend])
              P.coll(HSEND, HGATH, GROUPS, reads=[b_hsend], writes=[b_hgath])
          chk("A")

          with contextlib.ExitStack() as cx:
              VT_ = sb("g_vt", [128, 4, 1024], F32, cx); b_vt = Buf()
              VC_ = sb("g_vc", [128, 4, 1024], F32, cx); b_vc = Buf()
              SQg = sb("g_sq", [128, 4, 1024], F32, cx); b_sqg = Buf()
              VN_ = sb("g_vn", [128, 4, 1024], BF16, cx); b_vn = Buf()
              UTt = sb("g_ut", [128, 8, 512], BF16, cx); b_ut = Buf()
              OBt = sb("g_ob", [128, 8, 512], BF16, cx); b_ob = Buf()
              LNG = sb("g_lng", [128, 1024], F32, cx); LNB = sb("g_lnb", [128, 1024], F32, cx); b_ln = Buf()
              WST = sb("g_wst", [128, 8, 128], BF16, cx); BSP = sb("g_bsp", [1, 1024], BF16, cx); b_ws = Buf()
              ST1 = sb("g_s1", [128, 32], F32, cx); ST2 = sb("g_s2", [128, 32], F32, cx); b_st = Buf()
              P.dma("sp", LNG[:], lng[l].partition_broadcast(128), writes=[b_ln])
              P.dma("sp", LNB[:], lnb[l].partition_broadcast(128), writes=[b_ln])
              P.dma("pool", WST[:], wsT[l], writes=[b_ws])
              P.dma("pool", BSP[:], bsp[l], writes=[b_ws])
              for t5 in range(NT5):
                  P.dma("sp", VT_[:], VTd[t5 * 512:(t5 + 1) * 512, :].rearrange("(a p) c -> p a c", p=128),
                        reads=VTB[t5], writes=[b_vt])
                  P.dma("sp", UTt[:], UTd[:, t5 * 512:(t5 + 1) * 512].rearrange("(g p) t -> p g t", p=128),
                        reads=[UTB[g][t5] for g in range(8)], writes=[b_ut])
                  V4 = VT_[:].rearrange("p a (g d) -> p (a g) d", d=128)
                  C4 = VC_[:].rearrange("p a (g d) -> p (a g) d", d=128)
                  Q4 = SQg[:].rearrange("p a (g d) -> p (a g) d", d=128)
                  P.op("dve", lambda: V_.reduce_sum(ST1[:], V4, axis=AX.X), reads=[b_vt], writes=[b_st])
                  P.op("dve", lambda: V_.tensor_scalar(ST1[:], ST1[:], 1.0 / 128, None, ALU.mult), reads=[b_st], writes=[b_st])
                  P.op("dve", lambda: V_.tensor_tensor(C4, V4, ST1[:].unsqueeze(2).to_broadcast([128, 32, 128]), ALU.subtract),
                       reads=[b_vt, b_st], writes=[b_vc])
                  P.op("act", lambda: A_.activation(out=SQg[:], in_=VC_[:], func=AF.Square), reads=[b_vc], writes=[b_sqg])
                  P.op("dve", lambda: V_.reduce_sum(ST2[:], Q4, axis=AX.X), reads=[b_sqg], writes=[b_st])
                  P.op("dve", lambda: V_.tensor_scalar(ST2[:], ST2[:], 1.0 / 128, LN_EPS, ALU.mult, ALU.add), reads=[b_st], writes=[b_st])
                  P.op("act", lambda: A_.activation(out=ST2[:], in_=ST2[:], func=AF.Sqrt), reads=[b_st], writes=[b_st])
                  P.op("dve", lambda: V_.reciprocal(ST2[:], ST2[:]), reads=[b_st], writes=[b_st])
                  P.op("dve", lambda: V_.tensor_tensor(C4, C4, ST2[:].unsqueeze(2).to_broadcast([128, 32, 128]), ALU.mult),
                       reads=[b_vc, b_st], writes=[b_vc])
                  P.op("pool", lambda: G_.tensor_tensor(VC_[:], VC_[:], LNG[:].unsqueeze(1).to_broadcast([128, 4, 1024]), ALU.mult),
                       reads=[b_vc, b_ln], writes=[b_vc])
                  P.op("dve", lambda: V_.tensor_tensor(VN_[:], VC_[:], LNB[:].unsqueeze(1).to_broadcast([128, 4, 1024]), ALU.add),
                       reads=[b_vc, b_ln], writes=[b_vn])
                  for g in range(8):
                      pb = g
                      for a in range(4):
                          P.op("pe", lambda a=a, g=g, pb=pb: nc.tensor.matmul(
                              PB[pb][:, a * 128:(a + 1) * 128], VN_[:, a, g * 128:(g + 1) * 128], WST[:, g, :], start=True, stop=False),
                              reads=[b_vn, b_ws], writes=[PBB[pb]], inc=False)
                          P.op("pe", lambda a=a, g=g, pb=pb: nc.tensor.matmul(
                              PB[pb][:, a * 128:(a + 1) * 128], ones_row[0:1, :], BSP[0:1, g * 128:(g + 1) * 128], start=False, stop=True),
                              reads=[b_onesrow, b_ws, b_vn], writes=[PBB[pb]], inc=(a == 3))
                      P.op("dve", lambda g=g, pb=pb: V_.tensor_tensor(OBt[:, g, :], PB[pb][:], UTt[:, g, :], ALU.mult),
                           reads=[PBB[pb], b_ut], writes=[b_ob])
                  P.dma("sp", OT[1024:2048, t5 * 512:(t5 + 1) * 512].rearrange("(g p) t -> p g t", p=128), OBt[:],
                        reads=[b_ob], writes=[OTB[1][t5]])
              P.barrier()
          chk("G")

          GAM = sb(f"gam{l}", [128, 2, 8, NCH], F32); b_gam = Buf()
          with contextlib.ExitStack() as cx:
              GIN = [sb(f"b_gin{i}", [128, 640], F32, cx) for i in range(3)]; b_gin = [Buf() for _ in range(3)]
              SH = sb("b_sh", [128, 27, 512], F32, cx); b_sh = [Buf() for _ in range(27)]
              WTb = sb("b_wt", [128, 512], BF16, cx); ADb = sb("b_ad", [128, 512], BF16, cx)
              SGb = sb("b_sg", [128, 512], BF16, cx); SG2b = sb("b_sg2", [128, 512], BF16, cx); b_lo = Buf()
              W2s = sb("b_w2", [128, 2, 1024], BF16, cx); A2s = sb("b_a2", [128, 2, 1024], BF16, cx)
              G2a = sb("b_g2a", [128, 1024], BF16, cx); G2b = sb("b_g2b", [128, 1024], BF16, cx); b_lw = Buf()
              ROLES = {}
              SCQ = [sb(f"b_scq{i}", [128, 4, 4, 128], BF16, cx) for i in range(2)]; b_scq = [Buf(), Buf()]
              if pair:
                  HSL = [sb(f"b_hsl{i}", [128, 27, 64], F32, cx) for i in range(2)]; b_hsl = Buf()
                  HALO = sb("b_halo", [128, 27, 64], F32, cx); b_halo = Buf()
              VBs = [sb(f"b_vbs{i}", [128, 512], BF16, cx) for i in range(2)]; b_vbs = [Buf(), Buf()]
              tmi = {"q": 0, "v": 0, "g": 0}
              if l + 1 < L:
                  WMs = [sb(f"b_wm{i}", [128, NK, 128], BF16, cx) for i in range(2)]
                  mgen = mod_gen(l + 1, WMs, [Buf(), Buf()], 1, 128)
              else:
                  mgen = None

              def tmp(role, par):
                  key = (role, par % 2)
                  if key not in ROLES:
                      ROLES[key] = (sb(f"b_r_{role}{par % 2}", [128, 512], F32, cx), Buf())
                  return ROLES[key]

              P.dma("pool", W2s[0:64, :, :], w2[l].rearrange("d j c -> j d c"), writes=[b_lw])
              P.dma("pool", A2s[64:128, :, :], a2[l].rearrange("d j c -> j d c"), writes=[b_lw])
              P.dma("pool", G2a[:], g2[l, 0:128, :], writes=[b_lw])
              P.dma("pool", G2b[0:32, :], g2[l, 128:160, :], writes=[b_lw])

              for w in range(NW):
                  is_p = (w == NW - 1)
                  t0 = w * 512
                  W = 512
                  for j in range(27):
                      npar = 128 if j < 26 else 32
                      gi = j % 3
                      g_, bg = GIN[gi], b_gin[gi]
                      rows = slice(j * 128, j * 128 + npar)
                      if not is_p:
                          lo = t0 - 64
                          hi = t0 + W + 64
                          rd = [PTB[j][w]]
                          if lo < 0:
                              P.op("pool", lambda g_=g_, npar=npar: G_.memset(g_[0:npar, 0:64], 0.0), writes=[bg])
                              lo = 0
                          else:
                              rd.append(PTB[j][w - 1])
                          if hi > TS:
                              if pair:
                                  if j == 0:
                                      for sl_ in range(2):
                                          P.dma("sp", HSL[sl_][:], HGATH[sl_ * 3456:(sl_ + 1) * 3456, :].rearrange("(j p) t -> p j t", p=128),
                                                reads=[b_hgath], writes=[b_hsl])
                                      P.op("dve", lambda: V_.tensor_scalar(HALO[:], HSL[0][:], MSEL[:, 0:1], None, ALU.mult),
                                           reads=[b_hsl, b_msel], writes=[b_halo])
                                      P.op("dve", lambda: V_.scalar_tensor_tensor(out=HALO[:], in0=HSL[1][:], scalar=MSEL[:, 1:2], in1=HALO[:],
                                                                                  op0=ALU.mult, op1=ALU.add),
                                           reads=[b_hsl, b_msel, b_halo], writes=[b_halo])
                                  P.op("pool", lambda g_=g_, npar=npar, j=j: G_.tensor_copy(g_[0:npar, 576:640], HALO[0:npar, j, ::-1]),
                                       reads=[b_halo], writes=[bg])
                              else:
                                  P.op("pool", lambda g_=g_, npar=npar: G_.memset(g_[0:npar, 576:640], 0.0), writes=[bg])
                              hi = TS
                          else:
                              rd.append(PTB[j][w + 1])
                          P.dma("sp", g_[0:npar, lo - (t0 - 64):hi - (t0 - 64)], PT[rows, lo:hi], reads=rd, writes=[bg])
                          ctr = g_[0:npar, 64:64 + W]
                          acc = SH[0:npar, j, :]
                          P.op("act", lambda acc=acc, ctr=ctr, j=j, npar=npar: A_.activation(out=acc, in_=ctr, func=AF.Copy, scale=MU[0:npar, l, j, 4:5]),
                               reads=[bg, b_mu], writes=[b_sh[j]])
                          a3 = acc.rearrange("p (r c) -> p r c", c=64)
                          c3 = ctr.rearrange("p (r c) -> p r c", c=64)
                          P.op("dve", lambda a3=a3, c3=c3, j=j, npar=npar: V_.scalar_tensor_tensor(
                              out=a3[:, :, 1:64], in0=c3[:, :, 0:63], scalar=MU[0:npar, l, j, 0:1], in1=a3[:, :, 1:64], op0=ALU.mult, op1=ALU.add),
                              reads=[bg, b_mu, b_sh[j]], writes=[b_sh[j]])
                          P.op("dve", lambda a3=a3, c3=c3, j=j, npar=npar: V_.scalar_tensor_tensor(
                              out=a3[:, :, 0:63], in0=c3[:, :, 1:64], scalar=MU[0:npar, l, j, 1:2], in1=a3[:, :, 0:63], op0=ALU.mult, op1=ALU.add),
                              reads=[bg, b_mu, b_sh[j]], writes=[b_sh[j]])
                          P.op("dve", lambda acc=acc, g_=g_, j=j, npar=npar: V_.scalar_tensor_tensor(
                              out=acc, in0=g_[0:npar, 0:W], scalar=MU[0:npar, l, j, 2:3], in1=acc, op0=ALU.mult, op1=ALU.add),
                              reads=[bg, b_mu, b_sh[j]], writes=[b_sh[j]])
                          P.op("dve", lambda acc=acc, g_=g_, j=j, npar=npar: V_.scalar_tensor_tensor(
                              out=acc, in0=g_[0:npar, 128:128 + W], scalar=MU[0:npar, l, j, 3:4], in1=acc, op0=ALU.mult, op1=ALU.add),
                              reads=[bg, b_mu, b_sh[j]], writes=[b_sh[j]])
                      else:
                          P.dma("sp", g_[0:npar, 64:64 + W], PT[rows, t0:t0 + W], reads=[PTB[j][w]], writes=[bg])
                          ctr = g_[0:npar, 64:64 + W]
                          acc = SH[0:npar, j, :]
                          P.op("act", lambda acc=acc, ctr=ctr, j=j, npar=npar: A_.activation(out=acc, in_=ctr, func=AF.Copy, scale=MU[0:npar, l, j, 5:6]),
                               reads=[bg, b_mu], writes=[b_sh[j]])
                          a3 = acc.rearrange("p (r c) -> p r c", c=SEQP)
                          c3 = ctr.rearrange("p (r c) -> p r c", c=SEQP)
                          P.op("dve", lambda a3=a3, c3=c3, j=j, npar=npar: V_.scalar_tensor_tensor(
                              out=a3[:, :, 1:SEQP], in0=c3[:, :, 0:SEQP - 1], scalar=MU[0:npar, l, j, 0:1], in1=a3[:, :, 1:SEQP], op0=ALU.mult, op1=ALU.add),
                              reads=[bg, b_mu, b_sh[j]], writes=[b_sh[j]])
                          P.op("dve", lambda a3=a3, c3=c3, j=j, npar=npar: V_.scalar_tensor_tensor(
                              out=a3[:, :, 0:SEQP - 1], in0=c3[:, :, 1:SEQP], scalar=MU[0:npar, l, j, 1:2], in1=a3[:, :, 0:SEQP - 1], op0=ALU.mult, op1=ALU.add),
                              reads=[bg, b_mu, b_sh[j]], writes=[b_sh[j]])
                  P.op("act", lambda: A_.activation(out=WTb[0:64, :], in_=SH[0:64, 24, :], func=AF.Tanh), reads=[b_sh[24]], writes=[b_lo])
                  P.op("dve", lambda: V_.tensor_copy(ADb[64:128, :], SH[64:128, 24, :]), reads=[b_sh[24]], writes=[b_lo])
                  P.op("act", lambda: A_.activation(out=SGb[:], in_=SH[:, 25, :], func=AF.Sigmoid), reads=[b_sh[25]], writes=[b_lo])
                  P.op("act", lambda: A_.activation(out=SG2b[0:32, :], in_=SH[0:32, 26, :], func=AF.Sigmoid), reads=[b_sh[26]], writes=[b_lo])
                  for j in range(8):
                      jc = slice(j * 128, (j + 1) * 128)
                      Rj, Kj, Vj = SH[:, j, :], SH[:, 8 + j, :], SH[:, 16 + j, :]
                      bR, bK, bV = b_sh[j], b_sh[8 + j], b_sh[16 + j]
                      if mgen is not None:
                          for _ in range(3):
                              if next(mgen, "done") == "done":
                                  mgen = None
                                  break
                      pbg = 0
                      P.op("pe", lambda: nc.tensor.matmul(PB[pbg][:], G2a[:, jc], SGb[:], start=True, stop=False),
                           reads=[b_lw, b_lo], writes=[PBB[pbg]], inc=False)
                      P.op("pe", lambda: nc.tensor.matmul(PB[pbg][:], G2b[0:32, jc], SG2b[0:32, :], start=False, stop=True),
                           reads=[b_lw, b_lo], writes=[PBB[pbg]])
                      tg, btg = tmp("tg", j)
                      P.op("act", lambda tg=tg: A_.copy(tg[:], PB[pbg][:]), reads=[PBB[pbg]], writes=[btg])
                      P.dma("sp", GTd[jc, t0:t0 + W], tg[:], reads=[btg], writes=[GTB[w][j]])
                      vi = tmi["v"] % 2
                      tmi["v"] += 1
                      P.op("pool", lambda vi=vi, Vj=Vj: G_.tensor_copy(VBs[vi][:], Vj), reads=[bV], writes=[b_vbs[vi]])
                      P.dma("sp", VBd[w * 4:(w + 1) * 4, j].rearrange("c p t -> p c t"),
                            VBs[vi][:].rearrange("p (c t) -> p c t", t=128), reads=[b_vbs[vi]], writes=[VBB[w][j]])
                      t1, bt1 = tmp("t1", j)
                      P.op("act", lambda t1=t1, Kj=Kj, j=j: A_.activation(out=t1[:], in_=Kj, func=AF.Copy, scale=KKp[:, l, j:j + 1]),
                           reads=[bK, b_par], writes=[bt1])
                      t2, bt2 = tmp("t2", j)
                      P.op("act", lambda t1=t1, t2=t2: A_.activation(out=t2[:], in_=t1[:], func=AF.Square), reads=[bt1], writes=[bt2])
                      P.op("pe", lambda t2=t2: nc.tensor.matmul(PB[2][:], blk_f[:], t2[:], start=True, stop=True),
                           reads=[bt2, b_blk], writes=[PBB[2]])
                      P.op("dve", lambda t2=t2: V_.tensor_scalar_max(t2[:], PB[2][:], 1e-24), reads=[PBB[2]], writes=[bt2])
                      P.op("act", lambda t2=t2: A_.activation(out=t2[:], in_=t2[:], func=AF.Ln), reads=[bt2], writes=[bt2])
                      P.op("act", lambda t2=t2: A_.activation(out=t2[:], in_=t2[:], func=AF.Exp, scale=-0.5), reads=[bt2], writes=[bt2])
                      kkn, bkkn = tmp("kkn", j)
                      P.op("pool", lambda kkn=kkn, t1=t1, t2=t2: G_.tensor_tensor(kkn[:], t1[:], t2[:], ALU.mult),
                           reads=[bt1, bt2], writes=[bkkn])
                      for d in range(2):
                          pz = 3 + d
                          pa = 5 + d
                          P.op("pe", lambda d=d, pz=pz: nc.tensor.matmul(PB[pz][:], W2s[0:64, d, jc], WTb[0:64, :], start=True, stop=True),
                               reads=[b_lw, b_lo], writes=[PBB[pz]])
                          lws, blws = tmp("lws", d)
                          P.op("act", lambda lws=lws, pz=pz, d=d, j=j: A_.activation(out=lws[:], in_=PB[pz][:], func=AF.Sigmoid, bias=W0[:, l, d, j:j + 1]),
                               reads=[PBB[pz], b_par], writes=[blws])
                          P.op("pe", lambda d=d, pa=pa: nc.tensor.matmul(PB[pa][:], A2s[64:128, d, jc], ADb[64:128, :], start=True, stop=True),
                               reads=[b_lw, b_lo], writes=[PBB[pa]])
                          alr, balr = tmp("alr", d)
                          P.op("act", lambda alr=alr, pa=pa, d=d, j=j: A_.activation(out=alr[:], in_=PB[pa][:], func=AF.Sigmoid, bias=A0[:, l, d, j:j + 1]),
                               reads=[PBB[pa], b_par], writes=[balr])
                          kd, bkd = tmp("kd", d)
                          P.op("dve", lambda kd=kd, alr=alr, j=j: V_.tensor_scalar(kd[:], alr[:], -1.0, KAp[:, l, j:j + 1], ALU.add, ALU.mult),
                               reads=[balr, b_par], writes=[bkd])
                          P.op("dve", lambda kd=kd, Kj=Kj: V_.scalar_tensor_tensor(out=kd[:], in0=kd[:], scalar=1.0, in1=Kj, op0=ALU.add, op1=ALU.mult),
                               reads=[bkd, bK], writes=[bkd])
                          bv, bbv = tmp("bv", d)
                          P.op("pool", lambda bv=bv, kkn=kkn, alr=alr: G_.tensor_tensor(bv[:], kkn[:], alr[:], ALU.mult),
                               reads=[bkkn, balr], writes=[bbv])
                          t4, bt4 = tmp("t4", d)
                          P.op("dve", lambda t4=t4, Rj=Rj, kd=kd, j=j: V_.scalar_tensor_tensor(out=t4[:], in0=Rj, scalar=RKp[:, l, j:j + 1], in1=kd[:], op0=ALU.mult, op1=ALU.mult),
                               reads=[bR, bkd, b_par], writes=[bt4])
                          P.op("pe", lambda t4=t4, d=d: nc.tensor.matmul(PB[7][:], blk_f[:], t4[:], start=(d == 0), stop=(d == 1)),
                               reads=[bt4, b_blk], writes=[PBB[7]])
                          cs, bcs = tmp("cs", d)
                          P.op("dve", lambda cs=cs, lws=lws: V_.tensor_tensor_scan(cs[:], RST[:], lws[:], 0.0, ALU.mult, ALU.add),
                               reads=[blws, b_rst], writes=[bcs])
                          ge, bge = tmp("ge", d)
                          gi_, bgi = tmp("gi", d)
                          cs3 = cs[:].rearrange("p (c t) -> p c t", t=128)
                          if d == 0:
                              P.op("pool", lambda ge=ge, cs=cs, lws=lws: G_.tensor_tensor(ge[:], cs[:], lws[:], ALU.subtract),
                                   reads=[bcs, blws], writes=[bge])
                              GI, bGI = cs, bcs
                          else:
                              P.op("dve", lambda ge=ge, cs3=cs3: V_.tensor_tensor(
                                  ge[:].rearrange("p (c t) -> p c t", t=128), cs3[:, :, 127:128].to_broadcast([128, 4, 128]), cs3, ALU.subtract),
                                  reads=[bcs], writes=[bge])
                              P.op("pool", lambda gi_=gi_, ge=ge, lws=lws: G_.tensor_tensor(gi_[:], ge[:], lws[:], ALU.add),
                                   reads=[bge, blws], writes=[bgi])
                              GI, bGI = gi_, bgi
                          P.op("act", lambda cs3=cs3, d=d, j=j, w=w: A_.activation(out=GAM[:, d, j, w * 4:(w + 1) * 4], in_=cs3[:, :, 127], func=AF.Exp, scale=-c_),
                               reads=[bcs], writes=[b_gam])
                          e1, be1 = tmp("e1", d)
                          e2, be2 = tmp("e2", d)
                          e3, be3 = tmp("e3", d)
                          P.op("act", lambda e1=e1, GI=GI: A_.activation(out=e1[:], in_=GI[:], func=AF.Exp, scale=-c_), reads=[bGI], writes=[be1])
                          P.op("act", lambda e2=e2, ge=ge: A_.activation(out=e2[:], in_=ge[:], func=AF.Exp, scale=-c_), reads=[bge], writes=[be2])
                          P.op("act", lambda e3=e3, GI=GI: A_.activation(out=e3[:], in_=GI[:], func=AF.Exp, scale=c_), reads=[bGI], writes=[be3])
                          qi = tmi["q"] % 2
                          tmi["q"] += 1
                          q_, bq = SCQ[qi], b_scq[qi]
                          c3_ = lambda ap: ap.rearrange("p (c t) -> p c t", t=128)
                          P.op("dve", lambda q_=q_, kkn=kkn, e2=e2: V_.scalar_tensor_tensor(out=q_[:, :, 0, :], in0=c3_(kkn[:]), scalar=-1.0, in1=c3_(e2[:]), op0=ALU.mult, op1=ALU.mult),
                               reads=[bkkn, be2], writes=[bq])
                          P.op("pool", lambda q_=q_, Rj=Rj, e1=e1: G_.tensor_tensor(q_[:, :, 1, :], c3_(Rj), c3_(e1[:]), ALU.mult),
                               reads=[bR, be1, bq], writes=[bq])
                          P.op("dve", lambda q_=q_, bv=bv, e3=e3: V_.tensor_tensor(q_[:, :, 2, :], c3_(bv[:]), c3_(e3[:]), ALU.mult),
                               reads=[bbv, be3, bq], writes=[bq])
                          P.op("pool", lambda q_=q_, kd=kd, e3=e3: G_.tensor_tensor(q_[:, :, 3, :], c3_(kd[:]), c3_(e3[:]), ALU.mult),
                               reads=[bkd, be3, bq], writes=[bq])
                          P.dma("sp", SCd[w * 4:(w + 1) * 4, d, j].rearrange("c p q t -> p c (q t)"),
                                q_[:].rearrange("p c q t -> p c (q t)"), reads=[bq], writes=[SCB[w][d][j]])
                      tb, btb = tmp("tb", j)
                      P.op("dve", lambda tb=tb, Vj=Vj: V_.tensor_tensor(tb[:], PB[7][:], Vj, ALU.mult), reads=[PBB[7], bV], writes=[btb])
                      P.dma("sp", BTd[jc, t0:t0 + W], tb[:], reads=[btb], writes=[BTB[w][j]])
              if mgen is not None:
                  for _ in mgen:
                      pass
              P.barrier()
          chk("B1")

          with contextlib.ExitStack() as cx:
              NSC = 4
              SCT = [sb(f"s_sct{i}", [128, 8, 4, 128], BF16, cx) for i in range(NSC)]; b_sct = [Buf() for _ in range(NSC)]
              VBT = [sb(f"s_vbt{i}", [128, 8, 128], BF16, cx) for i in range(NSC)]; b_vbt = [Buf() for _ in range(NSC)]
              HS = [sb(f"s_hs{d}", [128, 8, 64], F32, cx) for d in range(2)]
              HBt = [sb(f"s_hb{d}", [128, 8, 64], BF16, cx) for d in range(2)]
              b_hs = [[Buf() for _ in range(NH)] for _ in range(2)]
              YS = [sb(f"s_ys{i}", [128, NH, 64], F32, cx) for i in range(2)]; b_ys = [Buf(), Buf()]
              YC = sb("s_yc", [128, NH, 64], F32, cx); b_yc = Buf()
              YQ = sb("s_yq", [128, NH, 64], F32, cx); b_yq = Buf()
              YO = [sb(f"s_yo{i}", [128, 1024], F32, cx) for i in range(2)]; b_yo = [Buf(), Buf()]
              GS1 = sb("s_gs1", [128, NH], F32, cx); GS2 = sb("s_gs2", [128, NH], F32, cx); b_gs = Buf()
              GNW = sb("s_gnw", [128, 1024], F32, cx); GNB = sb("s_gnb", [128, 1024], F32, cx); b_gn = Buf()
              SV = sb("s_sv", [64, NH, 64], F32, cx); b_sv = Buf()
              P.dma("sp", GNW[:], gnw[l].partition_broadcast(128), writes=[b_gn])
              P.dma("sp", GNB[:], gnb[l].partition_broadcast(128), writes=[b_gn])
              NS = 4
              TOK = [[sb(f"s_tok{s}_{hp}", [128, 3, 128], BF16, cx) for hp in range(2)] for s in range(NS)]
              b_tok = [[Buf(), Buf()] for _ in range(NS)]
              VTK = [sb(f"s_vtk{s}", [128, 64], BF16, cx) for s in range(NS)]; b_vtk = [Buf() for _ in range(NS)]
              SCM = [sb(f"s_scm{s}", [128, 512], BF16, cx) for s in range(NS)]; b_scm = [Buf() for _ in range(NS)]
              NMt = [sb(f"s_nm{s}", [128, 128], BF16, cx) for s in range(NS)]; b_nm = [Buf() for _ in range(NS)]
              XS = [sb(f"s_xs{s}", [128, 64], BF16, cx) for s in range(NS)]; b_xs = [Buf() for _ in range(NS)]
              QS = [sb(f"s_qs{s}", [128, 256], F32, cx) for s in range(NS)]; b_qs = [Buf() for _ in range(NS)]
              PW = [sb(f"s_pw{s}", [128, 128], F32, cx) for s in range(NS)]; b_pw = [Buf() for _ in range(NS)]
              MTb = [sb(f"s_mtb{s}", [128, 128], BF16, cx) for s in range(NS)]; b_mtb = [Buf() for _ in range(NS)]
              AH = [sb(f"s_ah{s}", [128, 128], BF16, cx) for s in range(NS)]; b_ah = [Buf() for _ in range(NS)]
              US = [sb(f"s_us{s}", [128, 64], BF16, cx) for s in range(NS)]; b_us = [Buf() for _ in range(NS)]
              TH = [sb(f"s_th{s}", [128, 64], F32, cx) for s in range(NS)]; b_th = [Buf() for _ in range(NS)]
              for s in range(NS):
                  for hp in range(2):
                      P.op("dve", lambda s=s, hp=hp: V_.memset(TOK[s][hp][:], 0.0), writes=[b_tok[s][hp]])
              PSTv = [PB[2 * s][:].bitcast(BF16) for s in range(NS)]
              sci = {"i": 0, "y": 0}

              def group(s, h, d, chg, sct, bsct, vbt, bvbt, ys, bys):
                  j, hp = h // 2, h % 2
                  P0 = hp * 64
                  hs = slice(P0, P0 + 64)
                  PSA, bPSA = PB[2 * s], PBB[2 * s]
                  PSC = PB[2 * s + 1]
                  PSD, bPSD = PSC, PBB[2 * s + 1]
                  PST, bPST = PSTv[s], bPSA
                  bN = bPSD
                  bX = bU = bY = bH = bPSA
                  R_N = PSC[:, 384:512]
                  R_X = PSA[:, 0:64]
                  R_U = PSA[:, 64:128]
                  R_Y = PSA[:, 128:192]
                  R_H = PSA[:, 192:256]
                  aT = sct[hs, j, 0, :]
                  rT = sct[hs, j, 1, :]
                  bT = sct[hs, j, 2, :]
                  kT = sct[hs, j, 3, :]
                  arT = sct[hs, j, 0:2, :].rearrange("p q t -> p (q t)")
                  vT = vbt[hs, j, :]
                  idh = id_b[hs, P0:P0 + 64]
                  tok, btok = TOK[s][hp], b_tok[s][hp]
                  for qi, src in enumerate((aT, bT, kT)):
                      P.op("pe", lambda qi=qi, src=src: nc.tensor.transpose(PST[:, qi * 64:(qi + 1) * 64], src, idh),
                           reads=[bsct, b_idb], writes=[bPST], inc=False)
                  P.op("pe", lambda: nc.tensor.transpose(PST[:, 192:256], vT, idh), reads=[bvbt, b_idb], writes=[bPST])
                  yield
                  import os as _os
                  _skip = _os.environ.get("K_DBG_SKIP", "")
                  if "tokevac" not in _skip:
                      P.op("act", lambda: A_.copy(tok[:, :, P0:P0 + 64], PST[:, 0:192].rearrange("p (q c) -> p q c", c=64)),
                           reads=[bPST], writes=[btok])
                  if "vtkevac" not in _skip:
                      P.op("dve", lambda: V_.tensor_copy(VTK[s][:], PST[:, 192:256]), reads=[bPST], writes=[b_vtk[s]])
                  if "scores" in _skip:
                      yield
                      raise _Stop()
                  P.op("pe", lambda: nc.tensor.matmul(PSA[:, 0:256], bT, arT, start=True, stop=True), reads=[bsct], writes=[bPSA], inc=False)
                  P.op("pe", lambda: nc.tensor.matmul(PSA[:, 256:512], kT, arT, start=True, stop=True), reads=[bsct], writes=[bPSA])
                  P.op("pe", lambda: nc.tensor.matmul(R_N, aT, bT, start=True, stop=True), reads=[bsct], writes=[bN])
                  yield
                  P.op("dve", lambda: V_.tensor_tensor(QS[s][:, 0:128], PSA[:, 0:128], MASK[d][:, 0:128], ALU.mult),
                       reads=[bPSA, b_mask[d]], writes=[b_qs[s]])
                  P.op("dve", lambda: V_.tensor_tensor(SCM[s][:, 128:512], PSA[:, 128:512], MASK[d][:, 128:512], ALU.mult),
                       reads=[bPSA, b_mask[d]], writes=[b_scm[s]])
                  P.op("dve", lambda: V_.tensor_tensor(PW[s][:], R_N, NMASK[d][:], ALU.mult), reads=[bN, b_nmask[d]], writes=[b_pw[s]])
                  P.op("pe", lambda: nc.tensor.matmul(R_X, SCM[s][:, 256:384], VTK[s][:], start=True, stop=True),
                       reads=[b_scm[s], b_vtk[s]], writes=[bX])
                  P.op("pe", lambda: nc.tensor.matmul(PSD[:, 0:128], PW[s][:], QS[s][:, 0:128], start=True, stop=True),
                       reads=[b_pw[s], b_qs[s]], writes=[bPSD], inc=False)
                  P.op("pe", lambda: nc.tensor.matmul(PSD[:, 256:384], QS[s][:, 0:128], PW[s][:], start=True, stop=True),
                       reads=[b_pw[s], b_qs[s]], writes=[bPSD])
                  yield
                  P.op("act", lambda: A_.copy(XS[s][:], R_X), reads=[bX], writes=[b_xs[s]])
                  P.op("pool", lambda: G_.tensor_tensor(QS[s][:, 128:256], QS[s][:, 0:128], id_f[:], ALU.add),
                       reads=[b_qs[s], b_idf], writes=[b_qs[s]])
                  P.op("act", lambda: A_.copy(QS[s][:, 0:128], PSD[:, 0:128]), reads=[bPSD, b_qs[s]], writes=[b_qs[s]])
                  P.op("act", lambda: A_.copy(PW[s][:], PSD[:, 256:384]), reads=[bPSD], writes=[b_pw[s]])
                  for lev in range(1, 6):
                      P.op("pe", lambda: nc.tensor.matmul(PSD[:, 0:256], PW[s][:], QS[s][:], start=True, stop=True),
                           reads=[b_pw[s], b_qs[s]], writes=[bPSD], inc=False)
                      P.op("pe", lambda: nc.tensor.matmul(PSD[:, 256:384], QS[s][:, 0:128], PW[s][:], start=True, stop=True),
                           reads=[b_pw[s], b_qs[s]], writes=[bPSD])
                      yield
                      P.op("act", lambda: A_.copy(QS[s][:, 0:128], PSD[:, 0:128]), reads=[bPSD], writes=[b_qs[s]])
                      P.op("dve", lambda: V_.tensor_tensor(QS[s][:, 128:256], PSD[:, 128:256], QS[s][:, 128:256], ALU.add),
                           reads=[bPSD, b_qs[s]], writes=[b_qs[s]])
                      P.op("act", lambda: A_.copy(PW[s][:], PSD[:, 256:384]), reads=[bPSD], writes=[b_pw[s]])
                  P.op("pe", lambda: nc.tensor.matmul(PSD[:, 128:256], PW[s][:], QS[s][:, 128:256], start=True, stop=True),
                       reads=[b_pw[s], b_qs[s]], writes=[bPSD])
                  yield
                  P.op("dve", lambda: V_.tensor_tensor(MTb[s][:], PSD[:, 128:256], QS[s][:, 128:256], ALU.add),
                       reads=[bPSD, b_qs[s]], writes=[b_mtb[s]])
                  MT = MTb[s][:]
                  P.op("pe", lambda: nc.tensor.matmul(R_U, MT, XS[s][:], start=True, stop=False),
                       reads=[b_mtb[s], b_xs[s]], writes=[bU], inc=False)
                  P.op("pe", lambda: nc.tensor.matmul(R_N, tok[:, 0, :], MT, start=True, stop=True),
                       reads=[btok, b_mtb[s]], writes=[bN])
                  yield
                  P.op("act", lambda: A_.copy(AH[s][hs, :], PSC[hs, 384:512]), reads=[bN], writes=[b_ah[s]])
                  hb = HBt[d][hs, j, :]
                  P.op("pe", lambda: nc.tensor.matmul(R_U, AH[s][hs, :], hb, start=False, stop=True),
                       reads=[b_ah[s], b_hs[d][h]], writes=[bU])
                  yield
                  P.op("act", lambda: A_.copy(US[s][:], R_U), reads=[bU], writes=[b_us[s]])
                  P.op("pe", lambda: nc.tensor.matmul(R_Y, rT, hb, start=True, stop=False),
                       reads=[bsct, b_hs[d][h]], writes=[bY], inc=False)
                  P.op("pe", lambda: nc.tensor.matmul(R_Y, SCM[s][:, 128:256], US[s][:], start=False, stop=False),
                       reads=[b_scm[s], b_us[s]], writes=[bY], inc=False)
                  P.op("pe", lambda: nc.tensor.matmul(R_Y, SCM[s][:, 384:512], VTK[s][:], start=False, stop=True),
                       reads=[b_scm[s], b_vtk[s]], writes=[bY], rec_reads=[bsct, b_hs[d][h], b_scm[s], b_us[s], b_vtk[s]])
                  P.op("pe", lambda: nc.tensor.matmul(R_H, tok[:, 1, :], US[s][:], start=True, stop=False),
                       reads=[btok, b_us[s]], writes=[bH], inc=False)
                  P.op("pe", lambda: nc.tensor.matmul(R_H, tok[:, 2, :], VTK[s][:], start=False, stop=True),
                       reads=[btok, b_vtk[s]], writes=[bH], rec_reads=[btok, b_us[s], b_vtk[s]])
                  yield
                  P.op("dve", lambda: V_.tensor_copy(ys[:, h, :], R_Y), reads=[bY], writes=[bys])
                  P.op("dve", lambda: V_.tensor_tensor(TH[s][hs, :], PSA[hs, 192:256], HS[d][hs, j, :], ALU.add),
                       reads=[bH, b_hs[d][h]], writes=[b_th[s]])
                  gam = GAM[hs, d, j, chg:chg + 1]
                  P.op("dve", lambda: V_.tensor_scalar(HS[d][hs, j, :], TH[s][hs, :], gam, None, ALU.mult),
                       reads=[b_th[s], b_gam], writes=[b_hs[d][h]])
                  P.op("act", lambda: A_.activation(out=HBt[d][hs, j, :], in_=TH[s][hs, :], func=AF.Copy, scale=gam),
                       reads=[b_th[s], b_gam, b_hs[d][h]], writes=[b_hs[d][h]])

              def run_pair(gens):
                  live = list(gens)
                  while live:
                      nxt = []
                      for g in live:
                          if stop and stop.startswith("B2:"):
                              sci["steps"] = sci.get("steps", 0) + 1
                              if sci["steps"] > int(stop.split(":")[1]):
                                  raise _Stop()
                          try:
                              next(g)
                              nxt.append(g)
                          except StopIteration:
                              pass
                      live = nxt

              def gn_and_store(d, chg, ys, bys):
                  yi = sci["y"] % 2
                  sci["y"] += 1
                  yo, byo = YO[yi], b_yo[yi]
                  P.op("dve", lambda: V_.reduce_sum(GS1[:], ys[:], axis=AX.X), reads=[bys], writes=[b_gs])
                  P.op("dve", lambda: V_.tensor_scalar(GS1[:], GS1[:], 1.0 / 64, None, ALU.mult), reads=[b_gs], writes=[b_gs])
                  P.op("dve", lambda: V_.tensor_tensor(YC[:], ys[:], GS1[:].unsqueeze(2).to_broadcast([128, NH, 64]), ALU.subtract),
                       reads=[bys, b_gs], writes=[b_yc])
                  P.op("act", lambda: A_.activation(out=YQ[:], in_=YC[:], func=AF.Square), reads=[b_yc], writes=[b_yq])
                  P.op("dve", lambda: V_.reduce_sum(GS2[:], YQ[:], axis=AX.X), reads=[b_yq], writes=[b_gs])
                  P.op("dve", lambda: V_.tensor_scalar(GS2[:], GS2[:], 1.0 / 64, GN_EPS, ALU.mult, ALU.add), reads=[b_gs], writes=[b_gs])
                  P.op("act", lambda: A_.activation(out=GS2[:], in_=GS2[:], func=AF.Sqrt), reads=[b_gs], writes=[b_gs])
                  P.op("dve", lambda: V_.reciprocal(GS2[:], GS2[:]), reads=[b_gs], writes=[b_gs])
                  P.op("dve", lambda: V_.tensor_tensor(YC[:], YC[:], GS2[:].unsqueeze(2).to_broadcast([128, NH, 64]), ALU.mult),
                       reads=[b_yc, b_gs], writes=[b_yc])
                  ycf = YC[:].rearrange("p h v -> p (h v)")
                  P.op("pool", lambda: G_.tensor_tensor(ycf, ycf, GNW[:], ALU.mult), reads=[b_yc, b_gn], writes=[b_yc])
                  P.op("pool", lambda: G_.tensor_tensor(yo[:], ycf, GNB[:], ALU.add), reads=[b_yc, b_gn], writes=[byo])
                  P.dma("sp", YGd[d, chg * 128:(chg + 1) * 128, :], yo[:], reads=[byo], writes=[YGB[d][chg]])

              def init_zero(d):
                  P.op("dve", lambda: V_.memset(HS[d][:], 0.0), reads=[b for b in b_hs[d]], writes=[b for b in b_hs[d]])
                  P.op("dve", lambda: V_.memset(HBt[d][:], 0.0), reads=[b for b in b_hs[d]], writes=[b for b in b_hs[d]])

              def init_dram(d):
                  P.dma("sp", SV[:], state0[l, d].rearrange("h v k -> v h k"), writes=[b_sv])
                  for j in range(8):
                      pb = 6 + j % 2
                      P.op("pe", lambda: nc.tensor.transpose(
                          PB[pb][:, 0:64], SV[:, 2 * j:2 * j + 2, :].rearrange("v h k -> v (h k)"), id_f[0:64, 0:64]),
                          reads=[b_sv, b_idf], writes=[PBB[pb]])
                      P.op("dve", lambda: V_.tensor_copy(HS[d][:, j, :], PB[pb][:, 0:64]),
                           reads=[PBB[pb]], writes=[b_hs[d][2 * j], b_hs[d][2 * j + 1]])
                      P.op("act", lambda: A_.copy(HBt[d][:, j, :], PB[pb][:, 0:64]),
                           reads=[PBB[pb], b_hs[d][2 * j], b_hs[d][2 * j + 1]], writes=[b_hs[d][2 * j], b_hs[d][2 * j + 1]])

              def send_state(d):
                  P.dma("sp", SSEND, HS[d][:].rearrange("p j v -> p (j v)"), reads=[b for b in b_hs[d]], writes=[b_ssend])
                  P.coll(SSEND, SGATH, GROUPS, reads=[b_ssend], writes=[b_sgath])

              def init_recv(d):
                  for sl_ in range(2):
                      P.dma("sp", RCV[sl_][:], SGATH[sl_ * 128:(sl_ + 1) * 128, :], reads=[b_sgath], writes=[b_rcv])
                  hsf = HS[d][:].rearrange("p j v -> p (j v)")
                  P.op("dve", lambda: V_.tensor_scalar(hsf, RCV[0][:], MSEL[:, 0:1], None, ALU.mult),
                       reads=[b_rcv, b_msel] + b_hs[d], writes=b_hs[d])
                  P.op("dve", lambda: V_.scalar_tensor_tensor(out=hsf, in0=RCV[1][:], scalar=MSEL[:, 1:2], in1=hsf, op0=ALU.mult, op1=ALU.add),
                       reads=[b_rcv, b_msel] + b_hs[d], writes=b_hs[d])
                  P.op("act", lambda: A_.copy(HBt[d][:].rearrange("p j v -> p (j v)"), hsf), reads=b_hs[d], writes=b_hs[d])

              def run_chunks(ch0, ncs, dirs):
                  for i in range(ncs):
                      for d in dirs:
                          ch = i if d == 0 else ncs - 1 - i
                          chg = ch0 + ch
                          w = chg // 4
                          si = sci["i"] % NSC
                          sci["i"] += 1
                          sct, bsct, vbt, bvbt = SCT[si], b_sct[si], VBT[si], b_vbt[si]
                          P.dma("sp", sct[:], SCd[chg, d].rearrange("j p q t -> p j q t"),
                                reads=[SCB[w][d][j] for j in range(8)], writes=[bsct])
                          P.dma("sp", vbt[:], VBd[chg].rearrange("j p t -> p j t"),
                                reads=[VBB[w][j] for j in range(8)], writes=[bvbt])
                          ysi = (sci["i"]) % 2
                          ys, bys = YS[ysi], b_ys[ysi]
                          for m in range(NH // NS):
                              run_pair([group(q, NS * m + q, d, chg, sct, bsct, vbt, bvbt, ys, bys) for q in range(NS)])
                          gn_and_store(d, chg, ys, bys)

              def final_states(pi):
                  for d in range(2):
                      for j in range(8):
                          pb = 6 + j % 2
                          P.op("pe", lambda: nc.tensor.transpose(PB[pb][0:64, 0:128], HS[d][:, j, :], id_f[:]),
                               reads=[b_hs[d][2 * j], b_hs[d][2 * j + 1], b_idf], writes=[PBB[pb]])
                          P.op("dve", lambda: V_.tensor_copy(
                              SV[:, 2 * j:2 * j + 2, :].rearrange("v h k -> v (h k)"), PB[pb][0:64, 0:128]),
                              reads=[PBB[pb]], writes=[b_sv])
                      P.dma("sp", nst[pi, l, d].rearrange("h v k -> v h k"), SV[:], reads=[b_sv], writes=[])

              prompts = [((TS + pi * SEQP) // 128, SEQP // 128, pi) for pi in range(NPR)]
              if pair:
                  RCV = [sb(f"s_rcv{i}", [128, 512], F32, cx) for i in range(2)]; b_rcv = Buf()
                  init_dram(0)
                  run_chunks(0, TS // 128, (0,))
                  send_state(0)
              else:
                  init_dram(0)
                  init_dram(1)
                  run_chunks(0, TS // 128, (0, 1))
              for (ch0, ncs, pi) in prompts:
                  init_zero(0)
                  init_zero(1)
                  run_chunks(ch0, ncs, (0, 1))
                  final_states(pi)
              if pair:
                  init_recv(1)
                  run_chunks(0, TS // 128, (1,))
              P.barrier()
          chk("B2")

          with contextlib.ExitStack() as cx:
              YA = sb("c_ya", [128, 4, 1024], F32, cx); b_ya = Buf()
              YBt = sb("c_yb", [128, 4, 1024], F32, cx); b_yb = Buf()
              BTt = sb("c_bt", [128, 8, 512], F32, cx); b_bt = Buf()
              GTt = sb("c_gt", [128, 8, 512], F32, cx); b_gt = Buf()
              OA = sb("c_oa", [128, 8, 512], BF16, cx); b_oa = Buf()
              TMq = [sb(f"c_tm{i}", [128, 512], F32, cx) for i in range(2)]; b_tmq = [Buf(), Buf()]
              for w in range(NW):
                  t0 = w * 512
                  P.dma("sp", YA[:], YGd[0, t0:t0 + 512, :].rearrange("(a p) c -> p a c", p=128),
                        reads=[YGB[0][w * 4 + a] for a in range(4)], writes=[b_ya])
                  P.dma("sp", YBt[:], YGd[1, t0:t0 + 512, :].rearrange("(a p) c -> p a c", p=128),
                        reads=[YGB[1][w * 4 + a] for a in range(4)], writes=[b_yb])
                  P.dma("sp", BTt[:], BTd[:, t0:t0 + 512].rearrange("(j p) t -> p j t", p=128),
                        reads=[BTB[w][j] for j in range(8)], writes=[b_bt])
                  P.dma("sp", GTt[:], GTd[:, t0:t0 + 512].rearrange("(j p) t -> p j t", p=128),
                        reads=[GTB[w][j] for j in range(8)], writes=[b_gt])
                  P.op("pool", lambda: G_.tensor_tensor(YA[:], YA[:], YBt[:], ALU.add), reads=[b_ya, b_yb], writes=[b_ya])
                  for j in range(8):
                      pb = j
                      for a in range(4):
                          P.op("pe", lambda a=a, j=j, pb=pb: nc.tensor.transpose(PB[pb][:, a * 128:(a + 1) * 128], YA[:, a, j * 128:(j + 1) * 128], id_f[:]),
                               reads=[b_ya, b_idf], writes=[PBB[pb]], inc=(a == 3))
                      q = j % 2
                      P.op("dve", lambda j=j, pb=pb, q=q: V_.tensor_tensor(TMq[q][:], PB[pb][:], BTt[:, j, :], ALU.add),
                           reads=[PBB[pb], b_bt], writes=[b_tmq[q]])
                      P.op("pool", lambda j=j, q=q: G_.tensor_tensor(OA[:, j, :], TMq[q][:], GTt[:, j, :], ALU.mult),
                           reads=[b_tmq[q], b_gt], writes=[b_oa])
                  P.dma("sp", OT[0:1024, t0:t0 + 512].rearrange("(j p) t -> p j t", p=128), OA[:], reads=[b_oa], writes=[OTB[0][w]])
              P.barrier()
          chk("B3")

          with contextlib.ExitStack() as cx:
              BIG = sb("d_big", [128, NF, 1024], BF16, cx); b_big = [Buf() for _ in range(NF)]
              H2T = sb("d_h2t", [128, NK, 1024], BF16, cx); b_h2 = Buf()
              WR = [sb(f"d_wr{i}", [128, 8192], BF16, cx) for i in range(2)]; b_wr = [Buf(), Buf()]
              XC = [sb(f"d_xc{i}", [128, 512], F32, cx) for i in range(3)]; b_xc = [Buf() for _ in range(3)]
              XN = [sb(f"d_xn{i}", [128, 512], F32, cx) for i in range(3)]; b_xn = [Buf() for _ in range(3)]
              SQc = [sb(f"d_sq{i}", [128, 512], F32, cx) for i in range(2)]; b_sqc = [Buf(), Buf()]
              RSc = sb("d_rs", [128, 1024], F32, cx); b_rsc = Buf()
              SGc = [sb(f"d_sg{i}", [128, 512], F32, cx) for i in range(2)]; b_sgc = [Buf(), Buf()]
              ci = {"w": 0, "x": 0, "n": 0, "q": 0, "s": 0, "pb": 0}
              wo_v = w_out[l].rearrange("(k p) n -> p k n", p=128)
              wg_v = w_g[l].rearrange("(k p) n -> p k n", p=128)
              wu_v = w_u[l].rearrange("(k p) n -> p k n", p=128)
              wd_v = w_d[l].rearrange("(k p) n -> p k n", p=128)
              for (t0, NT, vs) in STS:
                  nsub = NT // 512
                  for k in range(NK):
                      half = 0 if k < 8 else 1
                      P.dma("sp", BIG[:, k, 0:NT], OT[k * 128:(k + 1) * 128, t0:t0 + NT],
                            reads=[OTB[half][t0 // 512 + s_] for s_ in range(nsub)], writes=[b_big[k]])
                  for dq in range(4):
                      wi = ci["w"] % 2
                      ci["w"] += 1
                      wr = WR[wi][:].rearrange("p (k n) -> p k n", n=512)
                      P.dma("pool", wr, wo_v[:, :, dq * 512:(dq + 1) * 512], writes=[b_wr[wi]])
                      for db in range(4):
                          kd_ = dq * 4 + db
                          for sub in range(nsub):
                              t5 = t0 // 512 + sub
                              pb = 2 + ci["pb"] % 6
                              ci["pb"] += 1
                              for k in range(NK):
                                  P.op("pe", lambda k=k, pb=pb, db=db, sub=sub, wr=wr: nc.tensor.matmul(
                                      PB[pb][:], wr[:, k, db * 128:(db + 1) * 128], BIG[:, k, sub * 512:(sub + 1) * 512],
                                      start=(k == 0), stop=(k == NK - 1)),
                                      reads=[b_wr[wi], b_big[k]], writes=[PBB[pb]], inc=(k == NK - 1),
                                      rec_reads=[b_wr[wi]] + b_big[0:NK])
                              xi = ci["x"] % 3
                              ci["x"] += 1
                              P.dma("sp", XC[xi][:], XSRC[kd_ * 128:(kd_ + 1) * 128, t5 * 512:(t5 + 1) * 512],
                                    reads=[XB[kd_][t5]], writes=[b_xc[xi]])
                              ni = ci["n"] % 3
                              ci["n"] += 1
                              P.op("dve", lambda pb=pb, xi=xi, ni=ni, kd_=kd_: V_.scalar_tensor_tensor(
                                  out=XN[ni][:], in0=PB[pb][:], scalar=modv(l, 32, kd_, vs), in1=XC[xi][:], op0=ALU.mult, op1=ALU.add),
                                  reads=[PBB[pb], b_xc[xi], b_modt], writes=[b_xn[ni]])
                              P.dma("sp", XRES[kd_ * 128:(kd_ + 1) * 128, t5 * 512:(t5 + 1) * 512], XN[ni][:],
                                    reads=[b_xn[ni]], writes=[XB[kd_][t5]])
                              qi = ci["q"] % 2
                              ci["q"] += 1
                              P.op("act", lambda ni=ni, qi=qi: A_.activation(out=SQc[qi][:], in_=XN[ni][:], func=AF.Square),
                                   reads=[b_xn[ni]], writes=[b_sqc[qi]])
                              P.op("pe", lambda qi=qi, sub=sub, kd_=kd_: nc.tensor.matmul(
                                  PB[sub][:], ones_f[:], SQc[qi][:], start=(kd_ == 0), stop=(kd_ == NK - 1)),
                                  reads=[b_sqc[qi], b_ones], writes=[PBB[sub]])
                  for sub in range(nsub):
                      rs = RSc[:, sub * 512:(sub + 1) * 512]
                      P.op("dve", lambda rs=rs, sub=sub: V_.tensor_scalar(rs, PB[sub][:], 1.0 / D, RMS_EPS, ALU.mult, ALU.add),
                           reads=[PBB[sub]], writes=[b_rsc])
                      P.op("act", lambda rs=rs: A_.activation(out=rs, in_=rs, func=AF.Sqrt), reads=[b_rsc], writes=[b_rsc])
                      P.op("dve", lambda rs=rs: V_.reciprocal(rs, rs), reads=[b_rsc], writes=[b_rsc])
                  for k in range(NK):
                      for sub in range(nsub):
                          t5 = t0 // 512 + sub
                          xi = ci["x"] % 3
                          ci["x"] += 1
                          P.dma("sp", XC[xi][:], XRES[k * 128:(k + 1) * 128, t5 * 512:(t5 + 1) * 512],
                                reads=[XB[k][t5]], writes=[b_xc[xi]])
                          qi = ci["q"] % 2
                          ci["q"] += 1
                          P.op("dve", lambda xi=xi, qi=qi, sub=sub: V_.tensor_tensor(SQc[qi][:], XC[xi][:], RSc[:, sub * 512:(sub + 1) * 512], ALU.mult),
                               reads=[b_xc[xi], b_rsc], writes=[b_sqc[qi]])
                          P.op("act", lambda qi=qi, k=k, sub=sub: A_.activation(
                              out=H2T[:, k, sub * 512:(sub + 1) * 512], in_=SQc[qi][:], func=AF.Identity,
                              scale=SCL[:, l, 1, k, vs:vs + 1], bias=modv(l, 48, k, vs)),
                              reads=[b_sqc[qi], b_scl, b_modt], writes=[b_h2])
                  for fp in range(NF // 2):
                      wi = ci["w"] % 2
                      ci["w"] += 1
                      wr = WR[wi][:].rearrange("p (m k n) -> p m k n", m=2, n=256)
                      P.dma("pool", wr[:, 0], wg_v[:, :, fp * 256:(fp + 1) * 256], writes=[b_wr[wi]])
                      P.dma("pool", wr[:, 1], wu_v[:, :, fp * 256:(fp + 1) * 256], join=[b_wr[wi]])
                      for fb in range(2):
                          f = fp * 2 + fb
                          for sub in range(nsub):
                              pg = 2 + ci["pb"] % 6
                              ci["pb"] += 1
                              pu = 2 + ci["pb"] % 6
                              ci["pb"] += 1
                              for m, pb in ((0, pg), (1, pu)):
                                  for k in range(NK):
                                      P.op("pe", lambda k=k, pb=pb, m=m, fb=fb, sub=sub, wr=wr: nc.tensor.matmul(
                                          PB[pb][:], wr[:, m, k, fb * 128:(fb + 1) * 128], H2T[:, k, sub * 512:(sub + 1) * 512],
                                          start=(k == 0), stop=(k == NK - 1)),
                                          reads=[b_wr[wi], b_h2], writes=[PBB[pb]], inc=(k == NK - 1))
                              si = ci["s"] % 2
                              ci["s"] += 1
                              P.op("act", lambda si=si, pg=pg: A_.activation(out=SGc[si][:], in_=PB[pg][:], func=AF.Silu),
                                   reads=[PBB[pg]], writes=[b_sgc[si]])
                              P.op("dve", lambda si=si, pu=pu, f=f, sub=sub: V_.tensor_tensor(
                                  BIG[:, f, sub * 512:(sub + 1) * 512], PB[pu][:], SGc[si][:], ALU.mult),
                                  reads=[PBB[pu], b_sgc[si]], writes=[b_big[f]])
                  for kd_ in range(NK):
                      wi = ci["w"] % 2
                      ci["w"] += 1
                      wr = WR[wi][:, 0:NF * 128].rearrange("p (k n) -> p k n", n=128)
                      P.dma("pool", wr, wd_v[:, :, kd_ * 128:(kd_ + 1) * 128], writes=[b_wr[wi]])
                      for sub in range(nsub):
                          t5 = t0 // 512 + sub
                          pb = 2 + ci["pb"] % 6
                          ci["pb"] += 1
                          for f in range(NF):
                              P.op("pe", lambda f=f, pb=pb, sub=sub, wr=wr: nc.tensor.matmul(
                                  PB[pb][:], wr[:, f, :], BIG[:, f, sub * 512:(sub + 1) * 512], start=(f == 0), stop=(f == NF - 1)),
                                  reads=[b_wr[wi], b_big[f]], writes=[PBB[pb]], inc=(f == NF - 1),
                                  rec_reads=[b_wr[wi]] + b_big)
                          xi = ci["x"] % 3
                          ci["x"] += 1
                          P.dma("sp", XC[xi][:], XRES[kd_ * 128:(kd_ + 1) * 128, t5 * 512:(t5 + 1) * 512],
                                reads=[XB[kd_][t5]], writes=[b_xc[xi]])
                          ni = ci["n"] % 3
                          ci["n"] += 1
                          P.op("dve", lambda pb=pb, xi=xi, ni=ni, kd_=kd_: V_.scalar_tensor_tensor(
                              out=XN[ni][:], in0=PB[pb][:], scalar=modv(l, 80, kd_, vs), in1=XC[xi][:], op0=ALU.mult, op1=ALU.add),
                              reads=[PBB[pb], b_xc[xi], b_modt], writes=[b_xn[ni]])
                          P.dma("sp", XRES[kd_ * 128:(kd_ + 1) * 128, t5 * 512:(t5 + 1) * 512], XN[ni][:],
                                reads=[b_xn[ni]], writes=[XB[kd_][t5]])
              P.barrier()
          chk("C")

    except _Stop:
        P.barrier()
        return nc

    with contextlib.ExitStack() as cx:
        XT_ = sb("f_xt", [128, NK, 512], F32, cx); b_xt = Buf()
        SQ_ = [sb(f"f_sq{i}", [128, 512], F32, cx) for i in range(2)]; b_sq = [Buf(), Buf()]
        RS_ = sb("f_rs", [128, 512], F32, cx); b_rs = Buf()
        YT_ = [sb(f"f_yt{i}", [128, NK, 512], F32, cx) for i in range(2)]; b_yt = [Buf(), Buf()]
        for t5 in range(NT5):
            yi = t5 % 2
            rms_tile((XT_, b_xt, SQ_, b_sq, RS_, b_rs), XRES, XB, t5, YT_[yi], 0, b_yt[yi],
                     lambda k: NFt[:, k:k + 1], lambda k: 0.0, t5 % 2)
            P.dma("sp", yT.rearrange("(k p) t -> p k t", p=128)[:, :, t5 * 512:(t5 + 1) * 512], YT_[yi][:],
                  reads=[b_yt[yi]], writes=[])
        P.barrier()

    P.barrier()
    P.es.close()
    return nc


def _host_layout(inp, L, TS, core_sample, core_prompts, half=None):
    f = lambda a: np.ascontiguousarray(np.asarray(a, dtype=np.float32))
    mir = (half == 1)
    xs = np.asarray(inp["x_sample"][core_sample], np.float32)
    if half is not None:
        tsl = xs.shape[0] // 2
        xs = xs[half * tsl:(half + 1) * tsl]
    xp = [np.asarray(inp["x_prompt"][p], np.float32) for p in core_prompts]
    if mir:
        xs = xs[::-1]
        xp = [a[::-1] for a in xp]
    x = np.concatenate([xs] + xp, axis=0)
    m = {}
    m["xT"] = f(x.T)
    c = np.asarray(inp["c"][core_sample], np.float32)
    cc = np.asarray(inp["c_ctx"], np.float32)
    cv = np.stack([c.reshape(NK, 128).T, cc.reshape(NK, 128).T], axis=-1)
    m["cvec"] = f(cv)
    st = np.asarray(inp["state_rwkv"])[core_sample]
    m["state0"] = f(st[:, ::-1] if mir else st)

    def pT(a, n):
        a = np.asarray(a, np.float32).reshape(L, n, 128)
        return f(a.transpose(2, 0, 1))
    m["bmodT"] = pT(inp["b_mod"], 96)
    m["n1T"] = pT(inp["norm1_g"], NK)
    m["n2T"] = pT(inp["norm2_g"], NK)
    m["nfT"] = f(np.asarray(inp["final_norm_g"], np.float32).reshape(NK, 128).T)
    mu = np.asarray(inp["mu_shift"], np.float32)
    if mir:
        mu = mu[:, [1, 0, 3, 2], :]
    mup = np.zeros((L, 4, 27 * 128), np.float32)
    mup[:, :, :CS] = mu
    m["muT"] = f(mup.reshape(L, 4, 27, 128).transpose(3, 0, 2, 1))
    dsw = (lambda a: a[:, ::-1]) if mir else (lambda a: a)
    m["w0T"] = f(dsw(np.asarray(inp["w0"], np.float32)).reshape(L, 2, 8, 128).transpose(3, 0, 1, 2))
    m["a0T"] = f(dsw(np.asarray(inp["a0"], np.float32)).reshape(L, 2, 8, 128).transpose(3, 0, 1, 2))
    m["w2"] = f(dsw(np.asarray(inp["w2"], np.float32)))
    m["a2"] = f(dsw(np.asarray(inp["a2"], np.float32)))
    m["kkT"] = pT(inp["k_k"], 8)
    m["kaT"] = pT(inp["k_a"], 8)
    m["rkT"] = pT(np.asarray(inp["r_k"]).reshape(L, 1024), 8)
    m["gnw"] = f(np.asarray(inp["gn_w"]).reshape(L, 1024))
    m["gnb"] = f(np.asarray(inp["gn_b"]).reshape(L, 1024))
    m["lng"] = f(np.asarray(inp["gmlp_ln_g"]).reshape(L, 1024))
    m["lnb"] = f(np.asarray(inp["gmlp_ln_b"]).reshape(L, 1024))
    ws = np.asarray(inp["w_spatial"], np.float32)
    bs = np.asarray(inp["b_spatial"], np.float32)
    if mir:
        ws = ws[:, :, ::-1, ::-1]
        bs = bs[:, :, ::-1]
    m["wsT"] = f(ws.transpose(0, 3, 1, 2))
    m["bsp"] = f(bs.reshape(L, 1, 1024))
    if half is not None:
        sel = np.zeros((128, 2), np.float32)
        sel[:, 1 - half] = 1.0
        m["msel"] = sel
    return m


def kernel(**inp):
    L = int(np.asarray(inp["w_in"]).shape[0])
    TSF = int(np.asarray(inp["x_sample"]).shape[1])
    NB = int(np.asarray(inp["x_prompt"]).shape[0])
    NSMP = int(np.asarray(inp["x_sample"]).shape[0])
    ncores = NB // NPR
    pair = (ncores == 2 * NSMP) and (TSF % 1024 == 0)
    TS = TSF // 2 if pair else TSF
    nc = build(L, TS, pair=pair, ncores=ncores)
    shared = {}
    for k_, src in (("w_mod", "w_mod"), ("w_in", "w_in"), ("g2", "g2"),
                    ("w_out", "w_out"), ("w_g", "w_ffn_gate"), ("w_u", "w_ffn_up"), ("w_d", "w_ffn_down")):
        shared[k_] = np.ascontiguousarray(np.asarray(inp[src], dtype=np.float32))
    in_maps = []
    per = ncores // NSMP
    for c in range(ncores):
        s = c // per
        m = _host_layout(inp, L, TSF, s, [NPR * c + i for i in range(NPR)], half=(c % 2 if pair else None))
        m.update(shared)
        in_maps.append(m)
    res = run_bass_kernel_spmd(nc, in_maps, core_ids=list(range(ncores)))
    y_prompt = np.zeros((NB, SEQP, D), np.float32)
    y_sample = np.zeros((NSMP, TSF, D), np.float32)
    new_state = np.zeros((NB, L, 2, NH, HD, HD), np.float32)
    for c in range(ncores):
        r = res.results[c]
        mir = pair and (c % 2 == 1)
        y = np.asarray(r["yT"]).T
        nst_c = np.asarray(r["nst"])
        for i in range(NPR):
            yp = y[TS + i * SEQP:TS + (i + 1) * SEQP]
            y_prompt[NPR * c + i] = yp[::-1] if mir else yp
            new_state[NPR * c + i] = nst_c[i][:, ::-1] if mir else nst_c[i]
        ys = y[:TS]
        if pair:
            h = c % 2
            y_sample[c // 2, h * TS:(h + 1) * TS] = ys[::-1] if mir else ys
        elif c % per == 0:
            y_sample[c // per] = ys
    return (y_prompt, y_sample, new_state)
```

```python
import contextlib
import numpy as np
import concourse.bass as bass
import concourse.mybir as mybir
from concourse.bass_utils import run_bass_kernel_spmd

F32 = mybir.dt.float32
BF16 = mybir.dt.bfloat16
AF = mybir.ActivationFunctionType
ALU = mybir.AluOpType
AX = mybir.AxisListType

D = 2048
NK = 16
CS = 3360
PIN = 5408
DFF = 5632
NF = 44
NH = 16
HD = 64
SEQP = 256
NPR = 2
RMS_EPS = 1e-6
GN_EPS = 64 * 1e-5
LN_EPS = 1e-5
CDEC = 0.6065306597126334

EPOCH = 12000
N_DMA_SEMS = 24


class Buf:
    __slots__ = ("name", "writers", "readers", "excl")

    def __init__(self, name="", excl=False):
        self.name = name
        self.writers = {}
        self.readers = {}
        self.excl = excl


class Prog:
    def __init__(self, nc):
        self.nc = nc
        self.es = contextlib.ExitStack()
        self.eng = {"pe": nc.tensor, "act": nc.scalar, "dve": nc.vector,
                    "pool": nc.gpsimd, "sp": nc.sync}
        self.sems = {e: [] for e in self.eng}
        self.cnt = {e: 0 for e in self.eng}
        self.seen = {e: {} for e in self.eng}
        self.dma_sems = {}
        self.dma_vals = {}
        self.dma_next = {}
        self.all_dma = {}
        self.n_ins = 0
        for q in ("sp", "pool", "cc"):
            n = N_DMA_SEMS if q != "cc" else 2
            self.dma_sems[q] = [self.es.enter_context(nc.semaphore(f"dq_{q}_{i}")) for i in range(n)]
            self.dma_vals[q] = [0] * n
            self.dma_next[q] = 0
        for e in self.eng:
            self._new_epoch(e)

    def _new_epoch(self, e):
        s = self.es.enter_context(self.nc.semaphore(f"s_{e}_{len(self.sems[e])}"))
        self.sems[e].append(s)
        self.cnt[e] = 0

    def _wait_list(self, e, events):
        need = {}
        for ev in events:
            if ev is None:
                continue
            if ev[0] == "eng":
                _, src, ep, c = ev
                if src == e and e == "pe":
                    continue
                key = ("eng", src, ep)
                sem = self.sems[src][ep]
            else:
                _, q, i, c = ev
                key = ("dma", q, i)
                sem = self.dma_sems[q][i]
            if self.seen[e].get(key, 0) >= c:
                continue
            if key not in need or need[key][1] < c:
                need[key] = (sem, c)
        for key, (sem, c) in need.items():
            self.seen[e][key] = c
        return list(need.values())

    @staticmethod
    def _deps(reads, writes):
        evs = []
        for b in reads:
            evs.extend(b.writers.values())
        for b in writes:
            evs.extend(b.writers.values())
            evs.extend(b.readers.values())
        return evs

    @staticmethod
    def _record(ev, key, reads, writes):
        for b in writes:
            b.writers = {key: ev}
            b.readers = {}
        for b in reads:
            b.readers[key] = ev

    def op(self, e, fn, reads=(), writes=(), inc=True, rec_reads=None):
        if any(b.excl for b in reads):
            writes = list(writes) + [b for b in reads if b.excl]
            reads = [b for b in reads if not b.excl]
        waits = self._wait_list(e, self._deps(reads, writes))
        eng = self.eng[e]
        for sem, c in waits[:-1]:
            eng.wait_ge(sem, c)
        ins = fn()
        if waits:
            ins._wait_ge(waits[-1][0], waits[-1][1])
        self.n_ins += 1
        if not inc:
            return None
        if self.cnt[e] >= EPOCH:
            self._new_epoch(e)
        self.cnt[e] += 1
        ep = len(self.sems[e]) - 1
        ins.then_inc(self.sems[e][ep], 1)
        ev = ("eng", e, ep, self.cnt[e])
        self._record(ev, e, reads if rec_reads is None else rec_reads, writes)
        return ev

    def dma(self, q, out, in_, reads=(), writes=(), join=(), **kw):
        eng = self.eng[q]
        i = self.dma_next[q]
        self.dma_next[q] = (i + 1) % N_DMA_SEMS
        prev = ("dma", q, i, self.dma_vals[q][i]) if self.dma_vals[q][i] else None
        waits = self._wait_list(q, self._deps(reads, writes) + [prev])
        for sem, c in waits:
            eng.wait_ge(sem, c)
        self.dma_vals[q][i] += 16
        v = self.dma_vals[q][i]
        eng.dma_start(out=out, in_=in_, **kw).then_inc(self.dma_sems[q][i], 16)
        self.n_ins += 1
        ev = ("dma", q, i, v)
        self.all_dma[(q, i)] = ev
        self._record(ev, ("dma", q, i, v), reads, writes)
        for b in join:
            b.writers[("dma", q, i, v)] = ev
        return ev

    def coll(self, ins_ap, outs_ap, groups, reads=(), writes=()):
        q = "cc"
        eng = self.eng["pool"]
        i = self.dma_next[q]
        self.dma_next[q] = (i + 1) % len(self.dma_sems[q])
        prev = ("dma", q, i, self.dma_vals[q][i]) if self.dma_vals[q][i] else None
        for sem, c in self._wait_list("pool", self._deps(reads, writes) + [prev]):
            eng.wait_ge(sem, c)
        self.dma_vals[q][i] += 1
        v = self.dma_vals[q][i]
        eng.collective_compute("AllGather", ALU.bypass, replica_groups=groups, ins=[ins_ap], outs=[outs_ap]) \
            .then_inc(self.dma_sems[q][i], 1)
        self.n_ins += 1
        ev = ("dma", q, i, v)
        self.all_dma[(q, i)] = ev
        self._record(ev, ("dma", q, i, v), reads, writes)
        return ev

    def barrier(self):
        evs = []
        for e in self.eng:
            ep = len(self.sems[e]) - 1
            if self.cnt[e] > 0:
                evs.append(("eng", e, ep, self.cnt[e]))
            if ep > 0:
                evs.append(("eng", e, ep - 1, EPOCH))
        evs.extend(self.all_dma.values())
        for e in self.eng:
            for sem, c in self._wait_list(e, evs):
                self.eng[e].wait_ge(sem, c)


class _Stop(Exception):
    pass


def build(L, TS, dbg=False, stop=None, pair=False, ncores=8):
    nc = bass.Bass("TRN2", target_bir_lowering=False)
    P = Prog(nc)
    TT = TS + NPR * SEQP
    NT5 = TT // 512
    NCH = TT // 128
    NW = NT5
    c_ = CDEC

    def din(name, shape, dt=F32):
        return nc.dram_tensor(name, list(shape), dt, kind="ExternalInput").ap()

    def dscr(name, shape, dt=F32):
        kind = "ExternalOutput" if dbg else "Internal"
        return nc.dram_tensor(name, list(shape), dt, kind=kind).ap()

    xT_in = din("xT", [D, TT])
    cvec = din("cvec", [128, NK, 2])
    state0 = din("state0", [L, 2, NH, HD, HD])
    w_mod = din("w_mod", [L, D, 6 * D])
    bmodT = din("bmodT", [128, L, 96])
    n1T = din("n1T", [128, L, NK])
    n2T = din("n2T", [128, L, NK])
    nfT = din("nfT", [128, NK])
    w_in = din("w_in", [L, D, PIN])
    muT = din("muT", [128, L, 27, 4])
    w0T = din("w0T", [128, L, 2, 8])
    a0T = din("a0T", [128, L, 2, 8])
    w2 = din("w2", [L, 2, 64, 1024])
    a2 = din("a2", [L, 2, 64, 1024])
    g2 = din("g2", [L, 160, 1024])
    kkT = din("kkT", [128, L, 8])
    kaT = din("kaT", [128, L, 8])
    rkT = din("rkT", [128, L, 8])
    gnw = din("gnw", [L, 1024])
    gnb = din("gnb", [L, 1024])
    lng = din("lng", [L, 1024])
    lnb = din("lnb", [L, 1024])
    wsT = din("wsT", [L, 128, 8, 128])
    bsp = din("bsp", [L, 1, 1024])
    w_out = din("w_out", [L, D, D])
    w_g = din("w_g", [L, D, DFF])
    w_u = din("w_u", [L, D, DFF])
    w_d = din("w_d", [L, DFF, D])
    msel = din("msel", [128, 2]) if pair else None
    GROUPS = [[2 * i, 2 * i + 1] for i in range(ncores // 2)]

    yT = nc.dram_tensor("yT", [D, TT], F32, kind="ExternalOutput").ap()
    nst = nc.dram_tensor("nst", [NPR, L, 2, NH, HD, HD], F32, kind="ExternalOutput").ap()

    XRES = dscr("XRES", [D, TT])
    PT = dscr("PT", [27 * 128, TT])
    UTd = dscr("UTd", [1024, TT], BF16)
    VTd = dscr("VTd", [TT, 1024])
    OT = dscr("OT", [D, TT], BF16)
    SCd = dscr("SCd", [NCH, 2, 8, 128, 4, 128], BF16)
    VBd = dscr("VBd", [NCH, 8, 128, 128], BF16)
    BTd = dscr("BTd", [1024, TT])
    GTd = dscr("GTd", [1024, TT])
    YGd = dscr("YGd", [2, TT, 1024])
    if pair:
        HSEND = nc.dram_tensor("HSEND", [27 * 128, 64], F32).ap()
        HGATH = nc.dram_tensor("HGATH", [2 * 27 * 128, 64], F32).ap()
        SSEND = nc.dram_tensor("SSEND", [128, 512], F32).ap()
        SGATH = nc.dram_tensor("SGATH", [2 * 128, 512], F32).ap()
        b_hsend, b_hgath, b_ssend, b_sgath = Buf(), Buf(), Buf(), Buf()

    XB = [[Buf() for _ in range(NT5)] for _ in range(NK)]
    PTB = [[Buf() for _ in range(NT5)] for _ in range(27)]
    UTB = [[Buf() for _ in range(NT5)] for _ in range(8)]
    VTB = [[Buf() for _ in range(8)] for _ in range(NT5)]
    OTB = [[Buf() for _ in range(NT5)] for _ in range(2)]
    SCB = [[[Buf() for _ in range(8)] for _ in range(2)] for _ in range(NW)]
    VBB = [[Buf() for _ in range(8)] for _ in range(NW)]
    BTB = [[Buf() for _ in range(8)] for _ in range(NW)]
    GTB = [[Buf() for _ in range(8)] for _ in range(NW)]
    YGB = [[Buf() for _ in range(NCH)] for _ in range(2)]

    es = P.es

    uid = {"n": 0}

    def sb(name, shape, dt=F32, ctx=None):
        uid["n"] += 1
        return (ctx or es).enter_context(nc.sbuf_tensor(f"{name}_{uid['n']}", list(shape), dt))

    PB = [es.enter_context(nc.psum_tensor(f"pb{i}", [128, 512], F32)) for i in range(8)]
    PBB = [Buf(f"pb{i}", excl=True) for i in range(8)]

    ones_f = sb("ones_f", [128, 128]); b_ones = Buf()
    blk_f = sb("blk_f", [128, 128]); b_blk = Buf()
    id_f = sb("id_f", [128, 128]); b_idf = Buf()
    id_b = sb("id_b", [128, 128], BF16); b_idb = Buf()
    ones_row = sb("ones_row", [1, 128], BF16); b_onesrow = Buf()
    MASK = [sb(f"mask{d}", [128, 512]) for d in range(2)]; b_mask = [Buf(), Buf()]
    NMASK = [sb(f"nmask{d}", [128, 128]) for d in range(2)]; b_nmask = [Buf(), Buf()]
    RST = sb("rst", [128, 512]); b_rst = Buf()

    V_, G_, A_, S_ = nc.vector, nc.gpsimd, nc.scalar, nc.sync

    P.op("dve", lambda: V_.memset(ones_f[:], 1.0), writes=[b_ones])
    P.op("dve", lambda: V_.memset(ones_row[:], 1.0), writes=[b_onesrow])
    P.op("dve", lambda: V_.memset(blk_f[:], 0.0), writes=[b_blk])
    P.op("dve", lambda: V_.memset(blk_f[0:64, 0:64], 1.0), reads=[b_blk], writes=[b_blk])
    P.op("dve", lambda: V_.memset(blk_f[64:128, 64:128], 1.0), reads=[b_blk], writes=[b_blk])
    P.op("dve", lambda: V_.memset(id_f[:], 0.0), writes=[b_idf])
    P.op("pool", lambda: G_.affine_select(out=id_f[:], in_=id_f[:], pattern=[[-1, 128]], compare_op=ALU.not_equal,
                                          fill=1.0, base=0, channel_multiplier=1), reads=[b_idf], writes=[b_idf])
    P.op("dve", lambda: V_.tensor_copy(id_b[:], id_f[:]), reads=[b_idf], writes=[b_idb])

    def tri(dst, cm, coef, base, cmp, bufs):
        P.op("dve", lambda: V_.memset(dst, 1.0), writes=bufs)
        P.op("pool", lambda: G_.affine_select(out=dst, in_=dst, pattern=[[coef, 128]], compare_op=cmp,
                                              fill=0.0, base=base, channel_multiplier=cm), reads=bufs, writes=bufs)
    for q in range(4):
        strict = (q % 2 == 0)
        tri(MASK[0][:, q * 128:(q + 1) * 128], -1, 1, 0, ALU.is_gt if strict else ALU.is_ge, [b_mask[0]])
        tri(MASK[1][:, q * 128:(q + 1) * 128], 1, -1, 0, ALU.is_gt if strict else ALU.is_ge, [b_mask[1]])
    tri(NMASK[0][:], 1, -1, 0, ALU.is_gt, [b_nmask[0]])
    tri(NMASK[1][:], -1, 1, 0, ALU.is_gt, [b_nmask[1]])
    P.op("dve", lambda: V_.memset(RST[:], 1.0), writes=[b_rst])
    for c in range(4):
        P.op("dve", lambda c=c: V_.memset(RST[:, c * 128:c * 128 + 1], 0.0), reads=[b_rst], writes=[b_rst])

    MODT = sb("modt", [128, L, 96, 2]); b_modt = Buf()
    SCL = sb("scl", [128, L, 2, NK, 2]); b_scl = Buf()
    N1 = sb("n1", [128, L, NK]); N2 = sb("n2", [128, L, NK]); NFt = sb("nf", [128, NK]); b_n = Buf()
    MU = sb("mu", [128, L, 27, 6]); b_mu = Buf()
    W0 = sb("w0", [128, L, 2, 8]); A0 = sb("a0", [128, L, 2, 8])
    KKp = sb("kkp", [128, L, 8]); KAp = sb("kap", [128, L, 8]); RKp = sb("rkp", [128, L, 8]); b_par = Buf()
    BMT = sb("bmt", [128, L, 96])
    CV = sb("cv", [128, NK, 2]); CSb = sb("csb", [128, NK, 2], BF16); b_cv = Buf()
    MU4 = sb("mu4", [128, L, 27, 4])

    P.dma("sp", N1[:], n1T, writes=[b_n])
    P.dma("sp", N2[:], n2T, writes=[b_n])
    P.dma("sp", NFt[:], nfT, writes=[b_n])
    P.dma("sp", MU4[:], muT, writes=[b_mu])
    P.dma("sp", W0[:], w0T, writes=[b_par])
    P.dma("sp", A0[:], a0T, writes=[b_par])
    P.dma("sp", KKp[:], kkT, writes=[b_par])
    P.dma("sp", KAp[:], kaT, writes=[b_par])
    P.dma("sp", RKp[:], rkT, writes=[b_par])
    P.dma("sp", BMT[:], bmodT, writes=[b_modt])
    P.dma("sp", CV[:], cvec, writes=[b_cv])
    if pair:
        MSEL = sb("msel_t", [128, 2]); b_msel = Buf()
        P.dma("sp", MSEL[:], msel, writes=[b_msel])
    P.op("act", lambda: A_.activation(out=CSb[:], in_=CV[:], func=AF.Silu), reads=[b_cv], writes=[b_cv])
    P.op("dve", lambda: V_.tensor_copy(MU[:, :, :, 0:4], MU4[:]), reads=[b_mu], writes=[b_mu])
    P.op("dve", lambda: V_.reduce_sum(MU[:, :, :, 4], MU4[:], axis=AX.X), reads=[b_mu], writes=[b_mu])
    P.op("dve", lambda: V_.tensor_scalar(MU[:, :, :, 4], MU[:, :, :, 4], -1.0, 1.0, ALU.mult, ALU.add), reads=[b_mu], writes=[b_mu])
    P.op("dve", lambda: V_.tensor_tensor(MU[:, :, :, 5], MU4[:, :, :, 0], MU4[:, :, :, 1], ALU.add), reads=[b_mu], writes=[b_mu])
    P.op("dve", lambda: V_.tensor_scalar(MU[:, :, :, 5], MU[:, :, :, 5], -1.0, 1.0, ALU.mult, ALU.add), reads=[b_mu], writes=[b_mu])

    with contextlib.ExitStack() as cx:
        WM = [sb(f"wm{i}", [128, NK, 1024], BF16, cx) for i in range(2)]
        b_wm = [Buf(), Buf()]
        pc = 0
        for l in range(L):
            wv = w_mod[l].rearrange("(k p) n -> p k n", p=128)
            for piece in range(12):
                t = pc % 2
                pc += 1
                P.dma("pool", WM[t][:], wv[:, :, piece * 1024:(piece + 1) * 1024], writes=[b_wm[t]])
                for jb in range(8):
                    j = piece * 8 + jb
                    for k in range(NK):
                        last = (k == NK - 1) and (jb == 7)
                        P.op("pe", lambda t=t, jb=jb, k=k, j=j: nc.tensor.matmul(
                            PB[0][:, 2 * j:2 * j + 2], WM[t][:, k, jb * 128:(jb + 1) * 128], CSb[:, k, :],
                            start=(k == 0), stop=(k == NK - 1)),
                            reads=[b_wm[t], b_cv], writes=[PBB[0]], inc=last)
            P.op("dve", lambda l=l: V_.tensor_tensor(
                MODT[:, l], PB[0][:, 0:192].rearrange("p (j v) -> p j v", v=2),
                BMT[:, l, :].unsqueeze(2).to_broadcast([128, 96, 2]), ALU.add),
                reads=[PBB[0], b_modt], writes=[b_modt])
            for wn, (Nt, off) in enumerate(((N1, 16), (N2, 64))):
                P.op("dve", lambda l=l, wn=wn, Nt=Nt, off=off: V_.scalar_tensor_tensor(
                    out=SCL[:, l, wn], in0=MODT[:, l, off:off + 16, :], scalar=1.0,
                    in1=Nt[:, l, :].unsqueeze(2).to_broadcast([128, NK, 2]), op0=ALU.add, op1=ALU.mult),
                    reads=[b_modt, b_n], writes=[b_scl])
        P.barrier()
    if stop == "M":
        P.barrier(); P.es.close(); return nc

    def modv(l, off, k, v):
        return MODT[:, l, off + k, v:v + 1]

    def super_tiles():
        res = []
        t0 = 0
        while t0 < TS:
            nt = 1024 if TS - t0 >= 1024 else 512
            res.append((t0, nt, 0))
            t0 += nt
        res.append((TS, 512, 1))
        return res

    STS = super_tiles()
    rr = {"pb": 0}

    def rms_tile(cx_tiles, src_dram, srcB, t5, dst_ht, dst_col0, b_dst, scale_fn, bias_fn, ssbank, extra=None):
        XT_, b_xt, SQ_, b_sq, RS_, b_rs = cx_tiles
        P.dma("sp", XT_[:], src_dram.rearrange("(k p) t -> p k t", p=128)[:, :, t5 * 512:(t5 + 1) * 512],
              reads=[srcB[k][t5] for k in range(NK)], writes=[b_xt])
        for k in range(NK):
            s = k % 2
            P.op("act", lambda k=k, s=s: A_.activation(out=SQ_[s][:], in_=XT_[:, k, :], func=AF.Square),
                 reads=[b_xt], writes=[b_sq[s]])
            P.op("pe", lambda k=k, s=s: nc.tensor.matmul(PB[ssbank][:], ones_f[:], SQ_[s][:], start=(k == 0), stop=(k == NK - 1)),
                 reads=[b_sq[s], b_ones], writes=[PBB[ssbank]])
        P.op("dve", lambda: V_.tensor_scalar(RS_[:], PB[ssbank][:], 1.0 / D, RMS_EPS, ALU.mult, ALU.add),
             reads=[PBB[ssbank]], writes=[b_rs])
        P.op("act", lambda: A_.activation(out=RS_[:], in_=RS_[:], func=AF.Sqrt), reads=[b_rs], writes=[b_rs])
        P.op("dve", lambda: V_.reciprocal(RS_[:], RS_[:]), reads=[b_rs], writes=[b_rs])
        for k in range(NK):
            s = k % 2
            P.op("dve", lambda k=k, s=s: V_.tensor_tensor(SQ_[s][:], XT_[:, k, :], RS_[:], ALU.mult),
                 reads=[b_xt, b_rs, b_sq[s]], writes=[b_sq[s]])
            P.op("act", lambda k=k, s=s: A_.activation(out=dst_ht[:, k, dst_col0:dst_col0 + 512], in_=SQ_[s][:],
                                                        func=AF.Identity, scale=scale_fn(k), bias=bias_fn(k)),
                 reads=[b_sq[s], b_scl, b_modt, b_n], writes=[b_dst])

    def chk(ph):
        if stop == ph:
            raise _Stop()

    try:
      for l in range(L):
          XSRC = xT_in if l == 0 else XRES
          with contextlib.ExitStack() as cx:
              XT_ = sb("a_xt", [128, NK, 512], F32, cx); b_xt = Buf()
              SQ_ = [sb(f"a_sq{i}", [128, 512], F32, cx) for i in range(2)]; b_sq = [Buf(), Buf()]
              RS_ = sb("a_rs", [128, 512], F32, cx); b_rs = Buf()
              HT = sb("a_ht", [128, NK, 1024], BF16, cx); b_ht = Buf()
              WIN = [sb(f"a_win{i}", [128, NK, 512], BF16, cx) for i in range(2)]; b_win = [Buf(), Buf()]
              STG = [sb(f"a_stg{i}", [128, 512], F32, cx) for i in range(4)]; b_stg = [Buf() for _ in range(4)]
              STB = [sb(f"a_stb{i}", [128, 512], BF16, cx) for i in range(2)]; b_stb = [Buf() for _ in range(2)]
              wv = w_in[l].rearrange("(k p) n -> p k n", p=128)
              pieces = [(c0, 512) for c0 in range(0, 3072, 512)] + [(3072, 288)] + \
                       [(3360, 512), (3872, 512), (4384, 512), (4896, 512)]
              wc = 0
              sg = 0
              sgb = 0
              for (t0, NT, vs) in STS:
                  nsub = NT // 512
                  for sub in range(nsub):
                      t5 = t0 // 512 + sub
                      rms_tile((XT_, b_xt, SQ_, b_sq, RS_, b_rs), XSRC, XB, t5, HT, sub * 512, b_ht,
                               lambda k: SCL[:, l, 0, k, vs:vs + 1], lambda k: modv(l, 0, k, vs), 0)
                  for (c0, cw) in pieces:
                      wt = wc % 2
                      wc += 1
                      P.dma("pool", WIN[wt][:, :, 0:cw], wv[:, :, c0:c0 + cw], writes=[b_win[wt]])
                      if c0 < 4384:
                          nblk = (cw + 127) // 128
                          for bi in range(nblk):
                              bw = min(128, cw - bi * 128)
                              col = c0 + bi * 128
                              for sub in range(nsub):
                                  t5 = t0 // 512 + sub
                                  pb = 2 + rr["pb"] % 6
                                  rr["pb"] += 1
                                  for k in range(NK):
                                      P.op("pe", lambda k=k, pb=pb, bi=bi, bw=bw, wt=wt, sub=sub: nc.tensor.matmul(
                                          PB[pb][0:bw, :], WIN[wt][:, k, bi * 128:bi * 128 + bw], HT[:, k, sub * 512:(sub + 1) * 512],
                                          start=(k == 0), stop=(k == NK - 1)),
                                          reads=[b_win[wt], b_ht], writes=[PBB[pb]], inc=(k == NK - 1))
                                  if col < CS:
                                      jb = col // 128
                                      s = sg % 4
                                      sg += 1
                                      eng = "act" if (sg % 2) else "dve"
                                      if eng == "act":
                                          P.op("act", lambda s=s, pb=pb, bw=bw: A_.copy(STG[s][0:bw, :], PB[pb][0:bw, :]),
                                               reads=[PBB[pb]], writes=[b_stg[s]])
                                      else:
                                          P.op("dve", lambda s=s, pb=pb, bw=bw: V_.tensor_copy(STG[s][0:bw, :], PB[pb][0:bw, :]),
                                               reads=[PBB[pb]], writes=[b_stg[s]])
                                      P.dma("sp", PT[jb * 128:jb * 128 + bw, t5 * 512:(t5 + 1) * 512], STG[s][0:bw, :],
                                            reads=[b_stg[s]], writes=[PTB[jb][t5]])
                                  else:
                                      g = (col - CS) // 128
                                      s = sgb % 2
                                      sgb += 1
                                      P.op("act", lambda s=s, pb=pb: A_.activation(out=STB[s][:], in_=PB[pb][:], func=AF.Gelu_apprx_tanh),
                                           reads=[PBB[pb]], writes=[b_stb[s]])
                                      P.dma("sp", UTd[g * 128:(g + 1) * 128, t5 * 512:(t5 + 1) * 512], STB[s][:],
                                            reads=[b_stb[s]], writes=[UTB[g][t5]])
                      else:
                          vc0 = c0 - 4384
                          for sub in range(nsub):
                              t5 = t0 // 512 + sub
                              for ts in range(4):
                                  pb = 2 + rr["pb"] % 6
                                  rr["pb"] += 1
                                  for k in range(NK):
                                      P.op("pe", lambda k=k, pb=pb, wt=wt, sub=sub, ts=ts: nc.tensor.matmul(
                                          PB[pb][:], HT[:, k, sub * 512 + ts * 128:sub * 512 + (ts + 1) * 128], WIN[wt][:, k, :],
                                          start=(k == 0), stop=(k == NK - 1)),
                                          reads=[b_win[wt], b_ht], writes=[PBB[pb]], inc=(k == NK - 1))
                                  s = sg % 4
                                  sg += 1
                                  P.op("act", lambda s=s, pb=pb: A_.activation(out=STG[s][:], in_=PB[pb][:], func=AF.Gelu_apprx_tanh),
                                       reads=[PBB[pb]], writes=[b_stg[s]])
                                  tok0 = t5 * 512 + ts * 128
                                  P.dma("sp", VTd[tok0:tok0 + 128, vc0:vc0 + 512], STG[s][:],
                                        reads=[b_stg[s]], writes=[VTB[t5][ts * 2 + vc0 // 512]])
              P.barrier()
          if pair:
              P.dma("sp", HSEND, PT[:, TS - 64:TS], reads=[PTB[j][TS // 512 - 1] for j in range(27)], writes=[b_hsend])
              P.coll(HSEND, HGATH, GROUPS, reads=[b_hsend], writes=[b_hgath])
          chk("A")

          with contextlib.ExitStack() as cx:
              VT_ = sb("g_vt", [128, 4, 1024], F32, cx); b_vt = Buf()
              VC_ = sb("g_vc", [128, 4, 1024], F32, cx); b_vc = Buf()
              SQg = sb("g_sq", [128, 4, 1024], F32, cx); b_sqg = Buf()
              VN_ = sb("g_vn", [128, 4, 1024], BF16, cx); b_vn = Buf()
              UTt = sb("g_ut", [128, 8, 512], BF16, cx); b_ut = Buf()
              OBt = sb("g_ob", [128, 8, 512], BF16, cx); b_ob = Buf()
              LNG = sb("g_lng", [128, 1024], F32, cx); LNB = sb("g_lnb", [128, 1024], F32, cx); b_ln = Buf()
              WST = sb("g_wst", [128, 8, 128], BF16, cx); BSP = sb("g_bsp", [1, 1024], BF16, cx); b_ws = Buf()
              ST1 = sb("g_s1", [128, 32], F32, cx); ST2 = sb("g_s2", [128, 32], F32, cx); b_st = Buf()
              P.dma("sp", LNG[:], lng[l].partition_broadcast(128), writes=[b_ln])
              P.dma("sp", LNB[:], lnb[l].partition_broadcast(128), writes=[b_ln])
              P.dma("pool", WST[:], wsT[l], writes=[b_ws])
              P.dma("pool", BSP[:], bsp[l], writes=[b_ws])
              for t5 in range(NT5):
                  P.dma("sp", VT_[:], VTd[t5 * 512:(t5 + 1) * 512, :].rearrange("(a p) c -> p a c", p=128),
                        reads=VTB[t5], writes=[b_vt])
                  P.dma("sp", UTt[:], UTd[:, t5 * 512:(t5 + 1) * 512].rearrange("(g p) t -> p g t", p=128),
                        reads=[UTB[g][t5] for g in range(8)], writes=[b_ut])
                  V4 = VT_[:].rearrange("p a (g d) -> p (a g) d", d=128)
                  C4 = VC_[:].rearrange("p a (g d) -> p (a g) d", d=128)
                  Q4 = SQg[:].rearrange("p a (g d) -> p (a g) d", d=128)
                  P.op("dve", lambda: V_.reduce_sum(ST1[:], V4, axis=AX.X), reads=[b_vt], writes=[b_st])
                  P.op("dve", lambda: V_.tensor_scalar(ST1[:], ST1[:], 1.0 / 128, None, ALU.mult), reads=[b_st], writes=[b_st])
                  P.op("dve", lambda: V_.tensor_tensor(C4, V4, ST1[:].unsqueeze(2).to_broadcast([128, 32, 128]), ALU.subtract),
                       reads=[b_vt, b_st], writes=[b_vc])
                  P.op("act", lambda: A_.activation(out=SQg[:], in_=VC_[:], func=AF.Square), reads=[b_vc], writes=[b_sqg])
                  P.op("dve", lambda: V_.reduce_sum(ST2[:], Q4, axis=AX.X), reads=[b_sqg], writes=[b_st])
                  P.op("dve", lambda: V_.tensor_scalar(ST2[:], ST2[:], 1.0 / 128, LN_EPS, ALU.mult, ALU.add), reads=[b_st], writes=[b_st])
                  P.op("act", lambda: A_.activation(out=ST2[:], in_=ST2[:], func=AF.Sqrt), reads=[b_st], writes=[b_st])
                  P.op("dve", lambda: V_.reciprocal(ST2[:], ST2[:]), reads=[b_st], writes=[b_st])
                  P.op("dve", lambda: V_.tensor_tensor(C4, C4, ST2[:].unsqueeze(2).to_broadcast([128, 32, 128]), ALU.mult),
                       reads=[b_vc, b_st], writes=[b_vc])
                  P.op("pool", lambda: G_.tensor_tensor(VC_[:], VC_[:], LNG[:].unsqueeze(1).to_broadcast([128, 4, 1024]), ALU.mult),
                       reads=[b_vc, b_ln], writes=[b_vc])
                  P.op("dve", lambda: V_.tensor_tensor(VN_[:], VC_[:], LNB[:].unsqueeze(1).to_broadcast([128, 4, 1024]), ALU.add),
                       reads=[b_vc, b_ln], writes=[b_vn])
                  for g in range(8):
                      pb = g
                      for a in range(4):
                          P.op("pe", lambda a=a, g=g, pb=pb: nc.tensor.matmul(
                              PB[pb][:, a * 128:(a + 1) * 128], VN_[:, a, g * 128:(g + 1) * 128], WST[:, g, :], start=True, stop=False),
                              reads=[b_vn, b_ws], writes=[PBB[pb]], inc=False)
                          P.op("pe", lambda a=a, g=g, pb=pb: nc.tensor.matmul(
                              PB[pb][:, a * 128:(a + 1) * 128], ones_row[0:1, :], BSP[0:1, g * 128:(g + 1) * 128], start=False, stop=True),
                              reads=[b_onesrow, b_ws, b_vn], writes=[PBB[pb]], inc=(a == 3))
                      P.op("dve", lambda g=g, pb=pb: V_.tensor_tensor(OBt[:, g, :], PB[pb][:], UTt[:, g, :], ALU.mult),
                           reads=[PBB[pb], b_ut], writes=[b_ob])
                  P.dma("sp", OT[1024:2048, t5 * 512:(t5 + 1) * 512].rearrange("(g p) t -> p g t", p=128), OBt[:],
                        reads=[b_ob], writes=[OTB[1][t5]])
              P.barrier()
          chk("G")

          GAM = sb(f"gam{l}", [128, 2, 8, NCH], F32); b_gam = Buf()
          with contextlib.ExitStack() as cx:
              GIN = [sb(f"b_gin{i}", [128, 640], F32, cx) for i in range(3)]; b_gin = [Buf() for _ in range(3)]
              SH = sb("b_sh", [128, 27, 512], F32, cx); b_sh = [Buf() for _ in range(27)]
              WTb = sb("b_wt", [128, 512], BF16, cx); ADb = sb("b_ad", [128, 512], BF16, cx)
              SGb = sb("b_sg", [128, 512], BF16, cx); SG2b = sb("b_sg2", [128, 512], BF16, cx); b_lo = Buf()
              W2s = sb("b_w2", [128, 2, 1024], BF16, cx); A2s = sb("b_a2", [128, 2, 1024], BF16, cx)
              G2a = sb("b_g2a", [128, 1024], BF16, cx); G2b = sb("b_g2b", [128, 1024], BF16, cx); b_lw = Buf()
              ROLES = {}
              SCQ = [sb(f"b_scq{i}", [128, 4, 4, 128], BF16, cx) for i in range(2)]; b_scq = [Buf(), Buf()]
              if pair:
                  HSL = [sb(f"b_hsl{i}", [128, 27, 64], F32, cx) for i in range(2)]; b_hsl = Buf()
                  HALO = sb("b_halo", [128, 27, 64], F32, cx); b_halo = Buf()
              VBs = [sb(f"b_vbs{i}", [128, 512], BF16, cx) for i in range(2)]; b_vbs = [Buf(), Buf()]
              tmi = {"q": 0, "v": 0, "g": 0}

              def tmp(role, par):
                  key = (role, par % 2)
                  if key not in ROLES:
                      ROLES[key] = (sb(f"b_r_{role}{par % 2}", [128, 512], F32, cx), Buf())
                  return ROLES[key]

              P.dma("pool", W2s[0:64, :, :], w2[l].rearrange("d j c -> j d c"), writes=[b_lw])
              P.dma("pool", A2s[64:128, :, :], a2[l].rearrange("d j c -> j d c"), writes=[b_lw])
              P.dma("pool", G2a[:], g2[l, 0:128, :], writes=[b_lw])
              P.dma("pool", G2b[0:32, :], g2[l, 128:160, :], writes=[b_lw])

              for w in range(NW):
                  is_p = (w == NW - 1)
                  t0 = w * 512
                  W = 512
                  for j in range(27):
                      npar = 128 if j < 26 else 32
                      gi = j % 3
                      g_, bg = GIN[gi], b_gin[gi]
                      rows = slice(j * 128, j * 128 + npar)
                      if not is_p:
                          lo = t0 - 64
                          hi = t0 + W + 64
                          rd = [PTB[j][w]]
                          if lo < 0:
                              P.op("pool", lambda g_=g_, npar=npar: G_.memset(g_[0:npar, 0:64], 0.0), writes=[bg])
                              lo = 0
                          else:
                              rd.append(PTB[j][w - 1])
                          if hi > TS:
                              if pair:
                                  if j == 0:
                                      for sl_ in range(2):
                                          P.dma("sp", HSL[sl_][:], HGATH[sl_ * 3456:(sl_ + 1) * 3456, :].rearrange("(j p) t -> p j t", p=128),
                                                reads=[b_hgath], writes=[b_hsl])
                                      P.op("dve", lambda: V_.tensor_scalar(HALO[:], HSL[0][:], MSEL[:, 0:1], None, ALU.mult),
                                           reads=[b_hsl, b_msel], writes=[b_halo])
                                      P.op("dve", lambda: V_.scalar_tensor_tensor(out=HALO[:], in0=HSL[1][:], scalar=MSEL[:, 1:2], in1=HALO[:],
                                                                                  op0=ALU.mult, op1=ALU.add),
                                           reads=[b_hsl, b_msel, b_halo], writes=[b_halo])
                                  P.op("pool", lambda g_=g_, npar=npar, j=j: G_.tensor_copy(g_[0:npar, 576:640], HALO[0:npar, j, ::-1]),
                                       reads=[b_halo], writes=[bg])
                              else:
                                  P.op("pool", lambda g_=g_, npar=npar: G_.memset(g_[0:npar, 576:640], 0.0), writes=[bg])
                              hi = TS
                          else:
                              rd.append(PTB[j][w + 1])
                          P.dma("sp", g_[0:npar, lo - (t0 - 64):hi - (t0 - 64)], PT[rows, lo:hi], reads=rd, writes=[bg])
                          ctr = g_[0:npar, 64:64 + W]
                          acc = SH[0:npar, j, :]
                          P.op("act", lambda acc=acc, ctr=ctr, j=j, npar=npar: A_.activation(out=acc, in_=ctr, func=AF.Copy, scale=MU[0:npar, l, j, 4:5]),
                               reads=[bg, b_mu], writes=[b_sh[j]])
                          a3 = acc.rearrange("p (r c) -> p r c", c=64)
                          c3 = ctr.rearrange("p (r c) -> p r c", c=64)
                          P.op("dve", lambda a3=a3, c3=c3, j=j, npar=npar: V_.scalar_tensor_tensor(
                              out=a3[:, :, 1:64], in0=c3[:, :, 0:63], scalar=MU[0:npar, l, j, 0:1], in1=a3[:, :, 1:64], op0=ALU.mult, op1=ALU.add),
                              reads=[bg, b_mu, b_sh[j]], writes=[b_sh[j]])
                          P.op("dve", lambda a3=a3, c3=c3, j=j, npar=npar: V_.scalar_tensor_tensor(
                              out=a3[:, :, 0:63], in0=c3[:, :, 1:64], scalar=MU[0:npar, l, j, 1:2], in1=a3[:, :, 0:63], op0=ALU.mult, op1=ALU.add),
                              reads=[bg, b_mu, b_sh[j]], writes=[b_sh[j]])
                          P.op("dve", lambda acc=acc, g_=g_, j=j, npar=npar: V_.scalar_tensor_tensor(
                              out=acc, in0=g_[0:npar, 0:W], scalar=MU[0:npar, l, j, 2:3], in1=acc, op0=ALU.mult, op1=ALU.add),
                              reads=[bg, b_mu, b_sh[j]], writes=[b_sh[j]])
                          P.op("dve", lambda acc=acc, g_=g_, j=j, npar=npar: V_.scalar_tensor_tensor(
                              out=acc, in0=g_[0:npar, 128:128 + W], scalar=MU[0:npar, l, j, 3:4], in1=acc, op0=ALU.mult, op1=ALU.add),
                              reads=[bg, b_mu, b_sh[j]], writes=[b_sh[j]])
                      else:
                          P.dma("sp", g_[0:npar, 64:64 + W], PT[rows, t0:t0 + W], reads=[PTB[j][w]], writes=[bg])
                          ctr = g_[0:npar, 64:64 + W]
                          acc = SH[0:npar, j, :]
                          P.op("act", lambda acc=acc, ctr=ctr, j=j, npar=npar: A_.activation(out=acc, in_=ctr, func=AF.Copy, scale=MU[0:npar, l, j, 5:6]),
                               reads=[bg, b_mu], writes=[b_sh[j]])
                          a3 = acc.rearrange("p (r c) -> p r c", c=SEQP)
                          c3 = ctr.rearrange("p (r c) -> p r c", c=SEQP)
                          P.op("dve", lambda a3=a3, c3=c3, j=j, npar=npar: V_.scalar_tensor_tensor(
                              out=a3[:, :, 1:SEQP], in0=c3[:, :, 0:SEQP - 1], scalar=MU[0:npar, l, j, 0:1], in1=a3[:, :, 1:SEQP], op0=ALU.mult, op1=ALU.add),
                              reads=[bg, b_mu, b_sh[j]], writes=[b_sh[j]])
                          P.op("dve", lambda a3=a3, c3=c3, j=j, npar=npar: V_.scalar_tensor_tensor(
                              out=a3[:, :, 0:SEQP - 1], in0=c3[:, :, 1:SEQP], scalar=MU[0:npar, l, j, 1:2], in1=a3[:, :, 0:SEQP - 1], op0=ALU.mult, op1=ALU.add),
                              reads=[bg, b_mu, b_sh[j]], writes=[b_sh[j]])
                  P.op("act", lambda: A_.activation(out=WTb[0:64, :], in_=SH[0:64, 24, :], func=AF.Tanh), reads=[b_sh[24]], writes=[b_lo])
                  P.op("dve", lambda: V_.tensor_copy(ADb[64:128, :], SH[64:128, 24, :]), reads=[b_sh[24]], writes=[b_lo])
                  P.op("act", lambda: A_.activation(out=SGb[:], in_=SH[:, 25, :], func=AF.Sigmoid), reads=[b_sh[25]], writes=[b_lo])
                  P.op("act", lambda: A_.activation(out=SG2b[0:32, :], in_=SH[0:32, 26, :], func=AF.Sigmoid), reads=[b_sh[26]], writes=[b_lo])
                  for j in range(8):
                      jc = slice(j * 128, (j + 1) * 128)
                      Rj, Kj, Vj = SH[:, j, :], SH[:, 8 + j, :], SH[:, 16 + j, :]
                      bR, bK, bV = b_sh[j], b_sh[8 + j], b_sh[16 + j]
                      pbg = j % 2
                      P.op("pe", lambda: nc.tensor.matmul(PB[pbg][:], G2a[:, jc], SGb[:], start=True, stop=False),
                           reads=[b_lw, b_lo], writes=[PBB[pbg]], inc=False)
                      P.op("pe", lambda: nc.tensor.matmul(PB[pbg][:], G2b[0:32, jc], SG2b[0:32, :], start=False, stop=True),
                           reads=[b_lw, b_lo], writes=[PBB[pbg]])
                      tg, btg = tmp("tg", j)
                      P.op("act", lambda tg=tg: A_.copy(tg[:], PB[pbg][:]), reads=[PBB[pbg]], writes=[btg])
                      P.dma("sp", GTd[jc, t0:t0 + W], tg[:], reads=[btg], writes=[GTB[w][j]])
                      vi = tmi["v"] % 2
                      tmi["v"] += 1
                      P.op("pool", lambda vi=vi, Vj=Vj: G_.tensor_copy(VBs[vi][:], Vj), reads=[bV], writes=[b_vbs[vi]])
                      P.dma("sp", VBd[w * 4:(w + 1) * 4, j].rearrange("c p t -> p c t"),
                            VBs[vi][:].rearrange("p (c t) -> p c t", t=128), reads=[b_vbs[vi]], writes=[VBB[w][j]])
                      t1, bt1 = tmp("t1", j)
                      P.op("act", lambda t1=t1, Kj=Kj, j=j: A_.activation(out=t1[:], in_=Kj, func=AF.Copy, scale=KKp[:, l, j:j + 1]),
                           reads=[bK, b_par], writes=[bt1])
                      t2, bt2 = tmp("t2", j)
                      P.op("act", lambda t1=t1, t2=t2: A_.activation(out=t2[:], in_=t1[:], func=AF.Square), reads=[bt1], writes=[bt2])
                      P.op("pe", lambda t2=t2: nc.tensor.matmul(PB[2][:], blk_f[:], t2[:], start=True, stop=True),
                           reads=[bt2, b_blk], writes=[PBB[2]])
                      P.op("dve", lambda t2=t2: V_.tensor_scalar_max(t2[:], PB[2][:], 1e-24), reads=[PBB[2]], writes=[bt2])
                      P.op("act", lambda t2=t2: A_.activation(out=t2[:], in_=t2[:], func=AF.Sqrt), reads=[bt2], writes=[bt2])
                      P.op("dve", lambda t2=t2: V_.reciprocal(t2[:], t2[:]), reads=[bt2], writes=[bt2])
                      kkn, bkkn = tmp("kkn", j)
                      P.op("pool", lambda kkn=kkn, t1=t1, t2=t2: G_.tensor_tensor(kkn[:], t1[:], t2[:], ALU.mult),
                           reads=[bt1, bt2], writes=[bkkn])
                      def dbody(d):
                          pz = 3 + d
                          pa = 5 + d
                          P.op("pe", lambda d=d, pz=pz: nc.tensor.matmul(PB[pz][:], W2s[0:64, d, jc], WTb[0:64, :], start=True, stop=True),
                               reads=[b_lw, b_lo], writes=[PBB[pz]])
                          yield
                          lws, blws = tmp("lws", d)
                          P.op("act", lambda lws=lws, pz=pz, d=d, j=j: A_.activation(out=lws[:], in_=PB[pz][:], func=AF.Sigmoid, bias=W0[:, l, d, j:j + 1]),
                               reads=[PBB[pz], b_par], writes=[blws])
                          yield
                          P.op("pe", lambda d=d, pa=pa: nc.tensor.matmul(PB[pa][:], A2s[64:128, d, jc], ADb[64:128, :], start=True, stop=True),
                               reads=[b_lw, b_lo], writes=[PBB[pa]])
                          yield
                          alr, balr = tmp("alr", d)
                          P.op("act", lambda alr=alr, pa=pa, d=d, j=j: A_.activation(out=alr[:], in_=PB[pa][:], func=AF.Sigmoid, bias=A0[:, l, d, j:j + 1]),
                               reads=[PBB[pa], b_par], writes=[balr])
                          yield
                          kd, bkd = tmp("kd", d)
                          P.op("dve", lambda kd=kd, alr=alr, j=j: V_.tensor_scalar(kd[:], alr[:], -1.0, KAp[:, l, j:j + 1], ALU.add, ALU.mult),
                               reads=[balr, b_par], writes=[bkd])
                          yield
                          P.op("dve", lambda kd=kd, Kj=Kj: V_.scalar_tensor_tensor(out=kd[:], in0=kd[:], scalar=1.0, in1=Kj, op0=ALU.add, op1=ALU.mult),
                               reads=[bkd, bK], writes=[bkd])
                          yield
                          bv, bbv = tmp("bv", d)
                          P.op("pool", lambda bv=bv, kkn=kkn, alr=alr: G_.tensor_tensor(bv[:], kkn[:], alr[:], ALU.mult),
                               reads=[bkkn, balr], writes=[bbv])
                          yield
                          t4, bt4 = tmp("t4", d)
                          P.op("dve", lambda t4=t4, Rj=Rj, kd=kd, j=j: V_.scalar_tensor_tensor(out=t4[:], in0=Rj, scalar=RKp[:, l, j:j + 1], in1=kd[:], op0=ALU.mult, op1=ALU.mult),
                               reads=[bR, bkd, b_par], writes=[bt4])
                          yield
                          P.op("pe", lambda t4=t4, d=d: nc.tensor.matmul(PB[7][:], blk_f[:], t4[:], start=(d == 0), stop=(d == 1)),
                               reads=[bt4, b_blk], writes=[PBB[7]])
                          yield
                          cs, bcs = tmp("cs", d)
                          P.op("dve", lambda cs=cs, lws=lws: V_.tensor_tensor_scan(cs[:], RST[:], lws[:], 0.0, ALU.mult, ALU.add),
                               reads=[blws, b_rst], writes=[bcs])
                          yield
                          ge, bge = tmp("ge", d)
                          gi_, bgi = tmp("gi", d)
                          cs3 = cs[:].rearrange("p (c t) -> p c t", t=128)
                          if d == 0:
                              P.op("pool", lambda ge=ge, cs=cs, lws=lws: G_.tensor_tensor(ge[:], cs[:], lws[:], ALU.subtract),
                                   reads=[bcs, blws], writes=[bge])
                              yield
                              GI, bGI = cs, bcs
                          else:
                              P.op("dve", lambda ge=ge, cs3=cs3: V_.tensor_tensor(
                                  ge[:].rearrange("p (c t) -> p c t", t=128), cs3[:, :, 127:128].to_broadcast([128, 4, 128]), cs3, ALU.subtract),
                                  reads=[bcs], writes=[bge])
                              yield
                              P.op("pool", lambda gi_=gi_, ge=ge, lws=lws: G_.tensor_tensor(gi_[:], ge[:], lws[:], ALU.add),
                                   reads=[bge, blws], writes=[bgi])
                              yield
                              GI, bGI = gi_, bgi
                          P.op("act", lambda cs3=cs3, d=d, j=j, w=w: A_.activation(out=GAM[:, d, j, w * 4:(w + 1) * 4], in_=cs3[:, :, 127], func=AF.Exp, scale=-c_),
                               reads=[bcs], writes=[b_gam])
                          yield
                          e1, be1 = tmp("e1", d)
                          e2, be2 = tmp("e2", d)
                          e3, be3 = tmp("e3", d)
                          P.op("act", lambda e1=e1, GI=GI: A_.activation(out=e1[:], in_=GI[:], func=AF.Exp, scale=-c_), reads=[bGI], writes=[be1])
                          yield
                          P.op("act", lambda e2=e2, ge=ge: A_.activation(out=e2[:], in_=ge[:], func=AF.Exp, scale=-c_), reads=[bge], writes=[be2])
                          yield
                          P.op("act", lambda e3=e3, GI=GI: A_.activation(out=e3[:], in_=GI[:], func=AF.Exp, scale=c_), reads=[bGI], writes=[be3])
                          yield
                          qi = d
                          q_, bq = SCQ[qi], b_scq[qi]
                          c3_ = lambda ap: ap.rearrange("p (c t) -> p c t", t=128)
                          P.op("dve", lambda q_=q_, kkn=kkn, e2=e2: V_.scalar_tensor_tensor(out=q_[:, :, 0, :], in0=c3_(kkn[:]), scalar=-1.0, in1=c3_(e2[:]), op0=ALU.mult, op1=ALU.mult),
                               reads=[bkkn, be2], writes=[bq])
                          yield
                          P.op("pool", lambda q_=q_, Rj=Rj, e1=e1: G_.tensor_tensor(q_[:, :, 1, :], c3_(Rj), c3_(e1[:]), ALU.mult),
                               reads=[bR, be1, bq], writes=[bq])
                          yield
                          P.op("dve", lambda q_=q_, bv=bv, e3=e3: V_.tensor_tensor(q_[:, :, 2, :], c3_(bv[:]), c3_(e3[:]), ALU.mult),
                               reads=[bbv, be3, bq], writes=[bq])
                          yield
                          P.op("pool", lambda q_=q_, kd=kd, e3=e3: G_.tensor_tensor(q_[:, :, 3, :], c3_(kd[:]), c3_(e3[:]), ALU.mult),
                               reads=[bkd, be3, bq], writes=[bq])
                          yield
                          P.dma("sp", SCd[w * 4:(w + 1) * 4, d, j].rearrange("c p q t -> p c (q t)"),
                                q_[:].rearrange("p c q t -> p c (q t)"), reads=[bq], writes=[SCB[w][d][j]])
                          yield

                      gens_ = [dbody(0), dbody(1)]
                      while gens_:
                          nx_ = []
                          for g__ in gens_:
                              try:
                                  next(g__)
                                  nx_.append(g__)
                              except StopIteration:
                                  pass
                          gens_ = nx_
                      tb, btb = tmp("tb", j)
                      P.op("dve", lambda tb=tb, Vj=Vj: V_.tensor_tensor(tb[:], PB[7][:], Vj, ALU.mult), reads=[PBB[7], bV], writes=[btb])
                      P.dma("sp", BTd[jc, t0:t0 + W], tb[:], reads=[btb], writes=[BTB[w][j]])
              P.barrier()
          chk("B1")

          with contextlib.ExitStack() as cx:
              NSC = 4
              SCT = [sb(f"s_sct{i}", [128, 8, 4, 128], BF16, cx) for i in range(NSC)]; b_sct = [Buf() for _ in range(NSC)]
              VBT = [sb(f"s_vbt{i}", [128, 8, 128], BF16, cx) for i in range(NSC)]; b_vbt = [Buf() for _ in range(NSC)]
              HS = [sb(f"s_hs{d}", [128, 8, 64], F32, cx) for d in range(2)]
              HBt = [sb(f"s_hb{d}", [128, 8, 64], BF16, cx) for d in range(2)]
              b_hs = [[Buf() for _ in range(NH)] for _ in range(2)]
              YS = [sb(f"s_ys{i}", [128, NH, 64], F32, cx) for i in range(2)]; b_ys = [Buf(), Buf()]
              YC = sb("s_yc", [128, NH, 64], F32, cx); b_yc = Buf()
              YQ = sb("s_yq", [128, NH, 64], F32, cx); b_yq = Buf()
              YO = [sb(f"s_yo{i}", [128, 1024], F32, cx) for i in range(2)]; b_yo = [Buf(), Buf()]
              GS1 = sb("s_gs1", [128, NH], F32, cx); GS2 = sb("s_gs2", [128, NH], F32, cx); b_gs = Buf()
              GNW = sb("s_gnw", [128, 1024], F32, cx); GNB = sb("s_gnb", [128, 1024], F32, cx); b_gn = Buf()
              SV = sb("s_sv", [64, NH, 64], F32, cx); b_sv = Buf()
              P.dma("sp", GNW[:], gnw[l].partition_broadcast(128), writes=[b_gn])
              P.dma("sp", GNB[:], gnb[l].partition_broadcast(128), writes=[b_gn])
              NS = 4
              TOK = [[sb(f"s_tok{s}_{hp}", [128, 3, 128], BF16, cx) for hp in range(2)] for s in range(NS)]
              b_tok = [[Buf(), Buf()] for _ in range(NS)]
              VTK = [sb(f"s_vtk{s}", [128, 64], BF16, cx) for s in range(NS)]; b_vtk = [Buf() for _ in range(NS)]
              SCM = [sb(f"s_scm{s}", [128, 512], BF16, cx) for s in range(NS)]; b_scm = [Buf() for _ in range(NS)]
              NMt = [sb(f"s_nm{s}", [128, 128], BF16, cx) for s in range(NS)]; b_nm = [Buf() for _ in range(NS)]
              XS = [sb(f"s_xs{s}", [128, 64], BF16, cx) for s in range(NS)]; b_xs = [Buf() for _ in range(NS)]
              QS = [sb(f"s_qs{s}", [128, 256], F32, cx) for s in range(NS)]; b_qs = [Buf() for _ in range(NS)]
              PW = [sb(f"s_pw{s}", [128, 128], F32, cx) for s in range(NS)]; b_pw = [Buf() for _ in range(NS)]
              MTb = [sb(f"s_mtb{s}", [128, 128], BF16, cx) for s in range(NS)]; b_mtb = [Buf() for _ in range(NS)]
              AH = [sb(f"s_ah{s}", [128, 128], BF16, cx) for s in range(NS)]; b_ah = [Buf() for _ in range(NS)]
              US = [sb(f"s_us{s}", [128, 64], BF16, cx) for s in range(NS)]; b_us = [Buf() for _ in range(NS)]
              TH = [sb(f"s_th{s}", [128, 64], F32, cx) for s in range(NS)]; b_th = [Buf() for _ in range(NS)]
              for s in range(NS):
                  for hp in range(2):
                      P.op("dve", lambda s=s, hp=hp: V_.memset(TOK[s][hp][:], 0.0), writes=[b_tok[s][hp]])
              PSTv = [PB[2 * s][:].bitcast(BF16) for s in range(NS)]
              sci = {"i": 0, "y": 0}

              def group(s, h, d, chg, sct, bsct, vbt, bvbt, ys, bys):
                  j, hp = h // 2, h % 2
                  P0 = hp * 64
                  hs = slice(P0, P0 + 64)
                  PSA, bPSA = PB[2 * s], PBB[2 * s]
                  PSC = PB[2 * s + 1]
                  PSD, bPSD = PSC, PBB[2 * s + 1]
                  PST, bPST = PSTv[s], bPSA
                  bN = bPSD
                  bX = bU = bY = bH = bPSA
                  R_N = PSC[:, 384:512]
                  R_X = PSA[:, 0:64]
                  R_U = PSA[:, 64:128]
                  R_Y = PSA[:, 128:192]
                  R_H = PSA[:, 192:256]
                  aT = sct[hs, j, 0, :]
                  rT = sct[hs, j, 1, :]
                  bT = sct[hs, j, 2, :]
                  kT = sct[hs, j, 3, :]
                  arT = sct[hs, j, 0:2, :].rearrange("p q t -> p (q t)")
                  vT = vbt[hs, j, :]
                  idh = id_b[hs, P0:P0 + 64]
                  tok, btok = TOK[s][hp], b_tok[s][hp]
                  for qi, src in enumerate((aT, bT, kT)):
                      P.op("pe", lambda qi=qi, src=src: nc.tensor.transpose(PST[:, qi * 64:(qi + 1) * 64], src, idh),
                           reads=[bsct, b_idb], writes=[bPST], inc=False)
                  P.op("pe", lambda: nc.tensor.transpose(PST[:, 192:256], vT, idh), reads=[bvbt, b_idb], writes=[bPST])
                  yield
                  P.op("act", lambda: A_.copy(tok[:, :, P0:P0 + 64], PST[:, 0:192].rearrange("p (q c) -> p q c", c=64)),
                       reads=[bPST], writes=[btok])
                  P.op("dve", lambda: V_.tensor_copy(VTK[s][:], PST[:, 192:256]), reads=[bPST], writes=[b_vtk[s]])
                  P.op("pe", lambda: nc.tensor.matmul(PSA[:, 0:256], bT, arT, start=True, stop=True), reads=[bsct], writes=[bPSA], inc=False)
                  P.op("pe", lambda: nc.tensor.matmul(PSA[:, 256:512], kT, arT, start=True, stop=True), reads=[bsct], writes=[bPSA])
                  P.op("pe", lambda: nc.tensor.matmul(R_N, aT, bT, start=True, stop=True), reads=[bsct], writes=[bN])
                  yield
                  P.op("dve", lambda: V_.tensor_tensor(QS[s][:, 0:128], PSA[:, 0:128], MASK[d][:, 0:128], ALU.mult),
                       reads=[bPSA, b_mask[d]], writes=[b_qs[s]])
                  P.op("dve", lambda: V_.tensor_tensor(SCM[s][:, 128:512], PSA[:, 128:512], MASK[d][:, 128:512], ALU.mult),
                       reads=[bPSA, b_mask[d]], writes=[b_scm[s]])
                  P.op("dve", lambda: V_.tensor_tensor(PW[s][:], R_N, NMASK[d][:], ALU.mult), reads=[bN, b_nmask[d]], writes=[b_pw[s]])
                  P.op("pe", lambda: nc.tensor.matmul(R_X, SCM[s][:, 256:384], VTK[s][:], start=True, stop=True),
                       reads=[b_scm[s], b_vtk[s]], writes=[bX])
                  P.op("pe", lambda: nc.tensor.matmul(PSD[:, 0:128], PW[s][:], QS[s][:, 0:128], start=True, stop=True),
                       reads=[b_pw[s], b_qs[s]], writes=[bPSD], inc=False)
                  P.op("pe", lambda: nc.tensor.matmul(PSD[:, 256:384], QS[s][:, 0:128], PW[s][:], start=True, stop=True),
                       reads=[b_pw[s], b_qs[s]], writes=[bPSD])
                  yield
                  P.op("act", lambda: A_.copy(XS[s][:], R_X), reads=[bX], writes=[b_xs[s]])
                  P.op("pool", lambda: G_.tensor_tensor(QS[s][:, 128:256], QS[s][:, 0:128], id_f[:], ALU.add),
                       reads=[b_qs[s], b_idf], writes=[b_qs[s]])
                  P.op("act", lambda: A_.copy(QS[s][:, 0:128], PSD[:, 0:128]), reads=[bPSD, b_qs[s]], writes=[b_qs[s]])
                  P.op("act", lambda: A_.copy(PW[s][:], PSD[:, 256:384]), reads=[bPSD], writes=[b_pw[s]])
                  for lev in range(1, 6):
                      P.op("pe", lambda: nc.tensor.matmul(PSD[:, 0:256], PW[s][:], QS[s][:], start=True, stop=True),
                           reads=[b_pw[s], b_qs[s]], writes=[bPSD], inc=False)
                      P.op("pe", lambda: nc.tensor.matmul(PSD[:, 256:384], QS[s][:, 0:128], PW[s][:], start=True, stop=True),
                           reads=[b_pw[s], b_qs[s]], writes=[bPSD])
                      yield
                      P.op("act", lambda: A_.copy(QS[s][:, 0:128], PSD[:, 0:128]), reads=[bPSD], writes=[b_qs[s]])
                      P.op("dve", lambda: V_.tensor_tensor(QS[s][:, 128:256], PSD[:, 128:256], QS[s][:, 128:256], ALU.add),
                           reads=[bPSD, b_qs[s]], writes=[b_qs[s]])
                      P.op("act", lambda: A_.copy(PW[s][:], PSD[:, 256:384]), reads=[bPSD], writes=[b_pw[s]])
                  P.op("pe", lambda: nc.tensor.matmul(PSD[:, 128:256], PW[s][:], QS[s][:, 128:256], start=True, stop=True),
                       reads=[b_pw[s], b_qs[s]], writes=[bPSD])
                  yield
                  P.op("dve", lambda: V_.tensor_tensor(MTb[s][:], PSD[:, 128:256], QS[s][:, 128:256], ALU.add),
                       reads=[bPSD, b_qs[s]], writes=[b_mtb[s]])
                  MT = MTb[s][:]
                  P.op("pe", lambda: nc.tensor.matmul(R_U, MT, XS[s][:], start=True, stop=False),
                       reads=[b_mtb[s], b_xs[s]], writes=[bU], inc=False)
                  P.op("pe", lambda: nc.tensor.matmul(R_N, tok[:, 0, :], MT, start=True, stop=True),
                       reads=[btok, b_mtb[s]], writes=[bN])
                  yield
                  P.op("act", lambda: A_.copy(AH[s][hs, :], PSC[hs, 384:512]), reads=[bN], writes=[b_ah[s]])
                  hb = HBt[d][hs, j, :]
                  P.op("pe", lambda: nc.tensor.matmul(R_U, AH[s][hs, :], hb, start=False, stop=True),
                       reads=[b_ah[s], b_hs[d][h]], writes=[bU])
                  yield
                  P.op("act", lambda: A_.copy(US[s][:], R_U), reads=[bU], writes=[b_us[s]])
                  P.op("pe", lambda: nc.tensor.matmul(R_Y, rT, hb, start=True, stop=False),
                       reads=[bsct, b_hs[d][h]], writes=[bY], inc=False)
                  P.op("pe", lambda: nc.tensor.matmul(R_Y, SCM[s][:, 128:256], US[s][:], start=False, stop=False),
                       reads=[b_scm[s], b_us[s]], writes=[bY], inc=False)
                  P.op("pe", lambda: nc.tensor.matmul(R_Y, SCM[s][:, 384:512], VTK[s][:], start=False, stop=True),
                       reads=[b_scm[s], b_vtk[s]], writes=[bY], rec_reads=[bsct, b_hs[d][h], b_scm[s], b_us[s], b_vtk[s]])
                  P.op("pe", lambda: nc.tensor.matmul(R_H, tok[:, 1, :], US[s][:], start=True, stop=False),
                       reads=[btok, b_us[s]], writes=[bH], inc=False)
                  P.op("pe", lambda: nc.tensor.matmul(R_H, tok[:, 2, :], VTK[s][:], start=False, stop=True),
                       reads=[btok, b_vtk[s]], writes=[bH], rec_reads=[btok, b_us[s], b_vtk[s]])
                  yield
                  P.op("dve", lambda: V_.tensor_copy(ys[:, h, :], R_Y), reads=[bY], writes=[bys])
                  P.op("dve", lambda: V_.tensor_tensor(TH[s][hs, :], PSA[hs, 192:256], HS[d][hs, j, :], ALU.add),
                       reads=[bH, b_hs[d][h]], writes=[b_th[s]])
                  gam = GAM[hs, d, j, chg:chg + 1]
                  P.op("dve", lambda: V_.tensor_scalar(HS[d][hs, j, :], TH[s][hs, :], gam, None, ALU.mult),
                       reads=[b_th[s], b_gam], writes=[b_hs[d][h]])
                  P.op("act", lambda: A_.activation(out=HBt[d][hs, j, :], in_=TH[s][hs, :], func=AF.Copy, scale=gam),
                       reads=[b_th[s], b_gam, b_hs[d][h]], writes=[b_hs[d][h]])

              def run_pair(gens):
                  live = list(gens)
                  while live:
                      nxt = []
                      for g in live:
                          if stop and stop.startswith("B2:"):
                              sci["steps"] = sci.get("steps", 0) + 1
                              if sci["steps"] > int(stop.split(":")[1]):
                                  raise _Stop()
                          try:
                              next(g)
                              nxt.append(g)
                          except StopIteration:
                              pass
                      live = nxt

              def gn_and_store(d, chg, ys, bys):
                  yi = sci["y"] % 2
                  sci["y"] += 1
                  yo, byo = YO[yi], b_yo[yi]
                  P.op("dve", lambda: V_.reduce_sum(GS1[:], ys[:], axis=AX.X), reads=[bys], writes=[b_gs])
                  P.op("dve", lambda: V_.tensor_scalar(GS1[:], GS1[:], 1.0 / 64, None, ALU.mult), reads=[b_gs], writes=[b_gs])
                  P.op("dve", lambda: V_.tensor_tensor(YC[:], ys[:], GS1[:].unsqueeze(2).to_broadcast([128, NH, 64]), ALU.subtract),
                       reads=[bys, b_gs], writes=[b_yc])
                  P.op("act", lambda: A_.activation(out=YQ[:], in_=YC[:], func=AF.Square), reads=[b_yc], writes=[b_yq])
                  P.op("dve", lambda: V_.reduce_sum(GS2[:], YQ[:], axis=AX.X), reads=[b_yq], writes=[b_gs])
                  P.op("dve", lambda: V_.tensor_scalar(GS2[:], GS2[:], 1.0 / 64, GN_EPS, ALU.mult, ALU.add), reads=[b_gs], writes=[b_gs])
                  P.op("act", lambda: A_.activation(out=GS2[:], in_=GS2[:], func=AF.Sqrt), reads=[b_gs], writes=[b_gs])
                  P.op("dve", lambda: V_.reciprocal(GS2[:], GS2[:]), reads=[b_gs], writes=[b_gs])
                  P.op("dve", lambda: V_.tensor_tensor(YC[:], YC[:], GS2[:].unsqueeze(2).to_broadcast([128, NH, 64]), ALU.mult),
                       reads=[b_yc, b_gs], writes=[b_yc])
                  ycf = YC[:].rearrange("p h v -> p (h v)")
                  P.op("pool", lambda: G_.tensor_tensor(ycf, ycf, GNW[:], ALU.mult), reads=[b_yc, b_gn], writes=[b_yc])
                  P.op("pool", lambda: G_.tensor_tensor(yo[:], ycf, GNB[:], ALU.add), reads=[b_yc, b_gn], writes=[byo])
                  P.dma("sp", YGd[d, chg * 128:(chg + 1) * 128, :], yo[:], reads=[byo], writes=[YGB[d][chg]])

              def init_zero(d):
                  P.op("dve", lambda: V_.memset(HS[d][:], 0.0), reads=[b for b in b_hs[d]], writes=[b for b in b_hs[d]])
                  P.op("dve", lambda: V_.memset(HBt[d][:], 0.0), reads=[b for b in b_hs[d]], writes=[b for b in b_hs[d]])

              def init_dram(d):
                  P.dma("sp", SV[:], state0[l, d].rearrange("h v k -> v h k"), writes=[b_sv])
                  for j in range(8):
                      pb = 6 + j % 2
                      P.op("pe", lambda: nc.tensor.transpose(
                          PB[pb][:, 0:64], SV[:, 2 * j:2 * j + 2, :].rearrange("v h k -> v (h k)"), id_f[0:64, 0:64]),
                          reads=[b_sv, b_idf], writes=[PBB[pb]])
                      P.op("dve", lambda: V_.tensor_copy(HS[d][:, j, :], PB[pb][:, 0:64]),
                           reads=[PBB[pb]], writes=[b_hs[d][2 * j], b_hs[d][2 * j + 1]])
                      P.op("act", lambda: A_.copy(HBt[d][:, j, :], PB[pb][:, 0:64]),
                           reads=[PBB[pb], b_hs[d][2 * j], b_hs[d][2 * j + 1]], writes=[b_hs[d][2 * j], b_hs[d][2 * j + 1]])

              def send_state(d):
                  P.dma("sp", SSEND, HS[d][:].rearrange("p j v -> p (j v)"), reads=[b for b in b_hs[d]], writes=[b_ssend])
                  P.coll(SSEND, SGATH, GROUPS, reads=[b_ssend], writes=[b_sgath])

              def init_recv(d):
                  for sl_ in range(2):
                      P.dma("sp", RCV[sl_][:], SGATH[sl_ * 128:(sl_ + 1) * 128, :], reads=[b_sgath], writes=[b_rcv])
                  hsf = HS[d][:].rearrange("p j v -> p (j v)")
                  P.op("dve", lambda: V_.tensor_scalar(hsf, RCV[0][:], MSEL[:, 0:1], None, ALU.mult),
                       reads=[b_rcv, b_msel] + b_hs[d], writes=b_hs[d])
                  P.op("dve", lambda: V_.scalar_tensor_tensor(out=hsf, in0=RCV[1][:], scalar=MSEL[:, 1:2], in1=hsf, op0=ALU.mult, op1=ALU.add),
                       reads=[b_rcv, b_msel] + b_hs[d], writes=b_hs[d])
                  P.op("act", lambda: A_.copy(HBt[d][:].rearrange("p j v -> p (j v)"), hsf), reads=b_hs[d], writes=b_hs[d])

              def run_chunks(ch0, ncs, dirs):
                  for i in range(ncs):
                      for d in dirs:
                          ch = i if d == 0 else ncs - 1 - i
                          chg = ch0 + ch
                          w = chg // 4
                          si = sci["i"] % NSC
                          sci["i"] += 1
                          sct, bsct, vbt, bvbt = SCT[si], b_sct[si], VBT[si], b_vbt[si]
                          P.dma("sp", sct[:], SCd[chg, d].rearrange("j p q t -> p j q t"),
                                reads=[SCB[w][d][j] for j in range(8)], writes=[bsct])
                          P.dma("sp", vbt[:], VBd[chg].rearrange("j p t -> p j t"),
                                reads=[VBB[w][j] for j in range(8)], writes=[bvbt])
                          ysi = (sci["i"]) % 2
                          ys, bys = YS[ysi], b_ys[ysi]
                          for m in range(NH // NS):
                              run_pair([group(q, NS * m + q, d, chg, sct, bsct, vbt, bvbt, ys, bys) for q in range(NS)])
                          gn_and_store(d, chg, ys, bys)

              def final_states(pi):
                  for d in range(2):
                      for j in range(8):
                          pb = 6 + j % 2
                          P.op("pe", lambda: nc.tensor.transpose(PB[pb][0:64, 0:128], HS[d][:, j, :], id_f[:]),
                               reads=[b_hs[d][2 * j], b_hs[d][2 * j + 1], b_idf], writes=[PBB[pb]])
                          P.op("dve", lambda: V_.tensor_copy(
                              SV[:, 2 * j:2 * j + 2, :].rearrange("v h k -> v (h k)"), PB[pb][0:64, 0:128]),
                              reads=[PBB[pb]], writes=[b_sv])
                      P.dma("sp", nst[pi, l, d].rearrange("h v k -> v h k"), SV[:], reads=[b_sv], writes=[])

              prompts = [((TS + pi * SEQP) // 128, SEQP // 128, pi) for pi in range(NPR)]
              if pair:
                  RCV = [sb(f"s_rcv{i}", [128, 512], F32, cx) for i in range(2)]; b_rcv = Buf()
                  init_dram(0)
                  run_chunks(0, TS // 128, (0,))
                  send_state(0)
              else:
                  init_dram(0)
                  init_dram(1)
                  run_chunks(0, TS // 128, (0, 1))
              for (ch0, ncs, pi) in prompts:
                  init_zero(0)
                  init_zero(1)
                  run_chunks(ch0, ncs, (0, 1))
                  final_states(pi)
              if pair:
                  init_recv(1)
                  run_chunks(0, TS // 128, (1,))
              P.barrier()
          chk("B2")

          with contextlib.ExitStack() as cx:
              YA = sb("c_ya", [128, 4, 1024], F32, cx); b_ya = Buf()
              YBt = sb("c_yb", [128, 4, 1024], F32, cx); b_yb = Buf()
              BTt = sb("c_bt", [128, 8, 512], F32, cx); b_bt = Buf()
              GTt = sb("c_gt", [128, 8, 512], F32, cx); b_gt = Buf()
              OA = sb("c_oa", [128, 8, 512], BF16, cx); b_oa = Buf()
              TMq = [sb(f"c_tm{i}", [128, 512], F32, cx) for i in range(2)]; b_tmq = [Buf(), Buf()]
              for w in range(NW):
                  t0 = w * 512
                  P.dma("sp", YA[:], YGd[0, t0:t0 + 512, :].rearrange("(a p) c -> p a c", p=128),
                        reads=[YGB[0][w * 4 + a] for a in range(4)], writes=[b_ya])
                  P.dma("sp", YBt[:], YGd[1, t0:t0 + 512, :].rearrange("(a p) c -> p a c", p=128),
                        reads=[YGB[1][w * 4 + a] for a in range(4)], writes=[b_yb])
                  P.dma("sp", BTt[:], BTd[:, t0:t0 + 512].rearrange("(j p) t -> p j t", p=128),
                        reads=[BTB[w][j] for j in range(8)], writes=[b_bt])
                  P.dma("sp", GTt[:], GTd[:, t0:t0 + 512].rearrange("(j p) t -> p j t", p=128),
                        reads=[GTB[w][j] for j in range(8)], writes=[b_gt])
                  P.op("pool", lambda: G_.tensor_tensor(YA[:], YA[:], YBt[:], ALU.add), reads=[b_ya, b_yb], writes=[b_ya])
                  for j in range(8):
                      pb = j
                      for a in range(4):
                          P.op("pe", lambda a=a, j=j, pb=pb: nc.tensor.transpose(PB[pb][:, a * 128:(a + 1) * 128], YA[:, a, j * 128:(j + 1) * 128], id_f[:]),
                               reads=[b_ya, b_idf], writes=[PBB[pb]], inc=(a == 3))
                      q = j % 2
                      P.op("dve", lambda j=j, pb=pb, q=q: V_.tensor_tensor(TMq[q][:], PB[pb][:], BTt[:, j, :], ALU.add),
                           reads=[PBB[pb], b_bt], writes=[b_tmq[q]])
                      P.op("pool", lambda j=j, q=q: G_.tensor_tensor(OA[:, j, :], TMq[q][:], GTt[:, j, :], ALU.mult),
                           reads=[b_tmq[q], b_gt], writes=[b_oa])
                  P.dma("sp", OT[0:1024, t0:t0 + 512].rearrange("(j p) t -> p j t", p=128), OA[:], reads=[b_oa], writes=[OTB[0][w]])
              P.barrier()
          chk("B3")

          with contextlib.ExitStack() as cx:
              BIG = sb("d_big", [128, NF, 1024], BF16, cx); b_big = [Buf() for _ in range(NF)]
              H2T = sb("d_h2t", [128, NK, 1024], BF16, cx); b_h2 = Buf()
              WR = [sb(f"d_wr{i}", [128, 8192], BF16, cx) for i in range(2)]; b_wr = [Buf(), Buf()]
              XC = [sb(f"d_xc{i}", [128, 512], F32, cx) for i in range(3)]; b_xc = [Buf() for _ in range(3)]
              XN = [sb(f"d_xn{i}", [128, 512], F32, cx) for i in range(3)]; b_xn = [Buf() for _ in range(3)]
              SQc = [sb(f"d_sq{i}", [128, 512], F32, cx) for i in range(2)]; b_sqc = [Buf(), Buf()]
              RSc = sb("d_rs", [128, 1024], F32, cx); b_rsc = Buf()
              SGc = [sb(f"d_sg{i}", [128, 512], F32, cx) for i in range(2)]; b_sgc = [Buf(), Buf()]
              ci = {"w": 0, "x": 0, "n": 0, "q": 0, "s": 0, "pb": 0}
              wo_v = w_out[l].rearrange("(k p) n -> p k n", p=128)
              wg_v = w_g[l].rearrange("(k p) n -> p k n", p=128)
              wu_v = w_u[l].rearrange("(k p) n -> p k n", p=128)
              wd_v = w_d[l].rearrange("(k p) n -> p k n", p=128)
              for (t0, NT, vs) in STS:
                  nsub = NT // 512
                  for k in range(NK):
                      half = 0 if k < 8 else 1
                      P.dma("sp", BIG[:, k, 0:NT], OT[k * 128:(k + 1) * 128, t0:t0 + NT],
                            reads=[OTB[half][t0 // 512 + s_] for s_ in range(nsub)], writes=[b_big[k]])
                  for dq in range(4):
                      wi = ci["w"] % 2
                      ci["w"] += 1
                      wr = WR[wi][:].rearrange("p (k n) -> p k n", n=512)
                      P.dma("pool", wr, wo_v[:, :, dq * 512:(dq + 1) * 512], writes=[b_wr[wi]])
                      for db in range(4):
                          kd_ = dq * 4 + db
                          for sub in range(nsub):
                              t5 = t0 // 512 + sub
                              pb = 2 + ci["pb"] % 6
                              ci["pb"] += 1
                              for k in range(NK):
                                  P.op("pe", lambda k=k, pb=pb, db=db, sub=sub, wr=wr: nc.tensor.matmul(
                                      PB[pb][:], wr[:, k, db * 128:(db + 1) * 128], BIG[:, k, sub * 512:(sub + 1) * 512],
                                      start=(k == 0), stop=(k == NK - 1)),
                                      reads=[b_wr[wi], b_big[k]], writes=[PBB[pb]], inc=(k == NK - 1),
                                      rec_reads=[b_wr[wi]] + b_big[0:NK])
                              xi = ci["x"] % 3
                              ci["x"] += 1
                              P.dma("sp", XC[xi][:], XSRC[kd_ * 128:(kd_ + 1) * 128, t5 * 512:(t5 + 1) * 512],
                                    reads=[XB[kd_][t5]], writes=[b_xc[xi]])
                              ni = ci["n"] % 3
                              ci["n"] += 1
                              P.op("dve", lambda pb=pb, xi=xi, ni=ni, kd_=kd_: V_.scalar_tensor_tensor(
                                  out=XN[ni][:], in0=PB[pb][:], scalar=modv(l, 32, kd_, vs), in1=XC[xi][:], op0=ALU.mult, op1=ALU.add),
                                  reads=[PBB[pb], b_xc[xi], b_modt], writes=[b_xn[ni]])
                              P.dma("sp", XRES[kd_ * 128:(kd_ + 1) * 128, t5 * 512:(t5 + 1) * 512], XN[ni][:],
                                    reads=[b_xn[ni]], writes=[XB[kd_][t5]])
                              qi = ci["q"] % 2
                              ci["q"] += 1
                              P.op("act", lambda ni=ni, qi=qi: A_.activation(out=SQc[qi][:], in_=XN[ni][:], func=AF.Square),
                                   reads=[b_xn[ni]], writes=[b_sqc[qi]])
                              P.op("pe", lambda qi=qi, sub=sub, kd_=kd_: nc.tensor.matmul(
                                  PB[sub][:], ones_f[:], SQc[qi][:], start=(kd_ == 0), stop=(kd_ == NK - 1)),
                                  reads=[b_sqc[qi], b_ones], writes=[PBB[sub]])
                  for sub in range(nsub):
                      rs = RSc[:, sub * 512:(sub + 1) * 512]
                      P.op("dve", lambda rs=rs, sub=sub: V_.tensor_scalar(rs, PB[sub][:], 1.0 / D, RMS_EPS, ALU.mult, ALU.add),
                           reads=[PBB[sub]], writes=[b_rsc])
                      P.op("act", lambda rs=rs: A_.activation(out=rs, in_=rs, func=AF.Sqrt), reads=[b_rsc], writes=[b_rsc])
                      P.op("dve", lambda rs=rs: V_.reciprocal(rs, rs), reads=[b_rsc], writes=[b_rsc])
                  for k in range(NK):
                      for sub in range(nsub):
                          t5 = t0 // 512 + sub
                          xi = ci["x"] % 3
                          ci["x"] += 1
                          P.dma("sp", XC[xi][:], XRES[k * 128:(k + 1) * 128, t5 * 512:(t5 + 1) * 512],
                                reads=[XB[k][t5]], writes=[b_xc[xi]])
                          qi = ci["q"] % 2
                          ci["q"] += 1
                          P.op("dve", lambda xi=xi, qi=qi, sub=sub: V_.tensor_tensor(SQc[qi][:], XC[xi][:], RSc[:, sub * 512:(sub + 1) * 512], ALU.mult),
                               reads=[b_xc[xi], b_rsc], writes=[b_sqc[qi]])
                          P.op("act", lambda qi=qi, k=k, sub=sub: A_.activation(
                              out=H2T[:, k, sub * 512:(sub + 1) * 512], in_=SQc[qi][:], func=AF.Identity,
                              scale=SCL[:, l, 1, k, vs:vs + 1], bias=modv(l, 48, k, vs)),
                              reads=[b_sqc[qi], b_scl, b_modt], writes=[b_h2])
                  for fp in range(NF // 2):
                      wi = ci["w"] % 2
                      ci["w"] += 1
                      wr = WR[wi][:].rearrange("p (m k n) -> p m k n", m=2, n=256)
                      P.dma("pool", wr[:, 0], wg_v[:, :, fp * 256:(fp + 1) * 256], writes=[b_wr[wi]])
                      P.dma("pool", wr[:, 1], wu_v[:, :, fp * 256:(fp + 1) * 256], join=[b_wr[wi]])
                      for fb in range(2):
                          f = fp * 2 + fb
                          for sub in range(nsub):
                              pg = 2 + ci["pb"] % 6
                              ci["pb"] += 1
                              pu = 2 + ci["pb"] % 6
                              ci["pb"] += 1
                              for m, pb in ((0, pg), (1, pu)):
                                  for k in range(NK):
                                      P.op("pe", lambda k=k, pb=pb, m=m, fb=fb, sub=sub, wr=wr: nc.tensor.matmul(
                                          PB[pb][:], wr[:, m, k, fb * 128:(fb + 1) * 128], H2T[:, k, sub * 512:(sub + 1) * 512],
                                          start=(k == 0), stop=(k == NK - 1)),
                                          reads=[b_wr[wi], b_h2], writes=[PBB[pb]], inc=(k == NK - 1))
                              si = ci["s"] % 2
                              ci["s"] += 1
                              P.op("act", lambda si=si, pg=pg: A_.activation(out=SGc[si][:], in_=PB[pg][:], func=AF.Silu),
                                   reads=[PBB[pg]], writes=[b_sgc[si]])
                              P.op("dve", lambda si=si, pu=pu, f=f, sub=sub: V_.tensor_tensor(
                                  BIG[:, f, sub * 512:(sub + 1) * 512], PB[pu][:], SGc[si][:], ALU.mult),
                                  reads=[PBB[pu], b_sgc[si]], writes=[b_big[f]])
                  for kd_ in range(NK):
                      wi = ci["w"] % 2
                      ci["w"] += 1
                      wr = WR[wi][:, 0:NF * 128].rearrange("p (k n) -> p k n", n=128)
                      P.dma("pool", wr, wd_v[:, :, kd_ * 128:(kd_ + 1) * 128], writes=[b_wr[wi]])
                      for sub in range(nsub):
                          t5 = t0 // 512 + sub
                          pb = 2 + ci["pb"] % 6
                          ci["pb"] += 1
                          for f in range(NF):
                              P.op("pe", lambda f=f, pb=pb, sub=sub, wr=wr: nc.tensor.matmul(
                                  PB[pb][:], wr[:, f, :], BIG[:, f, sub * 512:(sub + 1) * 512], start=(f == 0), stop=(f == NF - 1)),
                                  reads=[b_wr[wi], b_big[f]], writes=[PBB[pb]], inc=(f == NF - 1),
                                  rec_reads=[b_wr[wi]] + b_big)
                          xi = ci["x"] % 3
                          ci["x"] += 1
                          P.dma("sp", XC[xi][:], XRES[kd_ * 128:(kd_ + 1) * 128, t5 * 512:(t5 + 1) * 512],
                                reads=[XB[kd_][t5]], writes=[b_xc[xi]])
                          ni = ci["n"] % 3
                          ci["n"] += 1
                          P.op("dve", lambda pb=pb, xi=xi, ni=ni, kd_=kd_: V_.scalar_tensor_tensor(
                              out=XN[ni][:], in0=PB[pb][:], scalar=modv(l, 80, kd_, vs), in1=XC[xi][:], op0=ALU.mult, op1=ALU.add),
                              reads=[PBB[pb], b_xc[xi], b_modt], writes=[b_xn[ni]])
                          P.dma("sp", XRES[kd_ * 128:(kd_ + 1) * 128, t5 * 512:(t5 + 1) * 512], XN[ni][:],
                                reads=[b_xn[ni]], writes=[XB[kd_][t5]])
              P.barrier()
          chk("C")

    except _Stop:
        P.barrier()
        return nc

    with contextlib.ExitStack() as cx:
        XT_ = sb("f_xt", [128, NK, 512], F32, cx); b_xt = Buf()
        SQ_ = [sb(f"f_sq{i}", [128, 512], F32, cx) for i in range(2)]; b_sq = [Buf(), Buf()]
        RS_ = sb("f_rs", [128, 512], F32, cx); b_rs = Buf()
        YT_ = [sb(f"f_yt{i}", [128, NK, 512], F32, cx) for i in range(2)]; b_yt = [Buf(), Buf()]
        for t5 in range(NT5):
            yi = t5 % 2
            rms_tile((XT_, b_xt, SQ_, b_sq, RS_, b_rs), XRES, XB, t5, YT_[yi], 0, b_yt[yi],
                     lambda k: NFt[:, k:k + 1], lambda k: 0.0, t5 % 2)
            P.dma("sp", yT.rearrange("(k p) t -> p k t", p=128)[:, :, t5 * 512:(t5 + 1) * 512], YT_[yi][:],
                  reads=[b_yt[yi]], writes=[])
        P.barrier()

    P.barrier()
    P.es.close()
    return nc


def _host_layout(inp, L, TS, core_sample, core_prompts, half=None):
    f = lambda a: np.ascontiguousarray(np.asarray(a, dtype=np.float32))
    mir = (half == 1)
    xs = np.asarray(inp["x_sample"][core_sample], np.float32)
    if half is not None:
        tsl = xs.shape[0] // 2
        xs = xs[half * tsl:(half + 1) * tsl]
    xp = [np.asarray(inp["x_prompt"][p], np.float32) for p in core_prompts]
    if mir:
        xs = xs[::-1]
        xp = [a[::-1] for a in xp]
    x = np.concatenate([xs] + xp, axis=0)
    m = {}
    m["xT"] = f(x.T)
    c = np.asarray(inp["c"][core_sample], np.float32)
    cc = np.asarray(inp["c_ctx"], np.float32)
    cv = np.stack([c.reshape(NK, 128).T, cc.reshape(NK, 128).T], axis=-1)
    m["cvec"] = f(cv)
    st = np.asarray(inp["state_rwkv"])[core_sample]
    m["state0"] = f(st[:, ::-1] if mir else st)

    def pT(a, n):
        a = np.asarray(a, np.float32).reshape(L, n, 128)
        return f(a.transpose(2, 0, 1))
    m["bmodT"] = pT(inp["b_mod"], 96)
    m["n1T"] = pT(inp["norm1_g"], NK)
    m["n2T"] = pT(inp["norm2_g"], NK)
    m["nfT"] = f(np.asarray(inp["final_norm_g"], np.float32).reshape(NK, 128).T)
    mu = np.asarray(inp["mu_shift"], np.float32)
    if mir:
        mu = mu[:, [1, 0, 3, 2], :]
    mup = np.zeros((L, 4, 27 * 128), np.float32)
    mup[:, :, :CS] = mu
    m["muT"] = f(mup.reshape(L, 4, 27, 128).transpose(3, 0, 2, 1))
    dsw = (lambda a: a[:, ::-1]) if mir else (lambda a: a)
    m["w0T"] = f(dsw(np.asarray(inp["w0"], np.float32)).reshape(L, 2, 8, 128).transpose(3, 0, 1, 2))
    m["a0T"] = f(dsw(np.asarray(inp["a0"], np.float32)).reshape(L, 2, 8, 128).transpose(3, 0, 1, 2))
    m["w2"] = f(dsw(np.asarray(inp["w2"], np.float32)))
    m["a2"] = f(dsw(np.asarray(inp["a2"], np.float32)))
    m["kkT"] = pT(inp["k_k"], 8)
    m["kaT"] = pT(inp["k_a"], 8)
    m["rkT"] = pT(np.asarray(inp["r_k"]).reshape(L, 1024), 8)
    m["gnw"] = f(np.asarray(inp["gn_w"]).reshape(L, 1024))
    m["gnb"] = f(np.asarray(inp["gn_b"]).reshape(L, 1024))
    m["lng"] = f(np.asarray(inp["gmlp_ln_g"]).reshape(L, 1024))
    m["lnb"] = f(np.asarray(inp["gmlp_ln_b"]).reshape(L, 1024))
    ws = np.asarray(inp["w_spatial"], np.float32)
    bs = np.asarray(inp["b_spatial"], np.float32)
    if mir:
        ws = ws[:, :, ::-1, ::-1]
        bs = bs[:, :, ::-1]
    m["wsT"] = f(ws.transpose(0, 3, 1, 2))
    m["bsp"] = f(bs.reshape(L, 1, 1024))
    if half is not None:
        sel = np.zeros((128, 2), np.float32)
        sel[:, 1 - half] = 1.0
        m["msel"] = sel
    return m


def kernel(**inp):
    L = int(np.asarray(inp["w_in"]).shape[0])
    TSF = int(np.asarray(inp["x_sample"]).shape[1])
    NB = int(np.asarray(inp["x_prompt"]).shape[0])
    NSMP = int(np.asarray(inp["x_sample"]).shape[0])
    ncores = NB // NPR
    pair = (ncores == 2 * NSMP) and (TSF % 1024 == 0)
    TS = TSF // 2 if pair else TSF
    nc = build(L, TS, pair=pair, ncores=ncores)
    shared = {}
    for k_, src in (("w_mod", "w_mod"), ("w_in", "w_in"), ("g2", "g2"),
                    ("w_out", "w_out"), ("w_g", "w_ffn_gate"), ("w_u", "w_ffn_up"), ("w_d", "w_ffn_down")):
        shared[k_] = np.ascontiguousarray(np.asarray(inp[src], dtype=np.float32))
    in_maps = []
    per = ncores // NSMP
    for c in range(ncores):
        s = c // per
        m = _host_layout(inp, L, TSF, s, [NPR * c + i for i in range(NPR)], half=(c % 2 if pair else None))
        m.update(shared)
        in_maps.append(m)
    res = run_bass_kernel_spmd(nc, in_maps, core_ids=list(range(ncores)))
    y_prompt = np.zeros((NB, SEQP, D), np.float32)
    y_sample = np.zeros((NSMP, TSF, D), np.float32)
    new_state = np.zeros((NB, L, 2, NH, HD, HD), np.float32)
    for c in range(ncores):
        r = res.results[c]
        mir = pair and (c % 2 == 1)
        y = np.asarray(r["yT"]).T
        nst_c = np.asarray(r["nst"])
        for i in range(NPR):
            yp = y[TS + i * SEQP:TS + (i + 1) * SEQP]
            y_prompt[NPR * c + i] = yp[::-1] if mir else yp
            new_state[NPR * c + i] = nst_c[i][:, ::-1] if mir else nst_c[i]
        ys = y[:TS]
        if pair:
            h = c % 2
            y_sample[c // 2, h * TS:(h + 1) * TS] = ys[::-1] if mir else ys
        elif c % per == 0:
            y_sample[c // per] = ys
    return (y_prompt, y_sample, new_state)
```

```python
import contextlib
import numpy as np
import concourse.bass as bass
import concourse.mybir as mybir
from concourse.bass_utils import run_bass_kernel_spmd

F32 = mybir.dt.float32
BF16 = mybir.dt.bfloat16
AF = mybir.ActivationFunctionType
ALU = mybir.AluOpType
AX = mybir.AxisListType

D = 2048
NK = 16
CS = 3360
PIN = 5408
DFF = 5632
NF = 44
NH = 16
HD = 64
SEQP = 256
NPR = 2
RMS_EPS = 1e-6
GN_EPS = 64 * 1e-5
LN_EPS = 1e-5
CDEC = 0.6065306597126334

EPOCH = 12000
N_DMA_SEMS = 24


class Buf:
    __slots__ = ("name", "writers", "readers", "excl")

    def __init__(self, name="", excl=False):
        self.name = name
        self.writers = {}
        self.readers = {}
        self.excl = excl


class Prog:
    def __init__(self, nc):
        self.nc = nc
        self.es = contextlib.ExitStack()
        self.eng = {"pe": nc.tensor, "act": nc.scalar, "dve": nc.vector,
                    "pool": nc.gpsimd, "sp": nc.sync}
        self.sems = {e: [] for e in self.eng}
        self.cnt = {e: 0 for e in self.eng}
        self.seen = {e: {} for e in self.eng}
        self.dma_sems = {}
        self.dma_vals = {}
        self.dma_next = {}
        self.all_dma = {}
        self.n_ins = 0
        for q in ("sp", "pool", "cc"):
            n = N_DMA_SEMS if q != "cc" else 2
            self.dma_sems[q] = [self.es.enter_context(nc.semaphore(f"dq_{q}_{i}")) for i in range(n)]
            self.dma_vals[q] = [0] * n
            self.dma_next[q] = 0
        for e in self.eng:
            self._new_epoch(e)

    def _new_epoch(self, e):
        s = self.es.enter_context(self.nc.semaphore(f"s_{e}_{len(self.sems[e])}"))
        self.sems[e].append(s)
        self.cnt[e] = 0

    def _wait_list(self, e, events):
        need = {}
        for ev in events:
            if ev is None:
                continue
            if ev[0] == "eng":
                _, src, ep, c = ev
                if src == e and e == "pe":
                    continue
                key = ("eng", src, ep)
                sem = self.sems[src][ep]
            else:
                _, q, i, c = ev
                key = ("dma", q, i)
                sem = self.dma_sems[q][i]
            if self.seen[e].get(key, 0) >= c:
                continue
            if key not in need or need[key][1] < c:
                need[key] = (sem, c)
        for key, (sem, c) in need.items():
            self.seen[e][key] = c
        return list(need.values())

    @staticmethod
    def _deps(reads, writes):
        evs = []
        for b in reads:
            evs.extend(b.writers.values())
        for b in writes:
            evs.extend(b.writers.values())
            evs.extend(b.readers.values())
        return evs

    @staticmethod
    def _record(ev, key, reads, writes):
        for b in writes:
            b.writers = {key: ev}
            b.readers = {}
        for b in reads:
            b.readers[key] = ev

    def op(self, e, fn, reads=(), writes=(), inc=True, rec_reads=None):
        if any(b.excl for b in reads):
            writes = list(writes) + [b for b in reads if b.excl]
            reads = [b for b in reads if not b.excl]
        waits = self._wait_list(e, self._deps(reads, writes))
        eng = self.eng[e]
        for sem, c in waits[:-1]:
            eng.wait_ge(sem, c)
        ins = fn()
        if waits:
            ins._wait_ge(waits[-1][0], waits[-1][1])
        self.n_ins += 1
        if not inc:
            return None
        if self.cnt[e] >= EPOCH:
            self._new_epoch(e)
        self.cnt[e] += 1
        ep = len(self.sems[e]) - 1
        ins.then_inc(self.sems[e][ep], 1)
        ev = ("eng", e, ep, self.cnt[e])
        self._record(ev, e, reads if rec_reads is None else rec_reads, writes)
        return ev

    def dma(self, q, out, in_, reads=(), writes=(), join=(), **kw):
        eng = self.eng[q]
        i = self.dma_next[q]
        self.dma_next[q] = (i + 1) % N_DMA_SEMS
        prev = ("dma", q, i, self.dma_vals[q][i]) if self.dma_vals[q][i] else None
        waits = self._wait_list(q, self._deps(reads, writes) + [prev])
        for sem, c in waits:
            eng.wait_ge(sem, c)
        self.dma_vals[q][i] += 16
        v = self.dma_vals[q][i]
        eng.dma_start(out=out, in_=in_, **kw).then_inc(self.dma_sems[q][i], 16)
        self.n_ins += 1
        ev = ("dma", q, i, v)
        self.all_dma[(q, i)] = ev
        self._record(ev, ("dma", q, i, v), reads, writes)
        for b in join:
            b.writers[("dma", q, i, v)] = ev
        return ev

    def coll(self, ins_ap, outs_ap, groups, reads=(), writes=()):
        q = "cc"
        eng = self.eng["pool"]
        i = self.dma_next[q]
        self.dma_next[q] = (i + 1) % len(self.dma_sems[q])
        prev = ("dma", q, i, self.dma_vals[q][i]) if self.dma_vals[q][i] else None
        for sem, c in self._wait_list("pool", self._deps(reads, writes) + [prev]):
            eng.wait_ge(sem, c)
        self.dma_vals[q][i] += 1
        v = self.dma_vals[q][i]
        eng.collective_compute("AllGather", ALU.bypass, replica_groups=groups, ins=[ins_ap], outs=[outs_ap]) \
            .then_inc(self.dma_sems[q][i], 1)
        self.n_ins += 1
        ev = ("dma", q, i, v)
        self.all_dma[(q, i)] = ev
        self._record(ev, ("dma", q, i, v), reads, writes)
        return ev

    def barrier(self):
        evs = []
        for e in self.eng:
            ep = len(self.sems[e]) - 1
            if self.cnt[e] > 0:
                evs.append(("eng", e, ep, self.cnt[e]))
            if ep > 0:
                evs.append(("eng", e, ep - 1, EPOCH))
        evs.extend(self.all_dma.values())
        for e in self.eng:
            for sem, c in self._wait_list(e, evs):
                self.eng[e].wait_ge(sem, c)


class _Stop(Exception):
    pass


def build(L, TS, dbg=False, stop=None, pair=False, ncores=8):
    nc = bass.Bass("TRN2", target_bir_lowering=False)
    P = Prog(nc)
    TT = TS + NPR * SEQP
    NT5 = TT // 512
    NCH = TT // 128
    NW = NT5
    c_ = CDEC

    def din(name, shape, dt=F32):
        return nc.dram_tensor(name, list(shape), dt, kind="ExternalInput").ap()

    def dscr(name, shape, dt=F32):
        kind = "ExternalOutput" if dbg else "Internal"
        return nc.dram_tensor(name, list(shape), dt, kind=kind).ap()

    xT_in = din("xT", [D, TT])
    cvec = din("cvec", [128, NK, 2])
    state0 = din("state0", [L, 2, NH, HD, HD])
    w_mod = din("w_mod", [L, D, 6 * D])
    bmodT = din("bmodT", [128, L, 96])
    n1T = din("n1T", [128, L, NK])
    n2T = din("n2T", [128, L, NK])
    nfT = din("nfT", [128, NK])
    w_in = din("w_in", [L, D, PIN])
    muT = din("muT", [128, L, 27, 4])
    w0T = din("w0T", [128, L, 2, 8])
    a0T = din("a0T", [128, L, 2, 8])
    w2 = din("w2", [L, 2, 64, 1024])
    a2 = din("a2", [L, 2, 64, 1024])
    g2 = din("g2", [L, 160, 1024])
    kkT = din("kkT", [128, L, 8])
    kaT = din("kaT", [128, L, 8])
    rkT = din("rkT", [128, L, 8])
    gnw = din("gnw", [L, 1024])
    gnb = din("gnb", [L, 1024])
    lng = din("lng", [L, 1024])
    lnb = din("lnb", [L, 1024])
    wsT = din("wsT", [L, 128, 8, 128])
    bsp = din("bsp", [L, 1, 1024])
    w_out = din("w_out", [L, D, D])
    w_g = din("w_g", [L, D, DFF])
    w_u = din("w_u", [L, D, DFF])
    w_d = din("w_d", [L, DFF, D])
    msel = din("msel", [128, 2]) if pair else None
    GROUPS = [[2 * i, 2 * i + 1] for i in range(ncores // 2)]

    yT = nc.dram_tensor("yT", [D, TT], F32, kind="ExternalOutput").ap()
    nst = nc.dram_tensor("nst", [NPR, L, 2, NH, HD, HD], F32, kind="ExternalOutput").ap()

    XRES = dscr("XRES", [D, TT])
    PT = dscr("PT", [27 * 128, TT])
    UTd = dscr("UTd", [1024, TT], BF16)
    VTd = dscr("VTd", [TT, 1024])
    OT = dscr("OT", [D, TT], BF16)
    SCd = dscr("SCd", [NCH, 2, 8, 128, 4, 128], BF16)
    VBd = dscr("VBd", [NCH, 8, 128, 128], BF16)
    BTd = dscr("BTd", [1024, TT])
    GTd = dscr("GTd", [1024, TT])
    YGd = dscr("YGd", [2, TT, 1024])
    if pair:
        HSEND = nc.dram_tensor("HSEND", [27 * 128, 64], F32).ap()
        HGATH = nc.dram_tensor("HGATH", [2 * 27 * 128, 64], F32).ap()
        SSEND = nc.dram_tensor("SSEND", [128, 512], F32).ap()
        SGATH = nc.dram_tensor("SGATH", [2 * 128, 512], F32).ap()
        b_hsend, b_hgath, b_ssend, b_sgath = Buf(), Buf(), Buf(), Buf()

    XB = [[Buf() for _ in range(NT5)] for _ in range(NK)]
    PTB = [[Buf() for _ in range(NT5)] for _ in range(27)]
    UTB = [[Buf() for _ in range(NT5)] for _ in range(8)]
    VTB = [[Buf() for _ in range(8)] for _ in range(NT5)]
    OTB = [[Buf() for _ in range(NT5)] for _ in range(2)]
    SCB = [[[Buf() for _ in range(8)] for _ in range(2)] for _ in range(NW)]
    VBB = [[Buf() for _ in range(8)] for _ in range(NW)]
    BTB = [[Buf() for _ in range(8)] for _ in range(NW)]
    GTB = [[Buf() for _ in range(8)] for _ in range(NW)]
    YGB = [[Buf() for _ in range(NCH)] for _ in range(2)]

    es = P.es

    uid = {"n": 0}

    def sb(name, shape, dt=F32, ctx=None):
        uid["n"] += 1
        return (ctx or es).enter_context(nc.sbuf_tensor(f"{name}_{uid['n']}", list(shape), dt))

    PB = [es.enter_context(nc.psum_tensor(f"pb{i}", [128, 512], F32)) for i in range(8)]
    PBB = [Buf(f"pb{i}", excl=True) for i in range(8)]

    ones_f = sb("ones_f", [128, 128]); b_ones = Buf()
    blk_f = sb("blk_f", [128, 128]); b_blk = Buf()
    id_f = sb("id_f", [128, 128]); b_idf = Buf()
    id_b = sb("id_b", [128, 128], BF16); b_idb = Buf()
    ones_row = sb("ones_row", [1, 128], BF16); b_onesrow = Buf()
    MASK = [sb(f"mask{d}", [128, 512]) for d in range(2)]; b_mask = [Buf(), Buf()]
    NMASK = [sb(f"nmask{d}", [128, 128]) for d in range(2)]; b_nmask = [Buf(), Buf()]
    RST = sb("rst", [128, 512]); b_rst = Buf()

    V_, G_, A_, S_ = nc.vector, nc.gpsimd, nc.scalar, nc.sync

    P.op("dve", lambda: V_.memset(ones_f[:], 1.0), writes=[b_ones])
    P.op("dve", lambda: V_.memset(ones_row[:], 1.0), writes=[b_onesrow])
    P.op("dve", lambda: V_.memset(blk_f[:], 0.0), writes=[b_blk])
    P.op("dve", lambda: V_.memset(blk_f[0:64, 0:64], 1.0), reads=[b_blk], writes=[b_blk])
    P.op("dve", lambda: V_.memset(blk_f[64:128, 64:128], 1.0), reads=[b_blk], writes=[b_blk])
    P.op("dve", lambda: V_.memset(id_f[:], 0.0), writes=[b_idf])
    P.op("pool", lambda: G_.affine_select(out=id_f[:], in_=id_f[:], pattern=[[-1, 128]], compare_op=ALU.not_equal,
                                          fill=1.0, base=0, channel_multiplier=1), reads=[b_idf], writes=[b_idf])
    P.op("dve", lambda: V_.tensor_copy(id_b[:], id_f[:]), reads=[b_idf], writes=[b_idb])

    def tri(dst, cm, coef, base, cmp, bufs):
        P.op("dve", lambda: V_.memset(dst, 1.0), writes=bufs)
        P.op("pool", lambda: G_.affine_select(out=dst, in_=dst, pattern=[[coef, 128]], compare_op=cmp,
                                              fill=0.0, base=base, channel_multiplier=cm), reads=bufs, writes=bufs)
    for q in range(4):
        strict = (q % 2 == 0)
        tri(MASK[0][:, q * 128:(q + 1) * 128], -1, 1, 0, ALU.is_gt if strict else ALU.is_ge, [b_mask[0]])
        tri(MASK[1][:, q * 128:(q + 1) * 128], 1, -1, 0, ALU.is_gt if strict else ALU.is_ge, [b_mask[1]])
    tri(NMASK[0][:], 1, -1, 0, ALU.is_gt, [b_nmask[0]])
    tri(NMASK[1][:], -1, 1, 0, ALU.is_gt, [b_nmask[1]])
    P.op("dve", lambda: V_.memset(RST[:], 1.0), writes=[b_rst])
    for c in range(4):
        P.op("dve", lambda c=c: V_.memset(RST[:, c * 128:c * 128 + 1], 0.0), reads=[b_rst], writes=[b_rst])

    MODT = sb("modt", [128, L, 96, 2]); b_modt = Buf()
    SCL = sb("scl", [128, L, 2, NK, 2]); b_scl = Buf()
    N1 = sb("n1", [128, L, NK]); N2 = sb("n2", [128, L, NK]); NFt = sb("nf", [128, NK]); b_n = Buf()
    MU = sb("mu", [128, L, 27, 6]); b_mu = Buf()
    W0 = sb("w0", [128, L, 2, 8]); A0 = sb("a0", [128, L, 2, 8])
    KKp = sb("kkp", [128, L, 8]); KAp = sb("kap", [128, L, 8]); RKp = sb("rkp", [128, L, 8]); b_par = Buf()
    BMT = sb("bmt", [128, L, 96])
    CV = sb("cv", [128, NK, 2]); CSb = sb("csb", [128, NK, 2], BF16); b_cv = Buf()
    MU4 = sb("mu4", [128, L, 27, 4])

    P.dma("sp", N1[:], n1T, writes=[b_n])
    P.dma("sp", N2[:], n2T, writes=[b_n])
    P.dma("sp", NFt[:], nfT, writes=[b_n])
    P.dma("sp", MU4[:], muT, writes=[b_mu])
    P.dma("sp", W0[:], w0T, writes=[b_par])
    P.dma("sp", A0[:], a0T, writes=[b_par])
    P.dma("sp", KKp[:], kkT, writes=[b_par])
    P.dma("sp", KAp[:], kaT, writes=[b_par])
    P.dma("sp", RKp[:], rkT, writes=[b_par])
    P.dma("sp", BMT[:], bmodT, writes=[b_modt])
    P.dma("sp", CV[:], cvec, writes=[b_cv])
    if pair:
        MSEL = sb("msel_t", [128, 2]); b_msel = Buf()
        P.dma("sp", MSEL[:], msel, writes=[b_msel])
    P.op("act", lambda: A_.activation(out=CSb[:], in_=CV[:], func=AF.Silu), reads=[b_cv], writes=[b_cv])
    P.op("dve", lambda: V_.tensor_copy(MU[:, :, :, 0:4], MU4[:]), reads=[b_mu], writes=[b_mu])
    P.op("dve", lambda: V_.reduce_sum(MU[:, :, :, 4], MU4[:], axis=AX.X), reads=[b_mu], writes=[b_mu])
    P.op("dve", lambda: V_.tensor_scalar(MU[:, :, :, 4], MU[:, :, :, 4], -1.0, 1.0, ALU.mult, ALU.add), reads=[b_mu], writes=[b_mu])
    P.op("dve", lambda: V_.tensor_tensor(MU[:, :, :, 5], MU4[:, :, :, 0], MU4[:, :, :, 1], ALU.add), reads=[b_mu], writes=[b_mu])
    P.op("dve", lambda: V_.tensor_scalar(MU[:, :, :, 5], MU[:, :, :, 5], -1.0, 1.0, ALU.mult, ALU.add), reads=[b_mu], writes=[b_mu])

    with contextlib.ExitStack() as cx:
        WM = [sb(f"wm{i}", [128, NK, 1024], BF16, cx) for i in range(2)]
        b_wm = [Buf(), Buf()]
        pc = 0
        for l in range(L):
            wv = w_mod[l].rearrange("(k p) n -> p k n", p=128)
            for piece in range(12):
                t = pc % 2
                pc += 1
                P.dma("pool", WM[t][:], wv[:, :, piece * 1024:(piece + 1) * 1024], writes=[b_wm[t]])
                for jb in range(8):
                    j = piece * 8 + jb
                    for k in range(NK):
                        last = (k == NK - 1) and (jb == 7)
                        P.op("pe", lambda t=t, jb=jb, k=k, j=j: nc.tensor.matmul(
                            PB[0][:, 2 * j:2 * j + 2], WM[t][:, k, jb * 128:(jb + 1) * 128], CSb[:, k, :],
                            start=(k == 0), stop=(k == NK - 1)),
                            reads=[b_wm[t], b_cv], writes=[PBB[0]], inc=last)
            P.op("dve", lambda l=l: V_.tensor_tensor(
                MODT[:, l], PB[0][:, 0:192].rearrange("p (j v) -> p j v", v=2),
                BMT[:, l, :].unsqueeze(2).to_broadcast([128, 96, 2]), ALU.add),
                reads=[PBB[0], b_modt], writes=[b_modt])
            for wn, (Nt, off) in enumerate(((N1, 16), (N2, 64))):
                P.op("dve", lambda l=l, wn=wn, Nt=Nt, off=off: V_.scalar_tensor_tensor(
                    out=SCL[:, l, wn], in0=MODT[:, l, off:off + 16, :], scalar=1.0,
                    in1=Nt[:, l, :].unsqueeze(2).to_broadcast([128, NK, 2]), op0=ALU.add, op1=ALU.mult),
                    reads=[b_modt, b_n], writes=[b_scl])
        P.barrier()
    if stop == "M":
        P.barrier(); P.es.close(); return nc

    def modv(l, off, k, v):
        return MODT[:, l, off + k, v:v + 1]

    def super_tiles():
        res = []
        t0 = 0
        while t0 < TS:
            nt = 1024 if TS - t0 >= 1024 else 512
            res.append((t0, nt, 0))
            t0 += nt
        res.append((TS, 512, 1))
        return res

    STS = super_tiles()
    rr = {"pb": 0}

    def rms_tile(cx_tiles, src_dram, srcB, t5, dst_ht, dst_col0, b_dst, scale_fn, bias_fn, ssbank, extra=None):
        XT_, b_xt, SQ_, b_sq, RS_, b_rs = cx_tiles
        P.dma("sp", XT_[:], src_dram.rearrange("(k p) t -> p k t", p=128)[:, :, t5 * 512:(t5 + 1) * 512],
              reads=[srcB[k][t5] for k in range(NK)], writes=[b_xt])
        for k in range(NK):
            s = k % 2
            P.op("act", lambda k=k, s=s: A_.activation(out=SQ_[s][:], in_=XT_[:, k, :], func=AF.Square),
                 reads=[b_xt], writes=[b_sq[s]])
            P.op("pe", lambda k=k, s=s: nc.tensor.matmul(PB[ssbank][:], ones_f[:], SQ_[s][:], start=(k == 0), stop=(k == NK - 1)),
                 reads=[b_sq[s], b_ones], writes=[PBB[ssbank]])
        P.op("dve", lambda: V_.tensor_scalar(RS_[:], PB[ssbank][:], 1.0 / D, RMS_EPS, ALU.mult, ALU.add),
             reads=[PBB[ssbank]], writes=[b_rs])
        P.op("act", lambda: A_.activation(out=RS_[:], in_=RS_[:], func=AF.Sqrt), reads=[b_rs], writes=[b_rs])
        P.op("dve", lambda: V_.reciprocal(RS_[:], RS_[:]), reads=[b_rs], writes=[b_rs])
        for k in range(NK):
            s = k % 2
            P.op("dve", lambda k=k, s=s: V_.tensor_tensor(SQ_[s][:], XT_[:, k, :], RS_[:], ALU.mult),
                 reads=[b_xt, b_rs, b_sq[s]], writes=[b_sq[s]])
            P.op("act", lambda k=k, s=s: A_.activation(out=dst_ht[:, k, dst_col0:dst_col0 + 512], in_=SQ_[s][:],
                                                        func=AF.Identity, scale=scale_fn(k), bias=bias_fn(k)),
                 reads=[b_sq[s], b_scl, b_modt, b_n], writes=[b_dst])

    def chk(ph):
        if stop == ph:
            raise _Stop()

    try:
      for l in range(L):
          XSRC = xT_in if l == 0 else XRES
          with contextlib.ExitStack() as cx:
              XT_ = sb("a_xt", [128, NK, 512], F32, cx); b_xt = Buf()
              SQ_ = [sb(f"a_sq{i}", [128, 512], F32, cx) for i in range(2)]; b_sq = [Buf(), Buf()]
              RS_ = sb("a_rs", [128, 512], F32, cx); b_rs = Buf()
              HT = sb("a_ht", [128, NK, 1024], BF16, cx); b_ht = Buf()
              WIN = [sb(f"a_win{i}", [128, NK, 512], BF16, cx) for i in range(2)]; b_win = [Buf(), Buf()]
              STG = [sb(f"a_stg{i}", [128, 512], F32, cx) for i in range(4)]; b_stg = [Buf() for _ in range(4)]
              STB = [sb(f"a_stb{i}", [128, 512], BF16, cx) for i in range(2)]; b_stb = [Buf() for _ in range(2)]
              wv = w_in[l].rearrange("(k p) n -> p k n", p=128)
              pieces = [(c0, 512) for c0 in range(0, 3072, 512)] + [(3072, 288)] + \
                       [(3360, 512), (3872, 512), (4384, 512), (4896, 512)]
              wc = 0
              sg = 0
              sgb = 0
              for (t0, NT, vs) in STS:
                  nsub = NT // 512
                  for sub in range(nsub):
                      t5 = t0 // 512 + sub
                      rms_tile((XT_, b_xt, SQ_, b_sq, RS_, b_rs), XSRC, XB, t5, HT, sub * 512, b_ht,
                               lambda k: SCL[:, l, 0, k, vs:vs + 1], lambda k: modv(l, 0, k, vs), 0)
                  for (c0, cw) in pieces:
                      wt = wc % 2
                      wc += 1
                      P.dma("pool", WIN[wt][:, :, 0:cw], wv[:, :, c0:c0 + cw], writes=[b_win[wt]])
                      if c0 < 4384:
                          nblk = (cw + 127) // 128
                          for bi in range(nblk):
                              bw = min(128, cw - bi * 128)
                              col = c0 + bi * 128
                              for sub in range(nsub):
                                  t5 = t0 // 512 + sub
                                  pb = 2 + rr["pb"] % 6
                                  rr["pb"] += 1
                                  for k in range(NK):
                                      P.op("pe", lambda k=k, pb=pb, bi=bi, bw=bw, wt=wt, sub=sub: nc.tensor.matmul(
                                          PB[pb][0:bw, :], WIN[wt][:, k, bi * 128:bi * 128 + bw], HT[:, k, sub * 512:(sub + 1) * 512],
                                          start=(k == 0), stop=(k == NK - 1)),
                                          reads=[b_win[wt], b_ht], writes=[PBB[pb]], inc=(k == NK - 1))
                                  if col < CS:
                                      jb = col // 128
                                      s = sg % 4
                                      sg += 1
                                      eng = "act" if (sg % 2) else "dve"
                                      if eng == "act":
                                          P.op("act", lambda s=s, pb=pb, bw=bw: A_.copy(STG[s][0:bw, :], PB[pb][0:bw, :]),
                                               reads=[PBB[pb]], writes=[b_stg[s]])
                                      else:
                                          P.op("dve", lambda s=s, pb=pb, bw=bw: V_.tensor_copy(STG[s][0:bw, :], PB[pb][0:bw, :]),
                                               reads=[PBB[pb]], writes=[b_stg[s]])
                                      P.dma("sp", PT[jb * 128:jb * 128 + bw, t5 * 512:(t5 + 1) * 512], STG[s][0:bw, :],
                                            reads=[b_stg[s]], writes=[PTB[jb][t5]])
                                  else:
                                      g = (col - CS) // 128
                                      s = sgb % 2
                                      sgb += 1
                                      P.op("act", lambda s=s, pb=pb: A_.activation(out=STB[s][:], in_=PB[pb][:], func=AF.Gelu_apprx_tanh),
                                           reads=[PBB[pb]], writes=[b_stb[s]])
                                      P.dma("sp", UTd[g * 128:(g + 1) * 128, t5 * 512:(t5 + 1) * 512], STB[s][:],
                                            reads=[b_stb[s]], writes=[UTB[g][t5]])
                      else:
                          vc0 = c0 - 4384
                          for sub in range(nsub):
                              t5 = t0 // 512 + sub
                              for ts in range(4):
                                  pb = 2 + rr["pb"] % 6
                                  rr["pb"] += 1
                                  for k in range(NK):
                                      P.op("pe", lambda k=k, pb=pb, wt=wt, sub=sub, ts=ts: nc.tensor.matmul(
                                          PB[pb][:], HT[:, k, sub * 512 + ts * 128:sub * 512 + (ts + 1) * 128], WIN[wt][:, k, :],
                                          start=(k == 0), stop=(k == NK - 1)),
                                          reads=[b_win[wt], b_ht], writes=[PBB[pb]], inc=(k == NK - 1))
                                  s = sg % 4
                                  sg += 1
                                  P.op("act", lambda s=s, pb=pb: A_.activation(out=STG[s][:], in_=PB[pb][:], func=AF.Gelu_apprx_tanh),
                                       reads=[PBB[pb]], writes=[b_stg[s]])
                                  tok0 = t5 * 512 + ts * 128
                                  P.dma("sp", VTd[tok0:tok0 + 128, vc0:vc0 + 512], STG[s][:],
                                        reads=[b_stg[s]], writes=[VTB[t5][ts * 2 + vc0 // 512]])
              P.barrier()
          if pair:
              P.dma("sp", HSEND, PT[:, TS - 64:TS], reads=[PTB[j][TS // 512 - 1] for j in range(27)], writes=[b_hsend])
              P.coll(HSEND, HGATH, GROUPS, reads=[b_hsend], writes=[b_hgath])
          chk("A")

          with contextlib.ExitStack() as cx:
              VT_ = sb("g_vt", [128, 4, 1024], F32, cx); b_vt = Buf()
              VC_ = sb("g_vc", [128, 4, 1024], F32, cx); b_vc = Buf()
              SQg = sb("g_sq", [128, 4, 1024], F32, cx); b_sqg = Buf()
              VN_ = sb("g_vn", [128, 4, 1024], BF16, cx); b_vn = Buf()
              UTt = sb("g_ut", [128, 8, 512], BF16, cx); b_ut = Buf()
              OBt = sb("g_ob", [128, 8, 512], BF16, cx); b_ob = Buf()
              LNG = sb("g_lng", [128, 1024], F32, cx); LNB = sb("g_lnb", [128, 1024], F32, cx); b_ln = Buf()
              WST = sb("g_wst", [128, 8, 128], BF16, cx); BSP = sb("g_bsp", [1, 1024], BF16, cx); b_ws = Buf()
              ST1 = sb("g_s1", [128, 32], F32, cx); ST2 = sb("g_s2", [128, 32], F32, cx); b_st = Buf()
              P.dma("sp", LNG[:], lng[l].partition_broadcast(128), writes=[b_ln])
              P.dma("sp", LNB[:], lnb[l].partition_broadcast(128), writes=[b_ln])
              P.dma("pool", WST[:], wsT[l], writes=[b_ws])
              P.dma("pool", BSP[:], bsp[l], writes=[b_ws])
              for t5 in range(NT5):
                  P.dma("sp", VT_[:], VTd[t5 * 512:(t5 + 1) * 512, :].rearrange("(a p) c -> p a c", p=128),
                        reads=VTB[t5], writes=[b_vt])
                  P.dma("sp", UTt[:], UTd[:, t5 * 512:(t5 + 1) * 512].rearrange("(g p) t -> p g t", p=128),
                        reads=[UTB[g][t5] for g in range(8)], writes=[b_ut])
                  V4 = VT_[:].rearrange("p a (g d) -> p (a g) d", d=128)
                  C4 = VC_[:].rearrange("p a (g d) -> p (a g) d", d=128)
                  Q4 = SQg[:].rearrange("p a (g d) -> p (a g) d", d=128)
                  P.op("dve", lambda: V_.reduce_sum(ST1[:], V4, axis=AX.X), reads=[b_vt], writes=[b_st])
                  P.op("dve", lambda: V_.tensor_scalar(ST1[:], ST1[:], 1.0 / 128, None, ALU.mult), reads=[b_st], writes=[b_st])
                  P.op("dve", lambda: V_.tensor_tensor(C4, V4, ST1[:].unsqueeze(2).to_broadcast([128, 32, 128]), ALU.subtract),
                       reads=[b_vt, b_st], writes=[b_vc])
                  P.op("act", lambda: A_.activation(out=SQg[:], in_=VC_[:], func=AF.Square), reads=[b_vc], writes=[b_sqg])
                  P.op("dve", lambda: V_.reduce_sum(ST2[:], Q4, axis=AX.X), reads=[b_sqg], writes=[b_st])
                  P.op("dve", lambda: V_.tensor_scalar(ST2[:], ST2[:], 1.0 / 128, LN_EPS, ALU.mult, ALU.add), reads=[b_st], writes=[b_st])
                  P.op("act", lambda: A_.activation(out=ST2[:], in_=ST2[:], func=AF.Sqrt), reads=[b_st], writes=[b_st])
                  P.op("dve", lambda: V_.reciprocal(ST2[:], ST2[:]), reads=[b_st], writes=[b_st])
                  P.op("dve", lambda: V_.tensor_tensor(C4, C4, ST2[:].unsqueeze(2).to_broadcast([128, 32, 128]), ALU.mult),
                       reads=[b_vc, b_st], writes=[b_vc])
                  P.op("pool", lambda: G_.tensor_tensor(VC_[:], VC_[:], LNG[:].unsqueeze(1).to_broadcast([128, 4, 1024]), ALU.mult),
                       reads=[b_vc, b_ln], writes=[b_vc])
                  P.op("dve", lambda: V_.tensor_tensor(VN_[:], VC_[:], LNB[:].unsqueeze(1).to_broadcast([128, 4, 1024]), ALU.add),
                       reads=[b_vc, b_ln], writes=[b_vn])
                  for g in range(8):
                      pb = g
                      for a in range(4):
                          P.op("pe", lambda a=a, g=g, pb=pb: nc.tensor.matmul(
                              PB[pb][:, a * 128:(a + 1) * 128], VN_[:, a, g * 128:(g + 1) * 128], WST[:, g, :], start=True, stop=False),
                              reads=[b_vn, b_ws], writes=[PBB[pb]], inc=False)
                          P.op("pe", lambda a=a, g=g, pb=pb: nc.tensor.matmul(
                              PB[pb][:, a * 128:(a + 1) * 128], ones_row[0:1, :], BSP[0:1, g * 128:(g + 1) * 128], start=False, stop=True),
                              reads=[b_onesrow, b_ws, b_vn], writes=[PBB[pb]], inc=(a == 3))
                      P.op("dve", lambda g=g, pb=pb: V_.tensor_tensor(OBt[:, g, :], PB[pb][:], UTt[:, g, :], ALU.mult),
                           reads=[PBB[pb], b_ut], writes=[b_ob])
                  P.dma("sp", OT[1024:2048, t5 * 512:(t5 + 1) * 512].rearrange("(g p) t -> p g t", p=128), OBt[:],
                        reads=[b_ob], writes=[OTB[1][t5]])
              P.barrier()
          chk("G")

          GAM = sb(f"gam{l}", [128, 2, 8, NCH], F32); b_gam = Buf()
          with contextlib.ExitStack() as cx:
              GIN = [sb(f"b_gin{i}", [128, 640], F32, cx) for i in range(3)]; b_gin = [Buf() for _ in range(3)]
              SH = sb("b_sh", [128, 27, 512], F32, cx); b_sh = [Buf() for _ in range(27)]
              WTb = sb("b_wt", [128, 512], BF16, cx); ADb = sb("b_ad", [128, 512], BF16, cx)
              SGb = sb("b_sg", [128, 512], BF16, cx); SG2b = sb("b_sg2", [128, 512], BF16, cx); b_lo = Buf()
              W2s = sb("b_w2", [128, 2, 1024], BF16, cx); A2s = sb("b_a2", [128, 2, 1024], BF16, cx)
              G2a = sb("b_g2a", [128, 1024], BF16, cx); G2b = sb("b_g2b", [128, 1024], BF16, cx); b_lw = Buf()
              ROLES = {}
              SCQ = [sb(f"b_scq{i}", [128, 4, 4, 128], BF16, cx) for i in range(2)]; b_scq = [Buf(), Buf()]
              if pair:
                  HSL = [sb(f"b_hsl{i}", [128, 27, 64], F32, cx) for i in range(2)]; b_hsl = Buf()
                  HALO = sb("b_halo", [128, 27, 64], F32, cx); b_halo = Buf()
              VBs = [sb(f"b_vbs{i}", [128, 512], BF16, cx) for i in range(2)]; b_vbs = [Buf(), Buf()]
              tmi = {"q": 0, "v": 0, "g": 0}

              def tmp(role, par):
                  key = (role, par % 2)
                  if key not in ROLES:
                      ROLES[key] = (sb(f"b_r_{role}{par % 2}", [128, 512], F32, cx), Buf())
                  return ROLES[key]

              P.dma("pool", W2s[0:64, :, :], w2[l].rearrange("d j c -> j d c"), writes=[b_lw])
              P.dma("pool", A2s[64:128, :, :], a2[l].rearrange("d j c -> j d c"), writes=[b_lw])
              P.dma("pool", G2a[:], g2[l, 0:128, :], writes=[b_lw])
              P.dma("pool", G2b[0:32, :], g2[l, 128:160, :], writes=[b_lw])

              for w in range(NW):
                  is_p = (w == NW - 1)
                  t0 = w * 512
                  W = 512
                  for j in range(27):
                      npar = 128 if j < 26 else 32
                      gi = j % 3
                      g_, bg = GIN[gi], b_gin[gi]
                      rows = slice(j * 128, j * 128 + npar)
                      if not is_p:
                          lo = t0 - 64
                          hi = t0 + W + 64
                          rd = [PTB[j][w]]
                          if lo < 0:
                              P.op("pool", lambda g_=g_, npar=npar: G_.memset(g_[0:npar, 0:64], 0.0), writes=[bg])
                              lo = 0
                          else:
                              rd.append(PTB[j][w - 1])
                          if hi > TS:
                              if pair:
                                  if j == 0:
                                      for sl_ in range(2):
                                          P.dma("sp", HSL[sl_][:], HGATH[sl_ * 3456:(sl_ + 1) * 3456, :].rearrange("(j p) t -> p j t", p=128),
                                                reads=[b_hgath], writes=[b_hsl])
                                      P.op("dve", lambda: V_.tensor_scalar(HALO[:], HSL[0][:], MSEL[:, 0:1], None, ALU.mult),
                                           reads=[b_hsl, b_msel], writes=[b_halo])
                                      P.op("dve", lambda: V_.scalar_tensor_tensor(out=HALO[:], in0=HSL[1][:], scalar=MSEL[:, 1:2], in1=HALO[:],
                                                                                  op0=ALU.mult, op1=ALU.add),
                                           reads=[b_hsl, b_msel, b_halo], writes=[b_halo])
                                  P.op("pool", lambda g_=g_, npar=npar, j=j: G_.tensor_copy(g_[0:npar, 576:640], HALO[0:npar, j, ::-1]),
                                       reads=[b_halo], writes=[bg])
                              else:
                                  P.op("pool", lambda g_=g_, npar=npar: G_.memset(g_[0:npar, 576:640], 0.0), writes=[bg])
                              hi = TS
                          else:
                              rd.append(PTB[j][w + 1])
                          P.dma("sp", g_[0:npar, lo - (t0 - 64):hi - (t0 - 64)], PT[rows, lo:hi], reads=rd, writes=[bg])
                          ctr = g_[0:npar, 64:64 + W]
                          acc = SH[0:npar, j, :]
                          P.op("act", lambda acc=acc, ctr=ctr, j=j, npar=npar: A_.activation(out=acc, in_=ctr, func=AF.Copy, scale=MU[0:npar, l, j, 4:5]),
                               reads=[bg, b_mu], writes=[b_sh[j]])
                          a3 = acc.rearrange("p (r c) -> p r c", c=64)
                          c3 = ctr.rearrange("p (r c) -> p r c", c=64)
                          P.op("dve", lambda a3=a3, c3=c3, j=j, npar=npar: V_.scalar_tensor_tensor(
                              out=a3[:, :, 1:64], in0=c3[:, :, 0:63], scalar=MU[0:npar, l, j, 0:1], in1=a3[:, :, 1:64], op0=ALU.mult, op1=ALU.add),
                              reads=[bg, b_mu, b_sh[j]], writes=[b_sh[j]])
                          P.op("dve", lambda a3=a3, c3=c3, j=j, npar=npar: V_.scalar_tensor_tensor(
                              out=a3[:, :, 0:63], in0=c3[:, :, 1:64], scalar=MU[0:npar, l, j, 1:2], in1=a3[:, :, 0:63], op0=ALU.mult, op1=ALU.add),
                              reads=[bg, b_mu, b_sh[j]], writes=[b_sh[j]])
                          P.op("dve", lambda acc=acc, g_=g_, j=j, npar=npar: V_.scalar_tensor_tensor(
                              out=acc, in0=g_[0:npar, 0:W], scalar=MU[0:npar, l, j, 2:3], in1=acc, op0=ALU.mult, op1=ALU.add),
                              reads=[bg, b_mu, b_sh[j]], writes=[b_sh[j]])
                          P.op("dve", lambda acc=acc, g_=g_, j=j, npar=npar: V_.scalar_tensor_tensor(
                              out=acc, in0=g_[0:npar, 128:128 + W], scalar=MU[0:npar, l, j, 3:4], in1=acc, op0=ALU.mult, op1=ALU.add),
                              reads=[bg, b_mu, b_sh[j]], writes=[b_sh[j]])
                      else:
                          P.dma("sp", g_[0:npar, 64:64 + W], PT[rows, t0:t0 + W], reads=[PTB[j][w]], writes=[bg])
                          ctr = g_[0:npar, 64:64 + W]
                          acc = SH[0:npar, j, :]
                          P.op("act", lambda acc=acc, ctr=ctr, j=j, npar=npar: A_.activation(out=acc, in_=ctr, func=AF.Copy, scale=MU[0:npar, l, j, 5:6]),
                               reads=[bg, b_mu], writes=[b_sh[j]])
                          a3 = acc.rearrange("p (r c) -> p r c", c=SEQP)
                          c3 = ctr.rearrange("p (r c) -> p r c", c=SEQP)
                          P.op("dve", lambda a3=a3, c3=c3, j=j, npar=npar: V_.scalar_tensor_tensor(
                              out=a3[:, :, 1:SEQP], in0=c3[:, :, 0:SEQP - 1], scalar=MU[0:npar, l, j, 0:1], in1=a3[:, :, 1:SEQP], op0=ALU.mult, op1=ALU.add),
                              reads=[bg, b_mu, b_sh[j]], writes=[b_sh[j]])
                          P.op("dve", lambda a3=a3, c3=c3, j=j, npar=npar: V_.scalar_tensor_tensor(
                              out=a3[:, :, 0:SEQP - 1], in0=c3[:, :, 1:SEQP], scalar=MU[0:npar, l, j, 1:2], in1=a3[:, :, 0:SEQP - 1], op0=ALU.mult, op1=ALU.add),
                              reads=[bg, b_mu, b_sh[j]], writes=[b_sh[j]])
                  P.op("act", lambda: A_.activation(out=WTb[0:64, :], in_=SH[0:64, 24, :], func=AF.Tanh), reads=[b_sh[24]], writes=[b_lo])
                  P.op("dve", lambda: V_.tensor_copy(ADb[64:128, :], SH[64:128, 24, :]), reads=[b_sh[24]], writes=[b_lo])
                  P.op("act", lambda: A_.activation(out=SGb[:], in_=SH[:, 25, :], func=AF.Sigmoid), reads=[b_sh[25]], writes=[b_lo])
                  P.op("act", lambda: A_.activation(out=SG2b[0:32, :], in_=SH[0:32, 26, :], func=AF.Sigmoid), reads=[b_sh[26]], writes=[b_lo])
                  def jbody(j):
                      jc = slice(j * 128, (j + 1) * 128)
                      Rj, Kj, Vj = SH[:, j, :], SH[:, 8 + j, :], SH[:, 16 + j, :]
                      bR, bK, bV = b_sh[j], b_sh[8 + j], b_sh[16 + j]
                      pbg = j % 2
                      P.op("pe", lambda: nc.tensor.matmul(PB[pbg][:], G2a[:, jc], SGb[:], start=True, stop=False),
                           reads=[b_lw, b_lo], writes=[PBB[pbg]], inc=False)
                      yield
                      P.op("pe", lambda: nc.tensor.matmul(PB[pbg][:], G2b[0:32, jc], SG2b[0:32, :], start=False, stop=True),
                           reads=[b_lw, b_lo], writes=[PBB[pbg]])
                      yield
                      tg, btg = tmp("tg", j)
                      P.op("act", lambda tg=tg: A_.copy(tg[:], PB[pbg][:]), reads=[PBB[pbg]], writes=[btg])
                      yield
                      P.dma("sp", GTd[jc, t0:t0 + W], tg[:], reads=[btg], writes=[GTB[w][j]])
                      yield
                      vi = tmi["v"] % 2
                      tmi["v"] += 1
                      P.op("pool", lambda vi=vi, Vj=Vj: G_.tensor_copy(VBs[vi][:], Vj), reads=[bV], writes=[b_vbs[vi]])
                      yield
                      P.dma("sp", VBd[w * 4:(w + 1) * 4, j].rearrange("c p t -> p c t"),
                            VBs[vi][:].rearrange("p (c t) -> p c t", t=128), reads=[b_vbs[vi]], writes=[VBB[w][j]])
                      yield
                      t1, bt1 = tmp("t1", j)
                      P.op("act", lambda t1=t1, Kj=Kj, j=j: A_.activation(out=t1[:], in_=Kj, func=AF.Copy, scale=KKp[:, l, j:j + 1]),
                           reads=[bK, b_par], writes=[bt1])
                      yield
                      t2, bt2 = tmp("t2", j)
                      P.op("act", lambda t1=t1, t2=t2: A_.activation(out=t2[:], in_=t1[:], func=AF.Square), reads=[bt1], writes=[bt2])
                      yield
                      P.op("pe", lambda t2=t2: nc.tensor.matmul(PB[2][:], blk_f[:], t2[:], start=True, stop=True),
                           reads=[bt2, b_blk], writes=[PBB[2]])
                      yield
                      P.op("dve", lambda t2=t2: V_.tensor_scalar_max(t2[:], PB[2][:], 1e-24), reads=[PBB[2]], writes=[bt2])
                      yield
                      P.op("act", lambda t2=t2: A_.activation(out=t2[:], in_=t2[:], func=AF.Sqrt), reads=[bt2], writes=[bt2])
                      yield
                      P.op("dve", lambda t2=t2: V_.reciprocal(t2[:], t2[:]), reads=[bt2], writes=[bt2])
                      yield
                      kkn, bkkn = tmp("kkn", j)
                      P.op("pool", lambda kkn=kkn, t1=t1, t2=t2: G_.tensor_tensor(kkn[:], t1[:], t2[:], ALU.mult),
                           reads=[bt1, bt2], writes=[bkkn])
                      yield
                      yield "HEAD_DONE"
                      def dbody(d):
                          pz = 3 + d
                          pa = 5 + d
                          P.op("pe", lambda d=d, pz=pz: nc.tensor.matmul(PB[pz][:], W2s[0:64, d, jc], WTb[0:64, :], start=True, stop=True),
                               reads=[b_lw, b_lo], writes=[PBB[pz]])
                          yield
                          lws, blws = tmp("lws", d)
                          P.op("act", lambda lws=lws, pz=pz, d=d, j=j: A_.activation(out=lws[:], in_=PB[pz][:], func=AF.Sigmoid, bias=W0[:, l, d, j:j + 1]),
                               reads=[PBB[pz], b_par], writes=[blws])
                          yield
                          P.op("pe", lambda d=d, pa=pa: nc.tensor.matmul(PB[pa][:], A2s[64:128, d, jc], ADb[64:128, :], start=True, stop=True),
                               reads=[b_lw, b_lo], writes=[PBB[pa]])
                          yield
                          alr, balr = tmp("alr", d)
                          P.op("act", lambda alr=alr, pa=pa, d=d, j=j: A_.activation(out=alr[:], in_=PB[pa][:], func=AF.Sigmoid, bias=A0[:, l, d, j:j + 1]),
                               reads=[PBB[pa], b_par], writes=[balr])
                          yield
                          kd, bkd = tmp("kd", d)
                          P.op("dve", lambda kd=kd, alr=alr, j=j: V_.tensor_scalar(kd[:], alr[:], -1.0, KAp[:, l, j:j + 1], ALU.add, ALU.mult),
                               reads=[balr, b_par], writes=[bkd])
                          yield
                          P.op("dve", lambda kd=kd, Kj=Kj: V_.scalar_tensor_tensor(out=kd[:], in0=kd[:], scalar=1.0, in1=Kj, op0=ALU.add, op1=ALU.mult),
                               reads=[bkd, bK], writes=[bkd])
                          yield
                          bv, bbv = tmp("bv", d)
                          P.op("pool", lambda bv=bv, kkn=kkn, alr=alr: G_.tensor_tensor(bv[:], kkn[:], alr[:], ALU.mult),
                               reads=[bkkn, balr], writes=[bbv])
                          yield
                          t4, bt4 = tmp("t4", d)
                          P.op("dve", lambda t4=t4, Rj=Rj, kd=kd, j=j: V_.scalar_tensor_tensor(out=t4[:], in0=Rj, scalar=RKp[:, l, j:j + 1], in1=kd[:], op0=ALU.mult, op1=ALU.mult),
                               reads=[bR, bkd, b_par], writes=[bt4])
                          yield
                          P.op("pe", lambda t4=t4, d=d: nc.tensor.matmul(PB[7][:], blk_f[:], t4[:], start=(d == 0), stop=(d == 1)),
                               reads=[bt4, b_blk], writes=[PBB[7]])
                          yield
                          cs, bcs = tmp("cs", d)
                          P.op("dve", lambda cs=cs, lws=lws: V_.tensor_tensor_scan(cs[:], RST[:], lws[:], 0.0, ALU.mult, ALU.add),
                               reads=[blws, b_rst], writes=[bcs])
                          yield
                          ge, bge = tmp("ge", d)
                          gi_, bgi = tmp("gi", d)
                          cs3 = cs[:].rearrange("p (c t) -> p c t", t=128)
                          if d == 0:
                              P.op("pool", lambda ge=ge, cs=cs, lws=lws: G_.tensor_tensor(ge[:], cs[:], lws[:], ALU.subtract),
                                   reads=[bcs, blws], writes=[bge])
                              yield
                              GI, bGI = cs, bcs
                          else:
                              P.op("dve", lambda ge=ge, cs3=cs3: V_.tensor_tensor(
                                  ge[:].rearrange("p (c t) -> p c t", t=128), cs3[:, :, 127:128].to_broadcast([128, 4, 128]), cs3, ALU.subtract),
                                  reads=[bcs], writes=[bge])
                              yield
                              P.op("pool", lambda gi_=gi_, ge=ge, lws=lws: G_.tensor_tensor(gi_[:], ge[:], lws[:], ALU.add),
                                   reads=[bge, blws], writes=[bgi])
                              yield
                              GI, bGI = gi_, bgi
                          P.op("act", lambda cs3=cs3, d=d, j=j, w=w: A_.activation(out=GAM[:, d, j, w * 4:(w + 1) * 4], in_=cs3[:, :, 127], func=AF.Exp, scale=-c_),
                               reads=[bcs], writes=[b_gam])
                          yield
                          e1, be1 = tmp("e1", d)
                          e2, be2 = tmp("e2", d)
                          e3, be3 = tmp("e3", d)
                          P.op("act", lambda e1=e1, GI=GI: A_.activation(out=e1[:], in_=GI[:], func=AF.Exp, scale=-c_), reads=[bGI], writes=[be1])
                          yield
                          P.op("act", lambda e2=e2, ge=ge: A_.activation(out=e2[:], in_=ge[:], func=AF.Exp, scale=-c_), reads=[bge], writes=[be2])
                          yield
                          P.op("act", lambda e3=e3, GI=GI: A_.activation(out=e3[:], in_=GI[:], func=AF.Exp, scale=c_), reads=[bGI], writes=[be3])
                          yield
                          qi = d
                          q_, bq = SCQ[qi], b_scq[qi]
                          c3_ = lambda ap: ap.rearrange("p (c t) -> p c t", t=128)
                          P.op("dve", lambda q_=q_, kkn=kkn, e2=e2: V_.scalar_tensor_tensor(out=q_[:, :, 0, :], in0=c3_(kkn[:]), scalar=-1.0, in1=c3_(e2[:]), op0=ALU.mult, op1=ALU.mult),
                               reads=[bkkn, be2], writes=[bq])
                          yield
                          P.op("pool", lambda q_=q_, Rj=Rj, e1=e1: G_.tensor_tensor(q_[:, :, 1, :], c3_(Rj), c3_(e1[:]), ALU.mult),
                               reads=[bR, be1, bq], writes=[bq])
                          yield
                          P.op("dve", lambda q_=q_, bv=bv, e3=e3: V_.tensor_tensor(q_[:, :, 2, :], c3_(bv[:]), c3_(e3[:]), ALU.mult),
                               reads=[bbv, be3, bq], writes=[bq])
                          yield
                          P.op("pool", lambda q_=q_, kd=kd, e3=e3: G_.tensor_tensor(q_[:, :, 3, :], c3_(kd[:]), c3_(e3[:]), ALU.mult),
                               reads=[bkd, be3, bq], writes=[bq])
                          yield
                          P.dma("sp", SCd[w * 4:(w + 1) * 4, d, j].rearrange("c p q t -> p c (q t)"),
                                q_[:].rearrange("p c q t -> p c (q t)"), reads=[bq], writes=[SCB[w][d][j]])
                          yield

                      gens_ = [dbody(0), dbody(1)]
                      while gens_:
                          nx_ = []
                          for g__ in gens_:
                              try:
                                  next(g__)
                                  nx_.append(g__)
                              except StopIteration:
                                  pass
                          gens_ = nx_
                          yield
                      tb, btb = tmp("tb", j)
                      P.op("dve", lambda tb=tb, Vj=Vj: V_.tensor_tensor(tb[:], PB[7][:], Vj, ALU.mult), reads=[PBB[7], bV], writes=[btb])
                      P.dma("sp", BTd[jc, t0:t0 + W], tb[:], reads=[btb], writes=[BTB[w][j]])

                  cur = jbody(0)
                  while next(cur) != "HEAD_DONE":
                      pass
                  for j in range(8):
                      nxt = jbody(j + 1) if j < 7 else None
                      nxt_head = nxt is not None
                      cur_live = True
                      while cur_live or nxt_head:
                          if cur_live:
                              try:
                                  next(cur)
                              except StopIteration:
                                  cur_live = False
                          if nxt_head:
                              if next(nxt) == "HEAD_DONE":
                                  nxt_head = False
                      cur = nxt
              P.barrier()
          chk("B1")

          with contextlib.ExitStack() as cx:
              NSC = 4
              SCT = [sb(f"s_sct{i}", [128, 8, 4, 128], BF16, cx) for i in range(NSC)]; b_sct = [Buf() for _ in range(NSC)]
              VBT = [sb(f"s_vbt{i}", [128, 8, 128], BF16, cx) for i in range(NSC)]; b_vbt = [Buf() for _ in range(NSC)]
              HS = [sb(f"s_hs{d}", [128, 8, 64], F32, cx) for d in range(2)]
              HBt = [sb(f"s_hb{d}", [128, 8, 64], BF16, cx) for d in range(2)]
              b_hs = [[Buf() for _ in range(NH)] for _ in range(2)]
              YS = [sb(f"s_ys{i}", [128, NH, 64], F32, cx) for i in range(2)]; b_ys = [Buf(), Buf()]
              YC = sb("s_yc", [128, NH, 64], F32, cx); b_yc = Buf()
              YQ = sb("s_yq", [128, NH, 64], F32, cx); b_yq = Buf()
              YO = [sb(f"s_yo{i}", [128, 1024], F32, cx) for i in range(2)]; b_yo = [Buf(), Buf()]
              GS1 = sb("s_gs1", [128, NH], F32, cx); GS2 = sb("s_gs2", [128, NH], F32, cx); b_gs = Buf()
              GNW = sb("s_gnw", [128, 1024], F32, cx); GNB = sb("s_gnb", [128, 1024], F32, cx); b_gn = Buf()
              SV = sb("s_sv", [64, NH, 64], F32, cx); b_sv = Buf()
              P.dma("sp", GNW[:], gnw[l].partition_broadcast(128), writes=[b_gn])
              P.dma("sp", GNB[:], gnb[l].partition_broadcast(128), writes=[b_gn])
              NS = 4
              TOK = [[sb(f"s_tok{s}_{hp}", [128, 3, 128], BF16, cx) for hp in range(2)] for s in range(NS)]
              b_tok = [[Buf(), Buf()] for _ in range(NS)]
              VTK = [sb(f"s_vtk{s}", [128, 64], BF16, cx) for s in range(NS)]; b_vtk = [Buf() for _ in range(NS)]
              SCM = [sb(f"s_scm{s}", [128, 512], BF16, cx) for s in range(NS)]; b_scm = [Buf() for _ in range(NS)]
              NMt = [sb(f"s_nm{s}", [128, 128], BF16, cx) for s in range(NS)]; b_nm = [Buf() for _ in range(NS)]
              XS = [sb(f"s_xs{s}", [128, 64], BF16, cx) for s in range(NS)]; b_xs = [Buf() for _ in range(NS)]
              QS = [sb(f"s_qs{s}", [128, 256], F32, cx) for s in range(NS)]; b_qs = [Buf() for _ in range(NS)]
              PW = [sb(f"s_pw{s}", [128, 128], F32, cx) for s in range(NS)]; b_pw = [Buf() for _ in range(NS)]
              MTb = [sb(f"s_mtb{s}", [128, 128], BF16, cx) for s in range(NS)]; b_mtb = [Buf() for _ in range(NS)]
              AH = [sb(f"s_ah{s}", [128, 128], BF16, cx) for s in range(NS)]; b_ah = [Buf() for _ in range(NS)]
              US = [sb(f"s_us{s}", [128, 64], BF16, cx) for s in range(NS)]; b_us = [Buf() for _ in range(NS)]
              TH = [sb(f"s_th{s}", [128, 64], F32, cx) for s in range(NS)]; b_th = [Buf() for _ in range(NS)]
              for s in range(NS):
                  for hp in range(2):
                      P.op("dve", lambda s=s, hp=hp: V_.memset(TOK[s][hp][:], 0.0), writes=[b_tok[s][hp]])
              PSTv = [PB[2 * s][:].bitcast(BF16) for s in range(NS)]
              sci = {"i": 0, "y": 0}

              def group(s, h, d, chg, sct, bsct, vbt, bvbt, ys, bys):
                  j, hp = h // 2, h % 2
                  P0 = hp * 64
                  hs = slice(P0, P0 + 64)
                  PSA, bPSA = PB[2 * s], PBB[2 * s]
                  PSC = PB[2 * s + 1]
                  PSD, bPSD = PSC, PBB[2 * s + 1]
                  PST, bPST = PSTv[s], bPSA
                  bN = bPSD
                  bX = bU = bY = bH = bPSA
                  R_N = PSC[:, 384:512]
                  R_X = PSA[:, 0:64]
                  R_U = PSA[:, 64:128]
                  R_Y = PSA[:, 128:192]
                  R_H = PSA[:, 192:256]
                  aT = sct[hs, j, 0, :]
                  rT = sct[hs, j, 1, :]
                  bT = sct[hs, j, 2, :]
                  kT = sct[hs, j, 3, :]
                  arT = sct[hs, j, 0:2, :].rearrange("p q t -> p (q t)")
                  vT = vbt[hs, j, :]
                  idh = id_b[hs, P0:P0 + 64]
                  tok, btok = TOK[s][hp], b_tok[s][hp]
                  for qi, src in enumerate((aT, bT, kT)):
                      P.op("pe", lambda qi=qi, src=src: nc.tensor.transpose(PST[:, qi * 64:(qi + 1) * 64], src, idh),
                           reads=[bsct, b_idb], writes=[bPST], inc=False)
                  P.op("pe", lambda: nc.tensor.transpose(PST[:, 192:256], vT, idh), reads=[bvbt, b_idb], writes=[bPST])
                  yield
                  P.op("act", lambda: A_.copy(tok[:, :, P0:P0 + 64], PST[:, 0:192].rearrange("p (q c) -> p q c", c=64)),
                       reads=[bPST], writes=[btok])
                  P.op("dve", lambda: V_.tensor_copy(VTK[s][:], PST[:, 192:256]), reads=[bPST], writes=[b_vtk[s]])
                  P.op("pe", lambda: nc.tensor.matmul(PSA[:, 0:256], bT, arT, start=True, stop=True), reads=[bsct], writes=[bPSA], inc=False)
                  P.op("pe", lambda: nc.tensor.matmul(PSA[:, 256:512], kT, arT, start=True, stop=True), reads=[bsct], writes=[bPSA])
                  P.op("pe", lambda: nc.tensor.matmul(R_N, aT, bT, start=True, stop=True), reads=[bsct], writes=[bN])
                  yield
                  P.op("dve", lambda: V_.tensor_tensor(QS[s][:, 0:128], PSA[:, 0:128], MASK[d][:, 0:128], ALU.mult),
                       reads=[bPSA, b_mask[d]], writes=[b_qs[s]])
                  P.op("dve", lambda: V_.tensor_tensor(SCM[s][:, 128:512], PSA[:, 128:512], MASK[d][:, 128:512], ALU.mult),
                       reads=[bPSA, b_mask[d]], writes=[b_scm[s]])
                  P.op("dve", lambda: V_.tensor_tensor(PW[s][:], R_N, NMASK[d][:], ALU.mult), reads=[bN, b_nmask[d]], writes=[b_pw[s]])
                  P.op("pe", lambda: nc.tensor.matmul(R_X, SCM[s][:, 256:384], VTK[s][:], start=True, stop=True),
                       reads=[b_scm[s], b_vtk[s]], writes=[bX])
                  P.op("pe", lambda: nc.tensor.matmul(PSD[:, 0:128], PW[s][:], QS[s][:, 0:128], start=True, stop=True),
                       reads=[b_pw[s], b_qs[s]], writes=[bPSD], inc=False)
                  P.op("pe", lambda: nc.tensor.matmul(PSD[:, 256:384], QS[s][:, 0:128], PW[s][:], start=True, stop=True),
                       reads=[b_pw[s], b_qs[s]], writes=[bPSD])
                  yield
                  P.op("act", lambda: A_.copy(XS[s][:], R_X), reads=[bX], writes=[b_xs[s]])
                  P.op("pool", lambda: G_.tensor_tensor(QS[s][:, 128:256], QS[s][:, 0:128], id_f[:], ALU.add),
                       reads=[b_qs[s], b_idf], writes=[b_qs[s]])
                  P.op("act", lambda: A_.copy(QS[s][:, 0:128], PSD[:, 0:128]), reads=[bPSD, b_qs[s]], writes=[b_qs[s]])
                  P.op("act", lambda: A_.copy(PW[s][:], PSD[:, 256:384]), reads=[bPSD], writes=[b_pw[s]])
                  for lev in range(1, 6):
                      P.op("pe", lambda: nc.tensor.matmul(PSD[:, 0:256], PW[s][:], QS[s][:], start=True, stop=True),
                           reads=[b_pw[s], b_qs[s]], writes=[bPSD], inc=False)
                      P.op("pe", lambda: nc.tensor.matmul(PSD[:, 256:384], QS[s][:, 0:128], PW[s][:], start=True, stop=True),
                           reads=[b_pw[s], b_qs[s]], writes=[bPSD])
                      yield
                      P.op("act", lambda: A_.copy(QS[s][:, 0:128], PSD[:, 0:128]), reads=[bPSD], writes=[b_qs[s]])
                      P.op("dve", lambda: V_.tensor_tensor(QS[s][:, 128:256], PSD[:, 128:256], QS[s][:, 128:256], ALU.add),
                           reads=[bPSD, b_qs[s]], writes=[b_qs[s]])
                      P.op("act", lambda: A_.copy(PW[s][:], PSD[:, 256:384]), reads=[bPSD], writes=[b_pw[s]])
                  P.op("pe", lambda: nc.tensor.matmul(PSD[:, 128:256], PW[s][:], QS[s][:, 128:256], start=True, stop=True),
                       reads=[b_pw[s], b_qs[s]], writes=[bPSD])
                  yield
                  P.op("dve", lambda: V_.tensor_tensor(MTb[s][:], PSD[:, 128:256], QS[s][:, 128:256], ALU.add),
                       reads=[bPSD, b_qs[s]], writes=[b_mtb[s]])
                  MT = MTb[s][:]
                  P.op("pe", lambda: nc.tensor.matmul(R_U, MT, XS[s][:], start=True, stop=False),
                       reads=[b_mtb[s], b_xs[s]], writes=[bU], inc=False)
                  P.op("pe", lambda: nc.tensor.matmul(R_N, tok[:, 0, :], MT, start=True, stop=True),
                       reads=[btok, b_mtb[s]], writes=[bN])
                  yield
                  P.op("act", lambda: A_.copy(AH[s][hs, :], PSC[hs, 384:512]), reads=[bN], writes=[b_ah[s]])
                  hb = HBt[d][hs, j, :]
                  P.op("pe", lambda: nc.tensor.matmul(R_U, AH[s][hs, :], hb, start=False, stop=True),
                       reads=[b_ah[s], b_hs[d][h]], writes=[bU])
                  yield
                  P.op("act", lambda: A_.copy(US[s][:], R_U), reads=[bU], writes=[b_us[s]])
                  P.op("pe", lambda: nc.tensor.matmul(R_Y, rT, hb, start=True, stop=False),
                       reads=[bsct, b_hs[d][h]], writes=[bY], inc=False)
                  P.op("pe", lambda: nc.tensor.matmul(R_Y, SCM[s][:, 128:256], US[s][:], start=False, stop=False),
                       reads=[b_scm[s], b_us[s]], writes=[bY], inc=False)
                  P.op("pe", lambda: nc.tensor.matmul(R_Y, SCM[s][:, 384:512], VTK[s][:], start=False, stop=True),
                       reads=[b_scm[s], b_vtk[s]], writes=[bY], rec_reads=[bsct, b_hs[d][h], b_scm[s], b_us[s], b_vtk[s]])
                  P.op("pe", lambda: nc.tensor.matmul(R_H, tok[:, 1, :], US[s][:], start=True, stop=False),
                       reads=[btok, b_us[s]], writes=[bH], inc=False)
                  P.op("pe", lambda: nc.tensor.matmul(R_H, tok[:, 2, :], VTK[s][:], start=False, stop=True),
                       reads=[btok, b_vtk[s]], writes=[bH], rec_reads=[btok, b_us[s], b_vtk[s]])
                  yield
                  P.op("dve", lambda: V_.tensor_copy(ys[:, h, :], R_Y), reads=[bY], writes=[bys])
                  P.op("dve", lambda: V_.tensor_tensor(TH[s][hs, :], PSA[hs, 192:256], HS[d][hs, j, :], ALU.add),
                       reads=[bH, b_hs[d][h]], writes=[b_th[s]])
                  gam = GAM[hs, d, j, chg:chg + 1]
                  P.op("dve", lambda: V_.tensor_scalar(HS[d][hs, j, :], TH[s][hs, :], gam, None, ALU.mult),
                       reads=[b_th[s], b_gam], writes=[b_hs[d][h]])
                  P.op("act", lambda: A_.activation(out=HBt[d][hs, j, :], in_=TH[s][hs, :], func=AF.Copy, scale=gam),
                       reads=[b_th[s], b_gam, b_hs[d][h]], writes=[b_hs[d][h]])

              def run_pair(gens):
                  live = list(gens)
                  while live:
                      nxt = []
                      for g in live:
                          if stop and stop.startswith("B2:"):
                              sci["steps"] = sci.get("steps", 0) + 1
                              if sci["steps"] > int(stop.split(":")[1]):
                                  raise _Stop()
                          try:
                              next(g)
                              nxt.append(g)
                          except StopIteration:
                              pass
                      live = nxt

              def gn_gen(d, chg, ys, bys):
                  yi = sci["y"] % 2
                  sci["y"] += 1
                  yo, byo = YO[yi], b_yo[yi]
                  P.op("dve", lambda: V_.reduce_sum(GS1[:], ys[:], axis=AX.X), reads=[bys], writes=[b_gs])
                  yield
                  P.op("dve", lambda: V_.tensor_scalar(GS1[:], GS1[:], 1.0 / 64, None, ALU.mult), reads=[b_gs], writes=[b_gs])
                  yield
                  P.op("dve", lambda: V_.tensor_tensor(YC[:], ys[:], GS1[:].unsqueeze(2).to_broadcast([128, NH, 64]), ALU.subtract),
                       reads=[bys, b_gs], writes=[b_yc])
                  yield
                  P.op("act", lambda: A_.activation(out=YQ[:], in_=YC[:], func=AF.Square), reads=[b_yc], writes=[b_yq])
                  yield
                  P.op("dve", lambda: V_.reduce_sum(GS2[:], YQ[:], axis=AX.X), reads=[b_yq], writes=[b_gs])
                  yield
                  P.op("dve", lambda: V_.tensor_scalar(GS2[:], GS2[:], 1.0 / 64, GN_EPS, ALU.mult, ALU.add), reads=[b_gs], writes=[b_gs])
                  yield
                  P.op("act", lambda: A_.activation(out=GS2[:], in_=GS2[:], func=AF.Sqrt), reads=[b_gs], writes=[b_gs])
                  yield
                  P.op("dve", lambda: V_.reciprocal(GS2[:], GS2[:]), reads=[b_gs], writes=[b_gs])
                  yield
                  P.op("dve", lambda: V_.tensor_tensor(YC[:], YC[:], GS2[:].unsqueeze(2).to_broadcast([128, NH, 64]), ALU.mult),
                       reads=[b_yc, b_gs], writes=[b_yc])
                  yield
                  ycf = YC[:].rearrange("p h v -> p (h v)")
                  P.op("pool", lambda: G_.tensor_tensor(ycf, ycf, GNW[:], ALU.mult), reads=[b_yc, b_gn], writes=[b_yc])
                  yield
                  P.op("pool", lambda: G_.tensor_tensor(yo[:], ycf, GNB[:], ALU.add), reads=[b_yc, b_gn], writes=[byo])
                  yield
                  P.dma("sp", YGd[d, chg * 128:(chg + 1) * 128, :], yo[:], reads=[byo], writes=[YGB[d][chg]])
                  yield

              def init_zero(d):
                  P.op("dve", lambda: V_.memset(HS[d][:], 0.0), reads=[b for b in b_hs[d]], writes=[b for b in b_hs[d]])
                  P.op("dve", lambda: V_.memset(HBt[d][:], 0.0), reads=[b for b in b_hs[d]], writes=[b for b in b_hs[d]])

              def init_dram(d):
                  P.dma("sp", SV[:], state0[l, d].rearrange("h v k -> v h k"), writes=[b_sv])
                  for j in range(8):
                      pb = 6 + j % 2
                      P.op("pe", lambda: nc.tensor.transpose(
                          PB[pb][:, 0:64], SV[:, 2 * j:2 * j + 2, :].rearrange("v h k -> v (h k)"), id_f[0:64, 0:64]),
                          reads=[b_sv, b_idf], writes=[PBB[pb]])
                      P.op("dve", lambda: V_.tensor_copy(HS[d][:, j, :], PB[pb][:, 0:64]),
                           reads=[PBB[pb]], writes=[b_hs[d][2 * j], b_hs[d][2 * j + 1]])
                      P.op("act", lambda: A_.copy(HBt[d][:, j, :], PB[pb][:, 0:64]),
                           reads=[PBB[pb], b_hs[d][2 * j], b_hs[d][2 * j + 1]], writes=[b_hs[d][2 * j], b_hs[d][2 * j + 1]])

              def send_state(d):
                  P.dma("sp", SSEND, HS[d][:].rearrange("p j v -> p (j v)"), reads=[b for b in b_hs[d]], writes=[b_ssend])
                  P.coll(SSEND, SGATH, GROUPS, reads=[b_ssend], writes=[b_sgath])

              def init_recv(d):
                  for sl_ in range(2):
                      P.dma("sp", RCV[sl_][:], SGATH[sl_ * 128:(sl_ + 1) * 128, :], reads=[b_sgath], writes=[b_rcv])
                  hsf = HS[d][:].rearrange("p j v -> p (j v)")
                  P.op("dve", lambda: V_.tensor_scalar(hsf, RCV[0][:], MSEL[:, 0:1], None, ALU.mult),
                       reads=[b_rcv, b_msel] + b_hs[d], writes=b_hs[d])
                  P.op("dve", lambda: V_.scalar_tensor_tensor(out=hsf, in0=RCV[1][:], scalar=MSEL[:, 1:2], in1=hsf, op0=ALU.mult, op1=ALU.add),
                       reads=[b_rcv, b_msel] + b_hs[d], writes=b_hs[d])
                  P.op("act", lambda: A_.copy(HBt[d][:].rearrange("p j v -> p (j v)"), hsf), reads=b_hs[d], writes=b_hs[d])

              def run_chunks(ch0, ncs, dirs):
                  for i in range(ncs):
                      for d in dirs:
                          ch = i if d == 0 else ncs - 1 - i
                          chg = ch0 + ch
                          w = chg // 4
                          si = sci["i"] % NSC
                          sci["i"] += 1
                          sct, bsct, vbt, bvbt = SCT[si], b_sct[si], VBT[si], b_vbt[si]
                          P.dma("sp", sct[:], SCd[chg, d].rearrange("j p q t -> p j q t"),
                                reads=[SCB[w][d][j] for j in range(8)], writes=[bsct])
                          P.dma("sp", vbt[:], VBd[chg].rearrange("j p t -> p j t"),
                                reads=[VBB[w][j] for j in range(8)], writes=[bvbt])
                          ysi = (sci["i"]) % 2
                          ys, bys = YS[ysi], b_ys[ysi]
                          for m in range(NH // NS):
                              gl = [group(q, NS * m + q, d, chg, sct, bsct, vbt, bvbt, ys, bys) for q in range(NS)]
                              if m == 0 and sci.get("gn") is not None:
                                  gl.append(sci["gn"])
                                  sci["gn"] = None
                              run_pair(gl)
                          sci["gn"] = gn_gen(d, chg, ys, bys)

              def final_states(pi):
                  for d in range(2):
                      for j in range(8):
                          pb = 6 + j % 2
                          P.op("pe", lambda: nc.tensor.transpose(PB[pb][0:64, 0:128], HS[d][:, j, :], id_f[:]),
                               reads=[b_hs[d][2 * j], b_hs[d][2 * j + 1], b_idf], writes=[PBB[pb]])
                          P.op("dve", lambda: V_.tensor_copy(
                              SV[:, 2 * j:2 * j + 2, :].rearrange("v h k -> v (h k)"), PB[pb][0:64, 0:128]),
                              reads=[PBB[pb]], writes=[b_sv])
                      P.dma("sp", nst[pi, l, d].rearrange("h v k -> v h k"), SV[:], reads=[b_sv], writes=[])

              prompts = [((TS + pi * SEQP) // 128, SEQP // 128, pi) for pi in range(NPR)]
              if pair:
                  RCV = [sb(f"s_rcv{i}", [128, 512], F32, cx) for i in range(2)]; b_rcv = Buf()
                  init_dram(0)
                  run_chunks(0, TS // 128, (0,))
                  send_state(0)
              else:
                  init_dram(0)
                  init_dram(1)
                  run_chunks(0, TS // 128, (0, 1))
              for (ch0, ncs, pi) in prompts:
                  init_zero(0)
                  init_zero(1)
                  run_chunks(ch0, ncs, (0, 1))
                  final_states(pi)
              if pair:
                  init_recv(1)
                  run_chunks(0, TS // 128, (1,))
              if sci.get("gn") is not None:
                  for _ in sci["gn"]:
                      pass
                  sci["gn"] = None
              P.barrier()
          chk("B2")

          with contextlib.ExitStack() as cx:
              YA = sb("c_ya", [128, 4, 1024], F32, cx); b_ya = Buf()
              YBt = sb("c_yb", [128, 4, 1024], F32, cx); b_yb = Buf()
              BTt = sb("c_bt", [128, 8, 512], F32, cx); b_bt = Buf()
              GTt = sb("c_gt", [128, 8, 512], F32, cx); b_gt = Buf()
              OA = sb("c_oa", [128, 8, 512], BF16, cx); b_oa = Buf()
              TMq = [sb(f"c_tm{i}", [128, 512], F32, cx) for i in range(2)]; b_tmq = [Buf(), Buf()]
              for w in range(NW):
                  t0 = w * 512
                  P.dma("sp", YA[:], YGd[0, t0:t0 + 512, :].rearrange("(a p) c -> p a c", p=128),
                        reads=[YGB[0][w * 4 + a] for a in range(4)], writes=[b_ya])
                  P.dma("sp", YBt[:], YGd[1, t0:t0 + 512, :].rearrange("(a p) c -> p a c", p=128),
                        reads=[YGB[1][w * 4 + a] for a in range(4)], writes=[b_yb])
                  P.dma("sp", BTt[:], BTd[:, t0:t0 + 512].rearrange("(j p) t -> p j t", p=128),
                        reads=[BTB[w][j] for j in range(8)], writes=[b_bt])
                  P.dma("sp", GTt[:], GTd[:, t0:t0 + 512].rearrange("(j p) t -> p j t", p=128),
                        reads=[GTB[w][j] for j in range(8)], writes=[b_gt])
                  P.op("pool", lambda: G_.tensor_tensor(YA[:], YA[:], YBt[:], ALU.add), reads=[b_ya, b_yb], writes=[b_ya])
                  for j in range(8):
                      pb = j
                      for a in range(4):
                          P.op("pe", lambda a=a, j=j, pb=pb: nc.tensor.transpose(PB[pb][:, a * 128:(a + 1) * 128], YA[:, a, j * 128:(j + 1) * 128], id_f[:]),
                               reads=[b_ya, b_idf], writes=[PBB[pb]], inc=(a == 3))
                      q = j % 2
                      P.op("dve", lambda j=j, pb=pb, q=q: V_.tensor_tensor(TMq[q][:], PB[pb][:], BTt[:, j, :], ALU.add),
                           reads=[PBB[pb], b_bt], writes=[b_tmq[q]])
                      P.op("pool", lambda j=j, q=q: G_.tensor_tensor(OA[:, j, :], TMq[q][:], GTt[:, j, :], ALU.mult),
                           reads=[b_tmq[q], b_gt], writes=[b_oa])
                  P.dma("sp", OT[0:1024, t0:t0 + 512].rearrange("(j p) t -> p j t", p=128), OA[:], reads=[b_oa], writes=[OTB[0][w]])
              P.barrier()
          chk("B3")

          with contextlib.ExitStack() as cx:
              BIG = sb("d_big", [128, NF, 1024], BF16, cx); b_big = [Buf() for _ in range(NF)]
              H2T = sb("d_h2t", [128, NK, 1024], BF16, cx); b_h2 = Buf()
              WR = [sb(f"d_wr{i}", [128, 8192], BF16, cx) for i in range(2)]; b_wr = [Buf(), Buf()]
              XC = [sb(f"d_xc{i}", [128, 512], F32, cx) for i in range(3)]; b_xc = [Buf() for _ in range(3)]
              XN = [sb(f"d_xn{i}", [128, 512], F32, cx) for i in range(3)]; b_xn = [Buf() for _ in range(3)]
              SQc = [sb(f"d_sq{i}", [128, 512], F32, cx) for i in range(2)]; b_sqc = [Buf(), Buf()]
              RSc = sb("d_rs", [128, 1024], F32, cx); b_rsc = Buf()
              SGc = [sb(f"d_sg{i}", [128, 512], F32, cx) for i in range(2)]; b_sgc = [Buf(), Buf()]
              ci = {"w": 0, "x": 0, "n": 0, "q": 0, "s": 0, "pb": 0}
              wo_v = w_out[l].rearrange("(k p) n -> p k n", p=128)
              wg_v = w_g[l].rearrange("(k p) n -> p k n", p=128)
              wu_v = w_u[l].rearrange("(k p) n -> p k n", p=128)
              wd_v = w_d[l].rearrange("(k p) n -> p k n", p=128)
              for (t0, NT, vs) in STS:
                  nsub = NT // 512
                  for k in range(NK):
                      half = 0 if k < 8 else 1
                      P.dma("sp", BIG[:, k, 0:NT], OT[k * 128:(k + 1) * 128, t0:t0 + NT],
                            reads=[OTB[half][t0 // 512 + s_] for s_ in range(nsub)], writes=[b_big[k]])
                  for dq in range(4):
                      wi = ci["w"] % 2
                      ci["w"] += 1
                      wr = WR[wi][:].rearrange("p (k n) -> p k n", n=512)
                      P.dma("pool", wr, wo_v[:, :, dq * 512:(dq + 1) * 512], writes=[b_wr[wi]])
                      for db in range(4):
                          kd_ = dq * 4 + db
                          for sub in range(nsub):
                              t5 = t0 // 512 + sub
                              pb = 2 + ci["pb"] % 6
                              ci["pb"] += 1
                              for k in range(NK):
                                  P.op("pe", lambda k=k, pb=pb, db=db, sub=sub, wr=wr: nc.tensor.matmul(
                                      PB[pb][:], wr[:, k, db * 128:(db + 1) * 128], BIG[:, k, sub * 512:(sub + 1) * 512],
                                      start=(k == 0), stop=(k == NK - 1)),
                                      reads=[b_wr[wi], b_big[k]], writes=[PBB[pb]], inc=(k == NK - 1),
                                      rec_reads=[b_wr[wi]] + b_big[0:NK])
                              xi = ci["x"] % 3
                              ci["x"] += 1
                              P.dma("sp", XC[xi][:], XSRC[kd_ * 128:(kd_ + 1) * 128, t5 * 512:(t5 + 1) * 512],
                                    reads=[XB[kd_][t5]], writes=[b_xc[xi]])
                              ni = ci["n"] % 3
                              ci["n"] += 1
                              P.op("dve", lambda pb=pb, xi=xi, ni=ni, kd_=kd_: V_.scalar_tensor_tensor(
                                  out=XN[ni][:], in0=PB[pb][:], scalar=modv(l, 32, kd_, vs), in1=XC[xi][:], op0=ALU.mult, op1=ALU.add),
                                  reads=[PBB[pb], b_xc[xi], b_modt], writes=[b_xn[ni]])
                              P.dma("sp", XRES[kd_ * 128:(kd_ + 1) * 128, t5 * 512:(t5 + 1) * 512], XN[ni][:],
                                    reads=[b_xn[ni]], writes=[XB[kd_][t5]])
                              qi = ci["q"] % 2
                              ci["q"] += 1
                              P.op("act", lambda ni=ni, qi=qi: A_.activation(out=SQc[qi][:], in_=XN[ni][:], func=AF.Square),
                                   reads=[b_xn[ni]], writes=[b_sqc[qi]])
                              P.op("pe", lambda qi=qi, sub=sub, kd_=kd_: nc.tensor.matmul(
                                  PB[sub][:], ones_f[:], SQc[qi][:], start=(kd_ == 0), stop=(kd_ == NK - 1)),
                                  reads=[b_sqc[qi], b_ones], writes=[PBB[sub]])
                  for sub in range(nsub):
                      rs = RSc[:, sub * 512:(sub + 1) * 512]
                      P.op("dve", lambda rs=rs, sub=sub: V_.tensor_scalar(rs, PB[sub][:], 1.0 / D, RMS_EPS, ALU.mult, ALU.add),
                           reads=[PBB[sub]], writes=[b_rsc])
                      P.op("act", lambda rs=rs: A_.activation(out=rs, in_=rs, func=AF.Sqrt), reads=[b_rsc], writes=[b_rsc])
                      P.op("dve", lambda rs=rs: V_.reciprocal(rs, rs), reads=[b_rsc], writes=[b_rsc])
                  for k in range(NK):
                      for sub in range(nsub):
                          t5 = t0 // 512 + sub
                          xi = ci["x"] % 3
                          ci["x"] += 1
                          P.dma("sp", XC[xi][:], XRES[k * 128:(k + 1) * 128, t5 * 512:(t5 + 1) * 512],
                                reads=[XB[k][t5]], writes=[b_xc[xi]])
                          qi = ci["q"] % 2
                          ci["q"] += 1
                          P.op("dve", lambda xi=xi, qi=qi, sub=sub: V_.tensor_tensor(SQc[qi][:], XC[xi][:], RSc[:, sub * 512:(sub + 1) * 512], ALU.mult),
                               reads=[b_xc[xi], b_rsc], writes=[b_sqc[qi]])
                          P.op("act", lambda qi=qi, k=k, sub=sub: A_.activation(
                              out=H2T[:, k, sub * 512:(sub + 1) * 512], in_=SQc[qi][:], func=AF.Identity,
                              scale=SCL[:, l, 1, k, vs:vs + 1], bias=modv(l, 48, k, vs)),
                              reads=[b_sqc[qi], b_scl, b_modt], writes=[b_h2])
                  for fp in range(NF // 2):
                      wi = ci["w"] % 2
                      ci["w"] += 1
                      wr = WR[wi][:].rearrange("p (m k n) -> p m k n", m=2, n=256)
                      P.dma("pool", wr[:, 0], wg_v[:, :, fp * 256:(fp + 1) * 256], writes=[b_wr[wi]])
                      P.dma("pool", wr[:, 1], wu_v[:, :, fp * 256:(fp + 1) * 256], join=[b_wr[wi]])
                      for fb in range(2):
                          f = fp * 2 + fb
                          for sub in range(nsub):
                              pg = 2 + ci["pb"] % 6
                              ci["pb"] += 1
                              pu = 2 + ci["pb"] % 6
                              ci["pb"] += 1
                              for m, pb in ((0, pg), (1, pu)):
                                  for k in range(NK):
                                      P.op("pe", lambda k=k, pb=pb, m=m, fb=fb, sub=sub, wr=wr: nc.tensor.matmul(
                                          PB[pb][:], wr[:, m, k, fb * 128:(fb + 1) * 128], H2T[:, k, sub * 512:(sub + 1) * 512],
                                          start=(k == 0), stop=(k == NK - 1)),
                                          reads=[b_wr[wi], b_h2], writes=[PBB[pb]], inc=(k == NK - 1))
                              si = ci["s"] % 2
                              ci["s"] += 1
                              P.op("act", lambda si=si, pg=pg: A_.activation(out=SGc[si][:], in_=PB[pg][:], func=AF.Silu),
                                   reads=[PBB[pg]], writes=[b_sgc[si]])
                              P.op("dve", lambda si=si, pu=pu, f=f, sub=sub: V_.tensor_tensor(
                                  BIG[:, f, sub * 512:(sub + 1) * 512], PB[pu][:], SGc[si][:], ALU.mult),
                                  reads=[PBB[pu], b_sgc[si]], writes=[b_big[f]])
                  for kd_ in range(NK):
                      wi = ci["w"] % 2
                      ci["w"] += 1
                      wr = WR[wi][:, 0:NF * 128].rearrange("p (k n) -> p k n", n=128)
                      P.dma("pool", wr, wd_v[:, :, kd_ * 128:(kd_ + 1) * 128], writes=[b_wr[wi]])
                      for sub in range(nsub):
                          t5 = t0 // 512 + sub
                          pb = 2 + ci["pb"] % 6
                          ci["pb"] += 1
                          for f in range(NF):
                              P.op("pe", lambda f=f, pb=pb, sub=sub, wr=wr: nc.tensor.matmul(
                                  PB[pb][:], wr[:, f, :], BIG[:, f, sub * 512:(sub + 1) * 512], start=(f == 0), stop=(f == NF - 1)),
                                  reads=[b_wr[wi], b_big[f]], writes=[PBB[pb]], inc=(f == NF - 1),
                                  rec_reads=[b_wr[wi]] + b_big)
                          xi = ci["x"] % 3
                          ci["x"] += 1
                          P.dma("sp", XC[xi][:], XRES[kd_ * 128:(kd_ + 1) * 128, t5 * 512:(t5 + 1) * 512],
                                reads=[XB[kd_][t5]], writes=[b_xc[xi]])
                          ni = ci["n"] % 3
                          ci["n"] += 1
                          P.op("dve", lambda pb=pb, xi=xi, ni=ni, kd_=kd_: V_.scalar_tensor_tensor(
                              out=XN[ni][:], in0=PB[pb][:], scalar=modv(l, 80, kd_, vs), in1=XC[xi][:], op0=ALU.mult, op1=ALU.add),
                              reads=[PBB[pb], b_xc[xi], b_modt], writes=[b_xn[ni]])
                          P.dma("sp", XRES[kd_ * 128:(kd_ + 1) * 128, t5 * 512:(t5 + 1) * 512], XN[ni][:],
                                reads=[b_xn[ni]], writes=[XB[kd_][t5]])
              P.barrier()
          chk("C")

    except _Stop:
        P.barrier()
        return nc

    with contextlib.ExitStack() as cx:
        XT_ = sb("f_xt", [128, NK, 512], F32, cx); b_xt = Buf()
        SQ_ = [sb(f"f_sq{i}", [128, 512], F32, cx) for i in range(2)]; b_sq = [Buf(), Buf()]
        RS_ = sb("f_rs", [128, 512], F32, cx); b_rs = Buf()
        YT_ = [sb(f"f_yt{i}", [128, NK, 512], F32, cx) for i in range(2)]; b_yt = [Buf(), Buf()]
        for t5 in range(NT5):
            yi = t5 % 2
            rms_tile((XT_, b_xt, SQ_, b_sq, RS_, b_rs), XRES, XB, t5, YT_[yi], 0, b_yt[yi],
                     lambda k: NFt[:, k:k + 1], lambda k: 0.0, t5 % 2)
            P.dma("sp", yT.rearrange("(k p) t -> p k t", p=128)[:, :, t5 * 512:(t5 + 1) * 512], YT_[yi][:],
                  reads=[b_yt[yi]], writes=[])
        P.barrier()

    P.barrier()
    P.es.close()
    return nc


def _host_layout(inp, L, TS, core_sample, core_prompts, half=None):
    f = lambda a: np.ascontiguousarray(np.asarray(a, dtype=np.float32))
    mir = (half == 1)
    xs = np.asarray(inp["x_sample"][core_sample], np.float32)
    if half is not None:
        tsl = xs.shape[0] // 2
        xs = xs[half * tsl:(half + 1) * tsl]
    xp = [np.asarray(inp["x_prompt"][p], np.float32) for p in core_prompts]
    if mir:
        xs = xs[::-1]
        xp = [a[::-1] for a in xp]
    x = np.concatenate([xs] + xp, axis=0)
    m = {}
    m["xT"] = f(x.T)
    c = np.asarray(inp["c"][core_sample], np.float32)
    cc = np.asarray(inp["c_ctx"], np.float32)
    cv = np.stack([c.reshape(NK, 128).T, cc.reshape(NK, 128).T], axis=-1)
    m["cvec"] = f(cv)
    st = np.asarray(inp["state_rwkv"])[core_sample]
    m["state0"] = f(st[:, ::-1] if mir else st)

    def pT(a, n):
        a = np.asarray(a, np.float32).reshape(L, n, 128)
        return f(a.transpose(2, 0, 1))
    m["bmodT"] = pT(inp["b_mod"], 96)
    m["n1T"] = pT(inp["norm1_g"], NK)
    m["n2T"] = pT(inp["norm2_g"], NK)
    m["nfT"] = f(np.asarray(inp["final_norm_g"], np.float32).reshape(NK, 128).T)
    mu = np.asarray(inp["mu_shift"], np.float32)
    if mir:
        mu = mu[:, [1, 0, 3, 2], :]
    mup = np.zeros((L, 4, 27 * 128), np.float32)
    mup[:, :, :CS] = mu
    m["muT"] = f(mup.reshape(L, 4, 27, 128).transpose(3, 0, 2, 1))
    dsw = (lambda a: a[:, ::-1]) if mir else (lambda a: a)
    m["w0T"] = f(dsw(np.asarray(inp["w0"], np.float32)).reshape(L, 2, 8, 128).transpose(3, 0, 1, 2))
    m["a0T"] = f(dsw(np.asarray(inp["a0"], np.float32)).reshape(L, 2, 8, 128).transpose(3, 0, 1, 2))
    m["w2"] = f(dsw(np.asarray(inp["w2"], np.float32)))
    m["a2"] = f(dsw(np.asarray(inp["a2"], np.float32)))
    m["kkT"] = pT(inp["k_k"], 8)
    m["kaT"] = pT(inp["k_a"], 8)
    m["rkT"] = pT(np.asarray(inp["r_k"]).reshape(L, 1024), 8)
    m["gnw"] = f(np.asarray(inp["gn_w"]).reshape(L, 1024))
    m["gnb"] = f(np.asarray(inp["gn_b"]).reshape(L, 1024))
    m["lng"] = f(np.asarray(inp["gmlp_ln_g"]).reshape(L, 1024))
    m["lnb"] = f(np.asarray(inp["gmlp_ln_b"]).reshape(L, 1024))
    ws = np.asarray(inp["w_spatial"], np.float32)
    bs = np.asarray(inp["b_spatial"], np.float32)
    if mir:
        ws = ws[:, :, ::-1, ::-1]
        bs = bs[:, :, ::-1]
    m["wsT"] = f(ws.transpose(0, 3, 1, 2))
    m["bsp"] = f(bs.reshape(L, 1, 1024))
    if half is not None:
        sel = np.zeros((128, 2), np.float32)
        sel[:, 1 - half] = 1.0
        m["msel"] = sel
    return m


def kernel(**inp):
    L = int(np.asarray(inp["w_in"]).shape[0])
    TSF = int(np.asarray(inp["x_sample"]).shape[1])
    NB = int(np.asarray(inp["x_prompt"]).shape[0])
    NSMP = int(np.asarray(inp["x_sample"]).shape[0])
    ncores = NB // NPR
    pair = (ncores == 2 * NSMP) and (TSF % 1024 == 0)
    TS = TSF // 2 if pair else TSF
    nc = build(L, TS, pair=pair, ncores=ncores)
    shared = {}
    for k_, src in (("w_mod", "w_mod"), ("w_in", "w_in"), ("g2", "g2"),
                    ("w_out", "w_out"), ("w_g", "w_ffn_gate"), ("w_u", "w_ffn_up"), ("w_d", "w_ffn_down")):
        shared[k_] = np.ascontiguousarray(np.asarray(inp[src], dtype=np.float32))
    in_maps = []
    per = ncores // NSMP
    for c in range(ncores):
        s = c // per
        m = _host_layout(inp, L, TSF, s, [NPR * c + i for i in range(NPR)], half=(c % 2 if pair else None))
        m.update(shared)
        in_maps.append(m)
    res = run_bass_kernel_spmd(nc, in_maps, core_ids=list(range(ncores)))
    y_prompt = np.zeros((NB, SEQP, D), np.float32)
    y_sample = np.zeros((NSMP, TSF, D), np.float32)
    new_state = np.zeros((NB, L, 2, NH, HD, HD), np.float32)
    for c in range(ncores):
        r = res.results[c]
        mir = pair and (c % 2 == 1)
        y = np.asarray(r["yT"]).T
        nst_c = np.asarray(r["nst"])
        for i in range(NPR):
            yp = y[TS + i * SEQP:TS + (i + 1) * SEQP]
            y_prompt[NPR * c + i] = yp[::-1] if mir else yp
            new_state[NPR * c + i] = nst_c[i][:, ::-1] if mir else nst_c[i]
        ys = y[:TS]
        if pair:
            h = c % 2
            y_sample[c // 2, h * TS:(h + 1) * TS] = ys[::-1] if mir else ys
        elif c % per == 0:
            y_sample[c // per] = ys
    return (y_prompt, y_sample, new_state)
```

```python
import contextlib
import numpy as np
import concourse.bass as bass
import concourse.mybir as mybir
from concourse.bass_utils import run_bass_kernel_spmd

F32 = mybir.dt.float32
BF16 = mybir.dt.bfloat16
AF = mybir.ActivationFunctionType
ALU = mybir.AluOpType
AX = mybir.AxisListType

D = 2048
NK = 16
CS = 3360
PIN = 5408
DFF = 5632
NF = 44
NH = 16
HD = 64
SEQP = 256
NPR = 2
RMS_EPS = 1e-6
GN_EPS = 64 * 1e-5
LN_EPS = 1e-5
CDEC = 0.6065306597126334

EPOCH = 12000
N_DMA_SEMS = 24


class Buf:
    __slots__ = ("name", "writers", "readers", "excl")

    def __init__(self, name="", excl=False):
        self.name = name
        self.writers = {}
        self.readers = {}
        self.excl = excl


class Prog:
    def __init__(self, nc):
        self.nc = nc
        self.es = contextlib.ExitStack()
        self.eng = {"pe": nc.tensor, "act": nc.scalar, "dve": nc.vector,
                    "pool": nc.gpsimd, "sp": nc.sync}
        self.sems = {e: [] for e in self.eng}
        self.cnt = {e: 0 for e in self.eng}
        self.seen = {e: {} for e in self.eng}
        self.dma_sems = {}
        self.dma_vals = {}
        self.dma_next = {}
        self.all_dma = {}
        self.n_ins = 0
        for q in ("sp", "pool", "cc"):
            n = N_DMA_SEMS if q != "cc" else 2
            self.dma_sems[q] = [self.es.enter_context(nc.semaphore(f"dq_{q}_{i}")) for i in range(n)]
            self.dma_vals[q] = [0] * n
            self.dma_next[q] = 0
        for e in self.eng:
            self._new_epoch(e)

    def _new_epoch(self, e):
        s = self.es.enter_context(self.nc.semaphore(f"s_{e}_{len(self.sems[e])}"))
        self.sems[e].append(s)
        self.cnt[e] = 0

    def _wait_list(self, e, events):
        need = {}
        for ev in events:
            if ev is None:
                continue
            if ev[0] == "eng":
                _, src, ep, c = ev
                if src == e and e == "pe":
                    continue
                key = ("eng", src, ep)
                sem = self.sems[src][ep]
            else:
                _, q, i, c = ev
                key = ("dma", q, i)
                sem = self.dma_sems[q][i]
            if self.seen[e].get(key, 0) >= c:
                continue
            if key not in need or need[key][1] < c:
                need[key] = (sem, c)
        for key, (sem, c) in need.items():
            self.seen[e][key] = c
        return list(need.values())

    @staticmethod
    def _deps(reads, writes):
        evs = []
        for b in reads:
            evs.extend(b.writers.values())
        for b in writes:
            evs.extend(b.writers.values())
            evs.extend(b.readers.values())
        return evs

    @staticmethod
    def _record(ev, key, reads, writes):
        for b in writes:
            b.writers = {key: ev}
            b.readers = {}
        for b in reads:
            b.readers[key] = ev

    def op(self, e, fn, reads=(), writes=(), inc=True, rec_reads=None):
        if any(b.excl for b in reads):
            writes = list(writes) + [b for b in reads if b.excl]
            reads = [b for b in reads if not b.excl]
        waits = self._wait_list(e, self._deps(reads, writes))
        eng = self.eng[e]
        for sem, c in waits[:-1]:
            eng.wait_ge(sem, c)
        ins = fn()
        if waits:
            ins._wait_ge(waits[-1][0], waits[-1][1])
        self.n_ins += 1
        if not inc:
            return None
        if self.cnt[e] >= EPOCH:
            self._new_epoch(e)
        self.cnt[e] += 1
        ep = len(self.sems[e]) - 1
        ins.then_inc(self.sems[e][ep], 1)
        ev = ("eng", e, ep, self.cnt[e])
        self._record(ev, e, reads if rec_reads is None else rec_reads, writes)
        return ev

    def dma(self, q, out, in_, reads=(), writes=(), join=(), **kw):
        eng = self.eng[q]
        i = self.dma_next[q]
        self.dma_next[q] = (i + 1) % N_DMA_SEMS
        prev = ("dma", q, i, self.dma_vals[q][i]) if self.dma_vals[q][i] else None
        waits = self._wait_list(q, self._deps(reads, writes) + [prev])
        for sem, c in waits:
            eng.wait_ge(sem, c)
        self.dma_vals[q][i] += 16
        v = self.dma_vals[q][i]
        eng.dma_start(out=out, in_=in_, **kw).then_inc(self.dma_sems[q][i], 16)
        self.n_ins += 1
        ev = ("dma", q, i, v)
        self.all_dma[(q, i)] = ev
        self._record(ev, ("dma", q, i, v), reads, writes)
        for b in join:
            b.writers[("dma", q, i, v)] = ev
        return ev

    def coll(self, ins_ap, outs_ap, groups, reads=(), writes=()):
        q = "cc"
        eng = self.eng["pool"]
        i = self.dma_next[q]
        self.dma_next[q] = (i + 1) % len(self.dma_sems[q])
        prev = ("dma", q, i, self.dma_vals[q][i]) if self.dma_vals[q][i] else None
        for sem, c in self._wait_list("pool", self._deps(reads, writes) + [prev]):
            eng.wait_ge(sem, c)
        self.dma_vals[q][i] += 1
        v = self.dma_vals[q][i]
        eng.collective_compute("AllGather", ALU.bypass, replica_groups=groups, ins=[ins_ap], outs=[outs_ap]) \
            .then_inc(self.dma_sems[q][i], 1)
        self.n_ins += 1
        ev = ("dma", q, i, v)
        self.all_dma[(q, i)] = ev
        self._record(ev, ("dma", q, i, v), reads, writes)
        return ev

    def barrier(self):
        evs = []
        for e in self.eng:
            ep = len(self.sems[e]) - 1
            if self.cnt[e] > 0:
                evs.append(("eng", e, ep, self.cnt[e]))
            if ep > 0:
                evs.append(("eng", e, ep - 1, EPOCH))
        evs.extend(self.all_dma.values())
        for e in self.eng:
            for sem, c in self._wait_list(e, evs):
                self.eng[e].wait_ge(sem, c)


class _Stop(Exception):
    pass


def build(L, TS, dbg=False, stop=None, pair=False, ncores=8):
    nc = bass.Bass("TRN2", target_bir_lowering=False)
    P = Prog(nc)
    TT = TS + NPR * SEQP
    NT5 = TT // 512
    NCH = TT // 128
    NW = NT5
    c_ = CDEC

    def din(name, shape, dt=F32):
        return nc.dram_tensor(name, list(shape), dt, kind="ExternalInput").ap()

    def dscr(name, shape, dt=F32):
        kind = "ExternalOutput" if dbg else "Internal"
        return nc.dram_tensor(name, list(shape), dt, kind=kind).ap()

    xT_in = din("xT", [D, TT])
    cvec = din("cvec", [128, NK, 2])
    state0 = din("state0", [L, 2, NH, HD, HD])
    w_mod = din("w_mod", [L, D, 6 * D])
    bmodT = din("bmodT", [128, L, 96])
    n1T = din("n1T", [128, L, NK])
    n2T = din("n2T", [128, L, NK])
    nfT = din("nfT", [128, NK])
    w_in = din("w_in", [L, D, PIN])
    muT = din("muT", [128, L, 27, 4])
    w0T = din("w0T", [128, L, 2, 8])
    a0T = din("a0T", [128, L, 2, 8])
    w2 = din("w2", [L, 2, 64, 1024])
    a2 = din("a2", [L, 2, 64, 1024])
    g2 = din("g2", [L, 160, 1024])
    kkT = din("kkT", [128, L, 8])
    kaT = din("kaT", [128, L, 8])
    rkT = din("rkT", [128, L, 8])
    gnw = din("gnw", [L, 1024])
    gnb = din("gnb", [L, 1024])
    lng = din("lng", [L, 1024])
    lnb = din("lnb", [L, 1024])
    wsT = din("wsT", [L, 128, 8, 128])
    bsp = din("bsp", [L, 1, 1024])
    w_out = din("w_out", [L, D, D])
    w_g = din("w_g", [L, D, DFF])
    w_u = din("w_u", [L, D, DFF])
    w_d = din("w_d", [L, DFF, D])
    msel = din("msel", [128, 2]) if pair else None
    GROUPS = [[2 * i, 2 * i + 1] for i in range(ncores // 2)]

    yT = nc.dram_tensor("yT", [D, TT], F32, kind="ExternalOutput").ap()
    nst = nc.dram_tensor("nst", [NPR, L, 2, NH, HD, HD], F32, kind="ExternalOutput").ap()

    XRES = dscr("XRES", [D, TT])
    PT = dscr("PT", [27 * 128, TT])
    UTd = dscr("UTd", [1024, TT], BF16)
    VTd = dscr("VTd", [TT, 1024])
    OT = dscr("OT", [D, TT], BF16)
    SCd = dscr("SCd", [NCH, 2, 8, 128, 4, 128], BF16)
    VBd = dscr("VBd", [NCH, 8, 128, 128], BF16)
    BTd = dscr("BTd", [1024, TT])
    GTd = dscr("GTd", [1024, TT])
    YGd = dscr("YGd", [2, TT, 1024])
    if pair:
        HSEND = nc.dram_tensor("HSEND", [27 * 128, 64], F32).ap()
        HGATH = nc.dram_tensor("HGATH", [2 * 27 * 128, 64], F32).ap()
        SSEND = nc.dram_tensor("SSEND", [128, 512], F32).ap()
        SGATH = nc.dram_tensor("SGATH", [2 * 128, 512], F32).ap()
        b_hsend, b_hgath, b_ssend, b_sgath = Buf(), Buf(), Buf(), Buf()

    XB = [[Buf() for _ in range(NT5)] for _ in range(NK)]
    PTB = [[Buf() for _ in range(NT5)] for _ in range(27)]
    UTB = [[Buf() for _ in range(NT5)] for _ in range(8)]
    VTB = [[Buf() for _ in range(8)] for _ in range(NT5)]
    OTB = [[Buf() for _ in range(NT5)] for _ in range(2)]
    SCB = [[[Buf() for _ in range(8)] for _ in range(2)] for _ in range(NW)]
    VBB = [[Buf() for _ in range(8)] for _ in range(NW)]
    BTB = [[Buf() for _ in range(8)] for _ in range(NW)]
    GTB = [[Buf() for _ in range(8)] for _ in range(NW)]
    YGB = [[Buf() for _ in range(NCH)] for _ in range(2)]

    es = P.es

    uid = {"n": 0}

    def sb(name, shape, dt=F32, ctx=None):
        uid["n"] += 1
        return (ctx or es).enter_context(nc.sbuf_tensor(f"{name}_{uid['n']}", list(shape), dt))

    PB = [es.enter_context(nc.psum_tensor(f"pb{i}", [128, 512], F32)) for i in range(8)]
    PBB = [Buf(f"pb{i}", excl=True) for i in range(8)]

    ones_f = sb("ones_f", [128, 128]); b_ones = Buf()
    blk_f = sb("blk_f", [128, 128]); b_blk = Buf()
    id_f = sb("id_f", [128, 128]); b_idf = Buf()
    id_b = sb("id_b", [128, 128], BF16); b_idb = Buf()
    ones_row = sb("ones_row", [1, 128], BF16); b_onesrow = Buf()
    MASK = [sb(f"mask{d}", [128, 512]) for d in range(2)]; b_mask = [Buf(), Buf()]
    NMASK = [sb(f"nmask{d}", [128, 128]) for d in range(2)]; b_nmask = [Buf(), Buf()]
    RST = sb("rst", [128, 512]); b_rst = Buf()

    V_, G_, A_, S_ = nc.vector, nc.gpsimd, nc.scalar, nc.sync

    P.op("dve", lambda: V_.memset(ones_f[:], 1.0), writes=[b_ones])
    P.op("dve", lambda: V_.memset(ones_row[:], 1.0), writes=[b_onesrow])
    P.op("dve", lambda: V_.memset(blk_f[:], 0.0), writes=[b_blk])
    P.op("dve", lambda: V_.memset(blk_f[0:64, 0:64], 1.0), reads=[b_blk], writes=[b_blk])
    P.op("dve", lambda: V_.memset(blk_f[64:128, 64:128], 1.0), reads=[b_blk], writes=[b_blk])
    P.op("dve", lambda: V_.memset(id_f[:], 0.0), writes=[b_idf])
    P.op("pool", lambda: G_.affine_select(out=id_f[:], in_=id_f[:], pattern=[[-1, 128]], compare_op=ALU.not_equal,
                                          fill=1.0, base=0, channel_multiplier=1), reads=[b_idf], writes=[b_idf])
    P.op("dve", lambda: V_.tensor_copy(id_b[:], id_f[:]), reads=[b_idf], writes=[b_idb])

    def tri(dst, cm, coef, base, cmp, bufs):
        P.op("dve", lambda: V_.memset(dst, 1.0), writes=bufs)
        P.op("pool", lambda: G_.affine_select(out=dst, in_=dst, pattern=[[coef, 128]], compare_op=cmp,
                                              fill=0.0, base=base, channel_multiplier=cm), reads=bufs, writes=bufs)
    for q in range(4):
        strict = (q % 2 == 0)
        tri(MASK[0][:, q * 128:(q + 1) * 128], -1, 1, 0, ALU.is_gt if strict else ALU.is_ge, [b_mask[0]])
        tri(MASK[1][:, q * 128:(q + 1) * 128], 1, -1, 0, ALU.is_gt if strict else ALU.is_ge, [b_mask[1]])
    tri(NMASK[0][:], 1, -1, 0, ALU.is_gt, [b_nmask[0]])
    tri(NMASK[1][:], -1, 1, 0, ALU.is_gt, [b_nmask[1]])
    P.op("dve", lambda: V_.memset(RST[:], 1.0), writes=[b_rst])
    for c in range(4):
        P.op("dve", lambda c=c: V_.memset(RST[:, c * 128:c * 128 + 1], 0.0), reads=[b_rst], writes=[b_rst])

    MODT = sb("modt", [128, L, 96, 2]); b_modt = Buf()
    SCL = sb("scl", [128, L, 2, NK, 2]); b_scl = Buf()
    N1 = sb("n1", [128, L, NK]); N2 = sb("n2", [128, L, NK]); NFt = sb("nf", [128, NK]); b_n = Buf()
    MU = sb("mu", [128, L, 27, 6]); b_mu = Buf()
    W0 = sb("w0", [128, L, 2, 8]); A0 = sb("a0", [128, L, 2, 8])
    KKp = sb("kkp", [128, L, 8]); KAp = sb("kap", [128, L, 8]); RKp = sb("rkp", [128, L, 8]); b_par = Buf()
    BMT = sb("bmt", [128, L, 96])
    CV = sb("cv", [128, NK, 2]); CSb = sb("csb", [128, NK, 2], BF16); b_cv = Buf()
    MU4 = sb("mu4", [128, L, 27, 4])

    P.dma("sp", N1[:], n1T, writes=[b_n])
    P.dma("sp", N2[:], n2T, writes=[b_n])
    P.dma("sp", NFt[:], nfT, writes=[b_n])
    P.dma("sp", MU4[:], muT, writes=[b_mu])
    P.dma("sp", W0[:], w0T, writes=[b_par])
    P.dma("sp", A0[:], a0T, writes=[b_par])
    P.dma("sp", KKp[:], kkT, writes=[b_par])
    P.dma("sp", KAp[:], kaT, writes=[b_par])
    P.dma("sp", RKp[:], rkT, writes=[b_par])
    P.dma("sp", BMT[:], bmodT, writes=[b_modt])
    P.dma("sp", CV[:], cvec, writes=[b_cv])
    if pair:
        MSEL = sb("msel_t", [128, 2]); b_msel = Buf()
        P.dma("sp", MSEL[:], msel, writes=[b_msel])
    P.op("act", lambda: A_.activation(out=CSb[:], in_=CV[:], func=AF.Silu), reads=[b_cv], writes=[b_cv])
    P.op("dve", lambda: V_.tensor_copy(MU[:, :, :, 0:4], MU4[:]), reads=[b_mu], writes=[b_mu])
    P.op("dve", lambda: V_.reduce_sum(MU[:, :, :, 4], MU4[:], axis=AX.X), reads=[b_mu], writes=[b_mu])
    P.op("dve", lambda: V_.tensor_scalar(MU[:, :, :, 4], MU[:, :, :, 4], -1.0, 1.0, ALU.mult, ALU.add), reads=[b_mu], writes=[b_mu])
    P.op("dve", lambda: V_.tensor_tensor(MU[:, :, :, 5], MU4[:, :, :, 0], MU4[:, :, :, 1], ALU.add), reads=[b_mu], writes=[b_mu])
    P.op("dve", lambda: V_.tensor_scalar(MU[:, :, :, 5], MU[:, :, :, 5], -1.0, 1.0, ALU.mult, ALU.add), reads=[b_mu], writes=[b_mu])

    with contextlib.ExitStack() as cx:
        WM = [sb(f"wm{i}", [128, NK, 1024], BF16, cx) for i in range(2)]
        b_wm = [Buf(), Buf()]
        pc = 0
        for l in range(L):
            wv = w_mod[l].rearrange("(k p) n -> p k n", p=128)
            for piece in range(12):
                t = pc % 2
                pc += 1
                P.dma("pool", WM[t][:], wv[:, :, piece * 1024:(piece + 1) * 1024], writes=[b_wm[t]])
                for jb in range(8):
                    j = piece * 8 + jb
                    for k in range(NK):
                        last = (k == NK - 1) and (jb == 7)
                        P.op("pe", lambda t=t, jb=jb, k=k, j=j: nc.tensor.matmul(
                            PB[0][:, 2 * j:2 * j + 2], WM[t][:, k, jb * 128:(jb + 1) * 128], CSb[:, k, :],
                            start=(k == 0), stop=(k == NK - 1)),
                            reads=[b_wm[t], b_cv], writes=[PBB[0]], inc=last)
            P.op("dve", lambda l=l: V_.tensor_tensor(
                MODT[:, l], PB[0][:, 0:192].rearrange("p (j v) -> p j v", v=2),
                BMT[:, l, :].unsqueeze(2).to_broadcast([128, 96, 2]), ALU.add),
                reads=[PBB[0], b_modt], writes=[b_modt])
            for wn, (Nt, off) in enumerate(((N1, 16), (N2, 64))):
                P.op("dve", lambda l=l, wn=wn, Nt=Nt, off=off: V_.scalar_tensor_tensor(
                    out=SCL[:, l, wn], in0=MODT[:, l, off:off + 16, :], scalar=1.0,
                    in1=Nt[:, l, :].unsqueeze(2).to_broadcast([128, NK, 2]), op0=ALU.add, op1=ALU.mult),
                    reads=[b_modt, b_n], writes=[b_scl])
        P.barrier()
    if stop == "M":
        P.barrier(); P.es.close(); return nc

    def modv(l, off, k, v):
        return MODT[:, l, off + k, v:v + 1]

    def super_tiles():
        res = []
        t0 = 0
        while t0 < TS:
            nt = 1024 if TS - t0 >= 1024 else 512
            res.append((t0, nt, 0))
            t0 += nt
        res.append((TS, 512, 1))
        return res

    STS = super_tiles()
    rr = {"pb": 0}

    def rms_gen(cx_tiles, src_dram, srcB, t5, dst_ht, dst_col0, b_dst, scale_fn, bias_fn, ssbank):
        XT_, b_xt, SQ_, b_sq, RS_, b_rs = cx_tiles
        P.dma("sp", XT_[:], src_dram.rearrange("(k p) t -> p k t", p=128)[:, :, t5 * 512:(t5 + 1) * 512],
              reads=[srcB[k][t5] for k in range(NK)], writes=[b_xt])
        yield
        for k in range(NK):
            s = k % 2
            P.op("act", lambda k=k, s=s: A_.activation(out=SQ_[s][:], in_=XT_[:, k, :], func=AF.Square),
                 reads=[b_xt], writes=[b_sq[s]])
            yield
            P.op("pe", lambda k=k, s=s: nc.tensor.matmul(PB[ssbank][:], ones_f[:], SQ_[s][:], start=(k == 0), stop=(k == NK - 1)),
                 reads=[b_sq[s], b_ones], writes=[PBB[ssbank]])
            yield
        P.op("dve", lambda: V_.tensor_scalar(RS_[:], PB[ssbank][:], 1.0 / D, RMS_EPS, ALU.mult, ALU.add),
             reads=[PBB[ssbank]], writes=[b_rs])
        yield
        P.op("act", lambda: A_.activation(out=RS_[:], in_=RS_[:], func=AF.Sqrt), reads=[b_rs], writes=[b_rs])
        yield
        P.op("dve", lambda: V_.reciprocal(RS_[:], RS_[:]), reads=[b_rs], writes=[b_rs])
        yield
        for k in range(NK):
            s = k % 2
            P.op("dve", lambda k=k, s=s: V_.tensor_tensor(SQ_[s][:], XT_[:, k, :], RS_[:], ALU.mult),
                 reads=[b_xt, b_rs, b_sq[s]], writes=[b_sq[s]])
            yield
            P.op("act", lambda k=k, s=s: A_.activation(out=dst_ht[:, k, dst_col0:dst_col0 + 512], in_=SQ_[s][:],
                                                        func=AF.Identity, scale=scale_fn(k), bias=bias_fn(k)),
                 reads=[b_sq[s], b_scl, b_modt, b_n], writes=[b_dst])
            yield

    def rms_tile(*a):
        for _ in rms_gen(*a):
            pass

    def chk(ph):
        if stop == ph:
            raise _Stop()

    try:
      for l in range(L):
          XSRC = xT_in if l == 0 else XRES
          with contextlib.ExitStack() as cx:
              XT_ = sb("a_xt", [128, NK, 512], F32, cx); b_xt = Buf()
              SQ_ = [sb(f"a_sq{i}", [128, 512], F32, cx) for i in range(2)]; b_sq = [Buf(), Buf()]
              RS_ = sb("a_rs", [128, 512], F32, cx); b_rs = Buf()
              HT2 = [sb(f"a_ht{i}", [128, NK, 1024], BF16, cx) for i in range(2)]; b_ht2 = [Buf(), Buf()]
              WIN = [sb(f"a_win{i}", [128, NK, 512], BF16, cx) for i in range(2)]; b_win = [Buf(), Buf()]
              STG = [sb(f"a_stg{i}", [128, 512], F32, cx) for i in range(4)]; b_stg = [Buf() for _ in range(4)]
              STB = [sb(f"a_stb{i}", [128, 512], BF16, cx) for i in range(2)]; b_stb = [Buf() for _ in range(2)]
              wv = w_in[l].rearrange("(k p) n -> p k n", p=128)
              pieces = [(c0, 512) for c0 in range(0, 3072, 512)] + [(3072, 288)] + \
                       [(3360, 512), (3872, 512), (4384, 512), (4896, 512)]
              wc = 0
              sg = 0
              sgb = 0
              def norm_gen(si):
                  (t0_, NT_, vs_) = STS[si]
                  for sub_ in range(NT_ // 512):
                      yield from rms_gen((XT_, b_xt, SQ_, b_sq, RS_, b_rs), XSRC, XB, t0_ // 512 + sub_, HT2[si % 2], sub_ * 512,
                                         b_ht2[si % 2], lambda k: SCL[:, l, 0, k, vs_:vs_ + 1], lambda k: modv(l, 0, k, vs_), 0)

              for _ in norm_gen(0):
                  pass
              for si, (t0, NT, vs) in enumerate(STS):
                  nsub = NT // 512
                  HT, b_ht = HT2[si % 2], b_ht2[si % 2]
                  ngen = [norm_gen(si + 1) if si + 1 < len(STS) else None]

                  def adv(n=2):
                      for _ in range(n):
                          if ngen[0] is not None and next(ngen[0], "done") == "done":
                              ngen[0] = None
                  for (c0, cw) in pieces:
                      wt = wc % 2
                      wc += 1
                      P.dma("pool", WIN[wt][:, :, 0:cw], wv[:, :, c0:c0 + cw], writes=[b_win[wt]])
                      if c0 < 4384:
                          nblk = (cw + 127) // 128
                          for bi in range(nblk):
                              bw = min(128, cw - bi * 128)
                              col = c0 + bi * 128
                              for sub in range(nsub):
                                  t5 = t0 // 512 + sub
                                  pb = 2 + rr["pb"] % 6
                                  rr["pb"] += 1
                                  for k in range(NK):
                                      P.op("pe", lambda k=k, pb=pb, bi=bi, bw=bw, wt=wt, sub=sub: nc.tensor.matmul(
                                          PB[pb][0:bw, :], WIN[wt][:, k, bi * 128:bi * 128 + bw], HT[:, k, sub * 512:(sub + 1) * 512],
                                          start=(k == 0), stop=(k == NK - 1)),
                                          reads=[b_win[wt], b_ht], writes=[PBB[pb]], inc=(k == NK - 1))
                                  if col < CS:
                                      jb = col // 128
                                      s = sg % 4
                                      sg += 1
                                      eng = "act" if (sg % 2) else "dve"
                                      if eng == "act":
                                          P.op("act", lambda s=s, pb=pb, bw=bw: A_.copy(STG[s][0:bw, :], PB[pb][0:bw, :]),
                                               reads=[PBB[pb]], writes=[b_stg[s]])
                                      else:
                                          P.op("dve", lambda s=s, pb=pb, bw=bw: V_.tensor_copy(STG[s][0:bw, :], PB[pb][0:bw, :]),
                                               reads=[PBB[pb]], writes=[b_stg[s]])
                                      P.dma("sp", PT[jb * 128:jb * 128 + bw, t5 * 512:(t5 + 1) * 512], STG[s][0:bw, :],
                                            reads=[b_stg[s]], writes=[PTB[jb][t5]])
                                      adv()
                                  else:
                                      g = (col - CS) // 128
                                      s = sgb % 2
                                      sgb += 1
                                      P.op("act", lambda s=s, pb=pb: A_.activation(out=STB[s][:], in_=PB[pb][:], func=AF.Gelu_apprx_tanh),
                                           reads=[PBB[pb]], writes=[b_stb[s]])
                                      P.dma("sp", UTd[g * 128:(g + 1) * 128, t5 * 512:(t5 + 1) * 512], STB[s][:],
                                            reads=[b_stb[s]], writes=[UTB[g][t5]])
                                      adv()
                      else:
                          vc0 = c0 - 4384
                          for sub in range(nsub):
                              t5 = t0 // 512 + sub
                              for ts in range(4):
                                  pb = 2 + rr["pb"] % 6
                                  rr["pb"] += 1
                                  for k in range(NK):
                                      P.op("pe", lambda k=k, pb=pb, wt=wt, sub=sub, ts=ts: nc.tensor.matmul(
                                          PB[pb][:], HT[:, k, sub * 512 + ts * 128:sub * 512 + (ts + 1) * 128], WIN[wt][:, k, :],
                                          start=(k == 0), stop=(k == NK - 1)),
                                          reads=[b_win[wt], b_ht], writes=[PBB[pb]], inc=(k == NK - 1))
                                  s = sg % 4
                                  sg += 1
                                  P.op("act", lambda s=s, pb=pb: A_.activation(out=STG[s][:], in_=PB[pb][:], func=AF.Gelu_apprx_tanh),
                                       reads=[PBB[pb]], writes=[b_stg[s]])
                                  tok0 = t5 * 512 + ts * 128
                                  P.dma("sp", VTd[tok0:tok0 + 128, vc0:vc0 + 512], STG[s][:],
                                        reads=[b_stg[s]], writes=[VTB[t5][ts * 2 + vc0 // 512]])
                                  adv()
                  adv(10 ** 6)
              P.barrier()
          if pair:
              P.dma("sp", HSEND, PT[:, TS - 64:TS], reads=[PTB[j][TS // 512 - 1] for j in range(27)], writes=[b_hsend])
              P.coll(HSEND, HGATH, GROUPS, reads=[b_hsend], writes=[b_hgath])
          chk("A")

          with contextlib.ExitStack() as cx:
              VT_ = sb("g_vt", [128, 4, 1024], F32, cx); b_vt = Buf()
              VC_ = sb("g_vc", [128, 4, 1024], F32, cx); b_vc = Buf()
              SQg = sb("g_sq", [128, 4, 1024], F32, cx); b_sqg = Buf()
              VN_ = sb("g_vn", [128, 4, 1024], BF16, cx); b_vn = Buf()
              UTt = sb("g_ut", [128, 8, 512], BF16, cx); b_ut = Buf()
              OBt = sb("g_ob", [128, 8, 512], BF16, cx); b_ob = Buf()
              LNG = sb("g_lng", [128, 1024], F32, cx); LNB = sb("g_lnb", [128, 1024], F32, cx); b_ln = Buf()
              WST = sb("g_wst", [128, 8, 128], BF16, cx); BSP = sb("g_bsp", [1, 1024], BF16, cx); b_ws = Buf()
              ST1 = sb("g_s1", [128, 32], F32, cx); ST2 = sb("g_s2", [128, 32], F32, cx); b_st = Buf()
              P.dma("sp", LNG[:], lng[l].partition_broadcast(128), writes=[b_ln])
              P.dma("sp", LNB[:], lnb[l].partition_broadcast(128), writes=[b_ln])
              P.dma("pool", WST[:], wsT[l], writes=[b_ws])
              P.dma("pool", BSP[:], bsp[l], writes=[b_ws])
              for t5 in range(NT5):
                  P.dma("sp", VT_[:], VTd[t5 * 512:(t5 + 1) * 512, :].rearrange("(a p) c -> p a c", p=128),
                        reads=VTB[t5], writes=[b_vt])
                  P.dma("sp", UTt[:], UTd[:, t5 * 512:(t5 + 1) * 512].rearrange("(g p) t -> p g t", p=128),
                        reads=[UTB[g][t5] for g in range(8)], writes=[b_ut])
                  V4 = VT_[:].rearrange("p a (g d) -> p (a g) d", d=128)
                  C4 = VC_[:].rearrange("p a (g d) -> p (a g) d", d=128)
                  Q4 = SQg[:].rearrange("p a (g d) -> p (a g) d", d=128)
                  P.op("dve", lambda: V_.reduce_sum(ST1[:], V4, axis=AX.X), reads=[b_vt], writes=[b_st])
                  P.op("dve", lambda: V_.tensor_scalar(ST1[:], ST1[:], 1.0 / 128, None, ALU.mult), reads=[b_st], writes=[b_st])
                  P.op("dve", lambda: V_.tensor_tensor(C4, V4, ST1[:].unsqueeze(2).to_broadcast([128, 32, 128]), ALU.subtract),
                       reads=[b_vt, b_st], writes=[b_vc])
                  P.op("act", lambda: A_.activation(out=SQg[:], in_=VC_[:], func=AF.Square), reads=[b_vc], writes=[b_sqg])
                  P.op("dve", lambda: V_.reduce_sum(ST2[:], Q4, axis=AX.X), reads=[b_sqg], writes=[b_st])
                  P.op("dve", lambda: V_.tensor_scalar(ST2[:], ST2[:], 1.0 / 128, LN_EPS, ALU.mult, ALU.add), reads=[b_st], writes=[b_st])
                  P.op("act", lambda: A_.activation(out=ST2[:], in_=ST2[:], func=AF.Sqrt), reads=[b_st], writes=[b_st])
                  P.op("dve", lambda: V_.reciprocal(ST2[:], ST2[:]), reads=[b_st], writes=[b_st])
                  P.op("dve", lambda: V_.tensor_tensor(C4, C4, ST2[:].unsqueeze(2).to_broadcast([128, 32, 128]), ALU.mult),
                       reads=[b_vc, b_st], writes=[b_vc])
                  P.op("pool", lambda: G_.tensor_tensor(VC_[:], VC_[:], LNG[:].unsqueeze(1).to_broadcast([128, 4, 1024]), ALU.mult),
                       reads=[b_vc, b_ln], writes=[b_vc])
                  P.op("dve", lambda: V_.tensor_tensor(VN_[:], VC_[:], LNB[:].unsqueeze(1).to_broadcast([128, 4, 1024]), ALU.add),
                       reads=[b_vc, b_ln], writes=[b_vn])
                  for g in range(8):
                      pb = g
                      for a in range(4):
                          P.op("pe", lambda a=a, g=g, pb=pb: nc.tensor.matmul(
                              PB[pb][:, a * 128:(a + 1) * 128], VN_[:, a, g * 128:(g + 1) * 128], WST[:, g, :], start=True, stop=False),
                              reads=[b_vn, b_ws], writes=[PBB[pb]], inc=False)
                          P.op("pe", lambda a=a, g=g, pb=pb: nc.tensor.matmul(
                              PB[pb][:, a * 128:(a + 1) * 128], ones_row[0:1, :], BSP[0:1, g * 128:(g + 1) * 128], start=False, stop=True),
                              reads=[b_onesrow, b_ws, b_vn], writes=[PBB[pb]], inc=(a == 3))
                      P.op("dve", lambda g=g, pb=pb: V_.tensor_tensor(OBt[:, g, :], PB[pb][:], UTt[:, g, :], ALU.mult),
                           reads=[PBB[pb], b_ut], writes=[b_ob])
                  P.dma("sp", OT[1024:2048, t5 * 512:(t5 + 1) * 512].rearrange("(g p) t -> p g t", p=128), OBt[:],
                        reads=[b_ob], writes=[OTB[1][t5]])
              P.barrier()
          chk("G")

          GAM = sb(f"gam{l}", [128, 2, 8, NCH], F32); b_gam = Buf()
          with contextlib.ExitStack() as cx:
              GIN = [sb(f"b_gin{i}", [128, 640], F32, cx) for i in range(3)]; b_gin = [Buf() for _ in range(3)]
              SH = sb("b_sh", [128, 27, 512], F32, cx); b_sh = [Buf() for _ in range(27)]
              WTb = sb("b_wt", [128, 512], BF16, cx); ADb = sb("b_ad", [128, 512], BF16, cx)
              SGb = sb("b_sg", [128, 512], BF16, cx); SG2b = sb("b_sg2", [128, 512], BF16, cx); b_lo = Buf()
              W2s = sb("b_w2", [128, 2, 1024], BF16, cx); A2s = sb("b_a2", [128, 2, 1024], BF16, cx)
              G2a = sb("b_g2a", [128, 1024], BF16, cx); G2b = sb("b_g2b", [128, 1024], BF16, cx); b_lw = Buf()
              ROLES = {}
              SCQ = [sb(f"b_scq{i}", [128, 4, 4, 128], BF16, cx) for i in range(2)]; b_scq = [Buf(), Buf()]
              if pair:
                  HSL = [sb(f"b_hsl{i}", [128, 27, 64], F32, cx) for i in range(2)]; b_hsl = Buf()
                  HALO = sb("b_halo", [128, 27, 64], F32, cx); b_halo = Buf()
              VBs = [sb(f"b_vbs{i}", [128, 512], BF16, cx) for i in range(2)]; b_vbs = [Buf(), Buf()]
              tmi = {"q": 0, "v": 0, "g": 0}

              def tmp(role, par):
                  key = (role, par % 2)
                  if key not in ROLES:
                      ROLES[key] = (sb(f"b_r_{role}{par % 2}", [128, 512], F32, cx), Buf())
                  return ROLES[key]

              P.dma("pool", W2s[0:64, :, :], w2[l].rearrange("d j c -> j d c"), writes=[b_lw])
              P.dma("pool", A2s[64:128, :, :], a2[l].rearrange("d j c -> j d c"), writes=[b_lw])
              P.dma("pool", G2a[:], g2[l, 0:128, :], writes=[b_lw])
              P.dma("pool", G2b[0:32, :], g2[l, 128:160, :], writes=[b_lw])

              for w in range(NW):
                  is_p = (w == NW - 1)
                  t0 = w * 512
                  W = 512
                  for j in range(27):
                      npar = 128 if j < 26 else 32
                      gi = j % 3
                      g_, bg = GIN[gi], b_gin[gi]
                      rows = slice(j * 128, j * 128 + npar)
                      if not is_p:
                          lo = t0 - 64
                          hi = t0 + W + 64
                          rd = [PTB[j][w]]
                          if lo < 0:
                              P.op("pool", lambda g_=g_, npar=npar: G_.memset(g_[0:npar, 0:64], 0.0), writes=[bg])
                              lo = 0
                          else:
                              rd.append(PTB[j][w - 1])
                          if hi > TS:
                              if pair:
                                  if j == 0:
                                      for sl_ in range(2):
                                          P.dma("sp", HSL[sl_][:], HGATH[sl_ * 3456:(sl_ + 1) * 3456, :].rearrange("(j p) t -> p j t", p=128),
                                                reads=[b_hgath], writes=[b_hsl])
                                      P.op("dve", lambda: V_.tensor_scalar(HALO[:], HSL[0][:], MSEL[:, 0:1], None, ALU.mult),
                                           reads=[b_hsl, b_msel], writes=[b_halo])
                                      P.op("dve", lambda: V_.scalar_tensor_tensor(out=HALO[:], in0=HSL[1][:], scalar=MSEL[:, 1:2], in1=HALO[:],
                                                                                  op0=ALU.mult, op1=ALU.add),
                                           reads=[b_hsl, b_msel, b_halo], writes=[b_halo])
                                  P.op("pool", lambda g_=g_, npar=npar, j=j: G_.tensor_copy(g_[0:npar, 576:640], HALO[0:npar, j, ::-1]),
                                       reads=[b_halo], writes=[bg])
                              else:
                                  P.op("pool", lambda g_=g_, npar=npar: G_.memset(g_[0:npar, 576:640], 0.0), writes=[bg])
                              hi = TS
                          else:
                              rd.append(PTB[j][w + 1])
                          P.dma("sp", g_[0:npar, lo - (t0 - 64):hi - (t0 - 64)], PT[rows, lo:hi], reads=rd, writes=[bg])
                          ctr = g_[0:npar, 64:64 + W]
                          acc = SH[0:npar, j, :]
                          P.op("act", lambda acc=acc, ctr=ctr, j=j, npar=npar: A_.activation(out=acc, in_=ctr, func=AF.Copy, scale=MU[0:npar, l, j, 4:5]),
                               reads=[bg, b_mu], writes=[b_sh[j]])
                          a3 = acc.rearrange("p (r c) -> p r c", c=64)
                          c3 = ctr.rearrange("p (r c) -> p r c", c=64)
                          P.op("dve", lambda a3=a3, c3=c3, j=j, npar=npar: V_.scalar_tensor_tensor(
                              out=a3[:, :, 1:64], in0=c3[:, :, 0:63], scalar=MU[0:npar, l, j, 0:1], in1=a3[:, :, 1:64], op0=ALU.mult, op1=ALU.add),
                              reads=[bg, b_mu, b_sh[j]], writes=[b_sh[j]])
                          P.op("dve", lambda a3=a3, c3=c3, j=j, npar=npar: V_.scalar_tensor_tensor(
                              out=a3[:, :, 0:63], in0=c3[:, :, 1:64], scalar=MU[0:npar, l, j, 1:2], in1=a3[:, :, 0:63], op0=ALU.mult, op1=ALU.add),
                              reads=[bg, b_mu, b_sh[j]], writes=[b_sh[j]])
                          P.op("dve", lambda acc=acc, g_=g_, j=j, npar=npar: V_.scalar_tensor_tensor(
                              out=acc, in0=g_[0:npar, 0:W], scalar=MU[0:npar, l, j, 2:3], in1=acc, op0=ALU.mult, op1=ALU.add),
                              reads=[bg, b_mu, b_sh[j]], writes=[b_sh[j]])
                          P.op("dve", lambda acc=acc, g_=g_, j=j, npar=npar: V_.scalar_tensor_tensor(
                              out=acc, in0=g_[0:npar, 128:128 + W], scalar=MU[0:npar, l, j, 3:4], in1=acc, op0=ALU.mult, op1=ALU.add),
                              reads=[bg, b_mu, b_sh[j]], writes=[b_sh[j]])
                      else:
                          P.dma("sp", g_[0:npar, 64:64 + W], PT[rows, t0:t0 + W], reads=[PTB[j][w]], writes=[bg])
                          ctr = g_[0:npar, 64:64 + W]
                          acc = SH[0:npar, j, :]
                          P.op("act", lambda acc=acc, ctr=ctr, j=j, npar=npar: A_.activation(out=acc, in_=ctr, func=AF.Copy, scale=MU[0:npar, l, j, 5:6]),
                               reads=[bg, b_mu], writes=[b_sh[j]])
                          a3 = acc.rearrange("p (r c) -> p r c", c=SEQP)
                          c3 = ctr.rearrange("p (r c) -> p r c", c=SEQP)
                          P.op("dve", lambda a3=a3, c3=c3, j=j, npar=npar: V_.scalar_tensor_tensor(
                              out=a3[:, :, 1:SEQP], in0=c3[:, :, 0:SEQP - 1], scalar=MU[0:npar, l, j, 0:1], in1=a3[:, :, 1:SEQP], op0=ALU.mult, op1=ALU.add),
                              reads=[bg, b_mu, b_sh[j]], writes=[b_sh[j]])
                          P.op("dve", lambda a3=a3, c3=c3, j=j, npar=npar: V_.scalar_tensor_tensor(
                              out=a3[:, :, 0:SEQP - 1], in0=c3[:, :, 1:SEQP], scalar=MU[0:npar, l, j, 1:2], in1=a3[:, :, 0:SEQP - 1], op0=ALU.mult, op1=ALU.add),
                              reads=[bg, b_mu, b_sh[j]], writes=[b_sh[j]])
                  P.op("act", lambda: A_.activation(out=WTb[0:64, :], in_=SH[0:64, 24, :], func=AF.Tanh), reads=[b_sh[24]], writes=[b_lo])
                  P.op("dve", lambda: V_.tensor_copy(ADb[64:128, :], SH[64:128, 24, :]), reads=[b_sh[24]], writes=[b_lo])
                  P.op("act", lambda: A_.activation(out=SGb[:], in_=SH[:, 25, :], func=AF.Sigmoid), reads=[b_sh[25]], writes=[b_lo])
                  P.op("act", lambda: A_.activation(out=SG2b[0:32, :], in_=SH[0:32, 26, :], func=AF.Sigmoid), reads=[b_sh[26]], writes=[b_lo])
                  def jbody(j):
                      jc = slice(j * 128, (j + 1) * 128)
                      Rj, Kj, Vj = SH[:, j, :], SH[:, 8 + j, :], SH[:, 16 + j, :]
                      bR, bK, bV = b_sh[j], b_sh[8 + j], b_sh[16 + j]
                      pbg = j % 2
                      P.op("pe", lambda: nc.tensor.matmul(PB[pbg][:], G2a[:, jc], SGb[:], start=True, stop=False),
                           reads=[b_lw, b_lo], writes=[PBB[pbg]], inc=False)
                      yield
                      P.op("pe", lambda: nc.tensor.matmul(PB[pbg][:], G2b[0:32, jc], SG2b[0:32, :], start=False, stop=True),
                           reads=[b_lw, b_lo], writes=[PBB[pbg]])
                      yield
                      tg, btg = tmp("tg", j)
                      P.op("act", lambda tg=tg: A_.copy(tg[:], PB[pbg][:]), reads=[PBB[pbg]], writes=[btg])
                      yield
                      P.dma("sp", GTd[jc, t0:t0 + W], tg[:], reads=[btg], writes=[GTB[w][j]])
                      yield
                      vi = tmi["v"] % 2
                      tmi["v"] += 1
                      P.op("pool", lambda vi=vi, Vj=Vj: G_.tensor_copy(VBs[vi][:], Vj), reads=[bV], writes=[b_vbs[vi]])
                      yield
                      P.dma("sp", VBd[w * 4:(w + 1) * 4, j].rearrange("c p t -> p c t"),
                            VBs[vi][:].rearrange("p (c t) -> p c t", t=128), reads=[b_vbs[vi]], writes=[VBB[w][j]])
                      yield
                      t1, bt1 = tmp("t1", j)
                      P.op("act", lambda t1=t1, Kj=Kj, j=j: A_.activation(out=t1[:], in_=Kj, func=AF.Copy, scale=KKp[:, l, j:j + 1]),
                           reads=[bK, b_par], writes=[bt1])
                      yield
                      t2, bt2 = tmp("t2", j)
                      P.op("act", lambda t1=t1, t2=t2: A_.activation(out=t2[:], in_=t1[:], func=AF.Square), reads=[bt1], writes=[bt2])
                      yield
                      P.op("pe", lambda t2=t2: nc.tensor.matmul(PB[2][:], blk_f[:], t2[:], start=True, stop=True),
                           reads=[bt2, b_blk], writes=[PBB[2]])
                      yield
                      P.op("dve", lambda t2=t2: V_.tensor_scalar_max(t2[:], PB[2][:], 1e-24), reads=[PBB[2]], writes=[bt2])
                      yield
                      P.op("act", lambda t2=t2: A_.activation(out=t2[:], in_=t2[:], func=AF.Sqrt), reads=[bt2], writes=[bt2])
                      yield
                      P.op("dve", lambda t2=t2: V_.reciprocal(t2[:], t2[:]), reads=[bt2], writes=[bt2])
                      yield
                      kkn, bkkn = tmp("kkn", j)
                      P.op("pool", lambda kkn=kkn, t1=t1, t2=t2: G_.tensor_tensor(kkn[:], t1[:], t2[:], ALU.mult),
                           reads=[bt1, bt2], writes=[bkkn])
                      yield
                      yield "HEAD_DONE"
                      def dbody(d):
                          pz = 3 + d
                          pa = 5 + d
                          P.op("pe", lambda d=d, pz=pz: nc.tensor.matmul(PB[pz][:], W2s[0:64, d, jc], WTb[0:64, :], start=True, stop=True),
                               reads=[b_lw, b_lo], writes=[PBB[pz]])
                          yield
                          lws, blws = tmp("lws", d)
                          P.op("act", lambda lws=lws, pz=pz, d=d, j=j: A_.activation(out=lws[:], in_=PB[pz][:], func=AF.Sigmoid, bias=W0[:, l, d, j:j + 1]),
                               reads=[PBB[pz], b_par], writes=[blws])
                          yield
                          P.op("pe", lambda d=d, pa=pa: nc.tensor.matmul(PB[pa][:], A2s[64:128, d, jc], ADb[64:128, :], start=True, stop=True),
                               reads=[b_lw, b_lo], writes=[PBB[pa]])
                          yield
                          alr, balr = tmp("alr", d)
                          P.op("act", lambda alr=alr, pa=pa, d=d, j=j: A_.activation(out=alr[:], in_=PB[pa][:], func=AF.Sigmoid, bias=A0[:, l, d, j:j + 1]),
                               reads=[PBB[pa], b_par], writes=[balr])
                          yield
                          kd, bkd = tmp("kd", d)
                          P.op("dve", lambda kd=kd, alr=alr, j=j: V_.tensor_scalar(kd[:], alr[:], -1.0, KAp[:, l, j:j + 1], ALU.add, ALU.mult),
                               reads=[balr, b_par], writes=[bkd])
                          yield
                          P.op("dve", lambda kd=kd, Kj=Kj: V_.scalar_tensor_tensor(out=kd[:], in0=kd[:], scalar=1.0, in1=Kj, op0=ALU.add, op1=ALU.mult),
                               reads=[bkd, bK], writes=[bkd])
                          yield
                          bv, bbv = tmp("bv", d)
                          P.op("pool", lambda bv=bv, kkn=kkn, alr=alr: G_.tensor_tensor(bv[:], kkn[:], alr[:], ALU.mult),
                               reads=[bkkn, balr], writes=[bbv])
                          yield
                          t4, bt4 = tmp("t4", d)
                          P.op("dve", lambda t4=t4, Rj=Rj, kd=kd, j=j: V_.scalar_tensor_tensor(out=t4[:], in0=Rj, scalar=RKp[:, l, j:j + 1], in1=kd[:], op0=ALU.mult, op1=ALU.mult),
                               reads=[bR, bkd, b_par], writes=[bt4])
                          yield
                          P.op("pe", lambda t4=t4, d=d: nc.tensor.matmul(PB[7][:], blk_f[:], t4[:], start=(d == 0), stop=(d == 1)),
                               reads=[bt4, b_blk], writes=[PBB[7]])
                          yield
                          cs, bcs = tmp("cs", d)
                          P.op("dve", lambda cs=cs, lws=lws: V_.tensor_tensor_scan(cs[:], RST[:], lws[:], 0.0, ALU.mult, ALU.add),
                               reads=[blws, b_rst], writes=[bcs])
                          yield
                          ge, bge = tmp("ge", d)
                          gi_, bgi = tmp("gi", d)
                          cs3 = cs[:].rearrange("p (c t) -> p c t", t=128)
                          if d == 0:
                              P.op("pool", lambda ge=ge, cs=cs, lws=lws: G_.tensor_tensor(ge[:], cs[:], lws[:], ALU.subtract),
                                   reads=[bcs, blws], writes=[bge])
                              yield
                              GI, bGI = cs, bcs
                          else:
                              P.op("dve", lambda ge=ge, cs3=cs3: V_.tensor_tensor(
                                  ge[:].rearrange("p (c t) -> p c t", t=128), cs3[:, :, 127:128].to_broadcast([128, 4, 128]), cs3, ALU.subtract),
                                  reads=[bcs], writes=[bge])
                              yield
                              P.op("pool", lambda gi_=gi_, ge=ge, lws=lws: G_.tensor_tensor(gi_[:], ge[:], lws[:], ALU.add),
                                   reads=[bge, blws], writes=[bgi])
                              yield
                              GI, bGI = gi_, bgi
                          P.op("act", lambda cs3=cs3, d=d, j=j, w=w: A_.activation(out=GAM[:, d, j, w * 4:(w + 1) * 4], in_=cs3[:, :, 127], func=AF.Exp, scale=-c_),
                               reads=[bcs], writes=[b_gam])
                          yield
                          e1, be1 = tmp("e1", d)
                          e2, be2 = tmp("e2", d)
                          e3, be3 = tmp("e3", d)
                          P.op("act", lambda e1=e1, GI=GI: A_.activation(out=e1[:], in_=GI[:], func=AF.Exp, scale=-c_), reads=[bGI], writes=[be1])
                          yield
                          P.op("act", lambda e2=e2, ge=ge: A_.activation(out=e2[:], in_=ge[:], func=AF.Exp, scale=-c_), reads=[bge], writes=[be2])
                          yield
                          P.op("act", lambda e3=e3, GI=GI: A_.activation(out=e3[:], in_=GI[:], func=AF.Exp, scale=c_), reads=[bGI], writes=[be3])
                          yield
                          qi = d
                          q_, bq = SCQ[qi], b_scq[qi]
                          c3_ = lambda ap: ap.rearrange("p (c t) -> p c t", t=128)
                          P.op("dve", lambda q_=q_, kkn=kkn, e2=e2: V_.scalar_tensor_tensor(out=q_[:, :, 0, :], in0=c3_(kkn[:]), scalar=-1.0, in1=c3_(e2[:]), op0=ALU.mult, op1=ALU.mult),
                               reads=[bkkn, be2], writes=[bq])
                          yield
                          P.op("pool", lambda q_=q_, Rj=Rj, e1=e1: G_.tensor_tensor(q_[:, :, 1, :], c3_(Rj), c3_(e1[:]), ALU.mult),
                               reads=[bR, be1, bq], writes=[bq])
                          yield
                          P.op("dve", lambda q_=q_, bv=bv, e3=e3: V_.tensor_tensor(q_[:, :, 2, :], c3_(bv[:]), c3_(e3[:]), ALU.mult),
                               reads=[bbv, be3, bq], writes=[bq])
                          yield
                          P.op("pool", lambda q_=q_, kd=kd, e3=e3: G_.tensor_tensor(q_[:, :, 3, :], c3_(kd[:]), c3_(e3[:]), ALU.mult),
                               reads=[bkd, be3, bq], writes=[bq])
                          yield
                          P.dma("sp", SCd[w * 4:(w + 1) * 4, d, j].rearrange("c p q t -> p c (q t)"),
                                q_[:].rearrange("p c q t -> p c (q t)"), reads=[bq], writes=[SCB[w][d][j]])
                          yield

                      gens_ = [dbody(0), dbody(1)]
                      while gens_:
                          nx_ = []
                          for g__ in gens_:
                              try:
                                  next(g__)
                                  nx_.append(g__)
                              except StopIteration:
                                  pass
                          gens_ = nx_
                          yield
                      tb, btb = tmp("tb", j)
                      P.op("dve", lambda tb=tb, Vj=Vj: V_.tensor_tensor(tb[:], PB[7][:], Vj, ALU.mult), reads=[PBB[7], bV], writes=[btb])
                      P.dma("sp", BTd[jc, t0:t0 + W], tb[:], reads=[btb], writes=[BTB[w][j]])

                  cur = jbody(0)
                  while next(cur) != "HEAD_DONE":
                      pass
                  for j in range(8):
                      nxt = jbody(j + 1) if j < 7 else None
                      nxt_head = nxt is not None
                      cur_live = True
                      while cur_live or nxt_head:
                          if cur_live:
                              try:
                                  next(cur)
                              except StopIteration:
                                  cur_live = False
                          if nxt_head:
                              if next(nxt) == "HEAD_DONE":
                                  nxt_head = False
                      cur = nxt
              P.barrier()
          chk("B1")

          with contextlib.ExitStack() as cx:
              NSC = 4
              SCT = [sb(f"s_sct{i}", [128, 8, 4, 128], BF16, cx) for i in range(NSC)]; b_sct = [Buf() for _ in range(NSC)]
              VBT = [sb(f"s_vbt{i}", [128, 8, 128], BF16, cx) for i in range(NSC)]; b_vbt = [Buf() for _ in range(NSC)]
              HS = [sb(f"s_hs{d}", [128, 8, 64], F32, cx) for d in range(2)]
              HBt = [sb(f"s_hb{d}", [128, 8, 64], BF16, cx) for d in range(2)]
              b_hs = [[Buf() for _ in range(NH)] for _ in range(2)]
              YS = [sb(f"s_ys{i}", [128, NH, 64], F32, cx) for i in range(2)]; b_ys = [Buf(), Buf()]
              YC = sb("s_yc", [128, NH, 64], F32, cx); b_yc = Buf()
              YQ = sb("s_yq", [128, NH, 64], F32, cx); b_yq = Buf()
              YO = [sb(f"s_yo{i}", [128, 1024], F32, cx) for i in range(2)]; b_yo = [Buf(), Buf()]
              GS1 = sb("s_gs1", [128, NH], F32, cx); GS2 = sb("s_gs2", [128, NH], F32, cx); b_gs = Buf()
              GNW = sb("s_gnw", [128, 1024], F32, cx); GNB = sb("s_gnb", [128, 1024], F32, cx); b_gn = Buf()
              SV = sb("s_sv", [64, NH, 64], F32, cx); b_sv = Buf()
              P.dma("sp", GNW[:], gnw[l].partition_broadcast(128), writes=[b_gn])
              P.dma("sp", GNB[:], gnb[l].partition_broadcast(128), writes=[b_gn])
              NS = 4
              TOK = [[sb(f"s_tok{s}_{hp}", [128, 3, 128], BF16, cx) for hp in range(2)] for s in range(NS)]
              b_tok = [[Buf(), Buf()] for _ in range(NS)]
              VTK = [sb(f"s_vtk{s}", [128, 64], BF16, cx) for s in range(NS)]; b_vtk = [Buf() for _ in range(NS)]
              SCM = [sb(f"s_scm{s}", [128, 512], BF16, cx) for s in range(NS)]; b_scm = [Buf() for _ in range(NS)]
              NMt = [sb(f"s_nm{s}", [128, 128], BF16, cx) for s in range(NS)]; b_nm = [Buf() for _ in range(NS)]
              XS = [sb(f"s_xs{s}", [128, 64], BF16, cx) for s in range(NS)]; b_xs = [Buf() for _ in range(NS)]
              QS = [sb(f"s_qs{s}", [128, 256], F32, cx) for s in range(NS)]; b_qs = [Buf() for _ in range(NS)]
              PW = [sb(f"s_pw{s}", [128, 128], F32, cx) for s in range(NS)]; b_pw = [Buf() for _ in range(NS)]
              MTb = [sb(f"s_mtb{s}", [128, 128], BF16, cx) for s in range(NS)]; b_mtb = [Buf() for _ in range(NS)]
              AH = [sb(f"s_ah{s}", [128, 128], BF16, cx) for s in range(NS)]; b_ah = [Buf() for _ in range(NS)]
              US = [sb(f"s_us{s}", [128, 64], BF16, cx) for s in range(NS)]; b_us = [Buf() for _ in range(NS)]
              TH = [sb(f"s_th{s}", [128, 64], F32, cx) for s in range(NS)]; b_th = [Buf() for _ in range(NS)]
              for s in range(NS):
                  for hp in range(2):
                      P.op("dve", lambda s=s, hp=hp: V_.memset(TOK[s][hp][:], 0.0), writes=[b_tok[s][hp]])
              PSTv = [PB[2 * s][:].bitcast(BF16) for s in range(NS)]
              sci = {"i": 0, "y": 0}

              def group(s, h, d, chg, sct, bsct, vbt, bvbt, ys, bys):
                  j, hp = h // 2, h % 2
                  P0 = hp * 64
                  hs = slice(P0, P0 + 64)
                  PSA, bPSA = PB[2 * s], PBB[2 * s]
                  PSC = PB[2 * s + 1]
                  PSD, bPSD = PSC, PBB[2 * s + 1]
                  PST, bPST = PSTv[s], bPSA
                  bN = bPSD
                  bX = bU = bY = bH = bPSA
                  R_N = PSC[:, 384:512]
                  R_X = PSA[:, 0:64]
                  R_U = PSA[:, 64:128]
                  R_Y = PSA[:, 128:192]
                  R_H = PSA[:, 192:256]
                  aT = sct[hs, j, 0, :]
                  rT = sct[hs, j, 1, :]
                  bT = sct[hs, j, 2, :]
                  kT = sct[hs, j, 3, :]
                  arT = sct[hs, j, 0:2, :].rearrange("p q t -> p (q t)")
                  vT = vbt[hs, j, :]
                  idh = id_b[hs, P0:P0 + 64]
                  tok, btok = TOK[s][hp], b_tok[s][hp]
                  for qi, src in enumerate((aT, bT, kT)):
                      P.op("pe", lambda qi=qi, src=src: nc.tensor.transpose(PST[:, qi * 64:(qi + 1) * 64], src, idh),
                           reads=[bsct, b_idb], writes=[bPST], inc=False)
                  P.op("pe", lambda: nc.tensor.transpose(PST[:, 192:256], vT, idh), reads=[bvbt, b_idb], writes=[bPST])
                  yield
                  P.op("act", lambda: A_.copy(tok[:, :, P0:P0 + 64], PST[:, 0:192].rearrange("p (q c) -> p q c", c=64)),
                       reads=[bPST], writes=[btok])
                  P.op("dve", lambda: V_.tensor_copy(VTK[s][:], PST[:, 192:256]), reads=[bPST], writes=[b_vtk[s]])
                  P.op("pe", lambda: nc.tensor.matmul(PSA[:, 0:256], bT, arT, start=True, stop=True), reads=[bsct], writes=[bPSA], inc=False)
                  P.op("pe", lambda: nc.tensor.matmul(PSA[:, 256:512], kT, arT, start=True, stop=True), reads=[bsct], writes=[bPSA])
                  P.op("pe", lambda: nc.tensor.matmul(R_N, aT, bT, start=True, stop=True), reads=[bsct], writes=[bN])
                  yield
                  P.op("dve", lambda: V_.tensor_tensor(QS[s][:, 0:128], PSA[:, 0:128], MASK[d][:, 0:128], ALU.mult),
                       reads=[bPSA, b_mask[d]], writes=[b_qs[s]])
                  P.op("dve", lambda: V_.tensor_tensor(SCM[s][:, 128:512], PSA[:, 128:512], MASK[d][:, 128:512], ALU.mult),
                       reads=[bPSA, b_mask[d]], writes=[b_scm[s]])
                  P.op("dve", lambda: V_.tensor_tensor(PW[s][:], R_N, NMASK[d][:], ALU.mult), reads=[bN, b_nmask[d]], writes=[b_pw[s]])
                  P.op("pe", lambda: nc.tensor.matmul(R_X, SCM[s][:, 256:384], VTK[s][:], start=True, stop=True),
                       reads=[b_scm[s], b_vtk[s]], writes=[bX])
                  P.op("pe", lambda: nc.tensor.matmul(PSD[:, 0:128], PW[s][:], QS[s][:, 0:128], start=True, stop=True),
                       reads=[b_pw[s], b_qs[s]], writes=[bPSD], inc=False)
                  P.op("pe", lambda: nc.tensor.matmul(PSD[:, 256:384], QS[s][:, 0:128], PW[s][:], start=True, stop=True),
                       reads=[b_pw[s], b_qs[s]], writes=[bPSD])
                  yield
                  P.op("act", lambda: A_.copy(XS[s][:], R_X), reads=[bX], writes=[b_xs[s]])
                  P.op("pool", lambda: G_.tensor_tensor(QS[s][:, 128:256], QS[s][:, 0:128], id_f[:], ALU.add),
                       reads=[b_qs[s], b_idf], writes=[b_qs[s]])
                  P.op("act", lambda: A_.copy(QS[s][:, 0:128], PSD[:, 0:128]), reads=[bPSD, b_qs[s]], writes=[b_qs[s]])
                  P.op("act", lambda: A_.copy(PW[s][:], PSD[:, 256:384]), reads=[bPSD], writes=[b_pw[s]])
                  for lev in range(1, 6):
                      P.op("pe", lambda: nc.tensor.matmul(PSD[:, 0:256], PW[s][:], QS[s][:], start=True, stop=True),
                           reads=[b_pw[s], b_qs[s]], writes=[bPSD], inc=False)
                      P.op("pe", lambda: nc.tensor.matmul(PSD[:, 256:384], QS[s][:, 0:128], PW[s][:], start=True, stop=True),
                           reads=[b_pw[s], b_qs[s]], writes=[bPSD])
                      yield
                      P.op("act", lambda: A_.copy(QS[s][:, 0:128], PSD[:, 0:128]), reads=[bPSD], writes=[b_qs[s]])
                      P.op("dve", lambda: V_.tensor_tensor(QS[s][:, 128:256], PSD[:, 128:256], QS[s][:, 128:256], ALU.add),
                           reads=[bPSD, b_qs[s]], writes=[b_qs[s]])
                      P.op("act", lambda: A_.copy(PW[s][:], PSD[:, 256:384]), reads=[bPSD], writes=[b_pw[s]])
                  P.op("pe", lambda: nc.tensor.matmul(PSD[:, 128:256], PW[s][:], QS[s][:, 128:256], start=True, stop=True),
                       reads=[b_pw[s], b_qs[s]], writes=[bPSD])
                  yield
                  P.op("dve", lambda: V_.tensor_tensor(MTb[s][:], PSD[:, 128:256], QS[s][:, 128:256], ALU.add),
                       reads=[bPSD, b_qs[s]], writes=[b_mtb[s]])
                  MT = MTb[s][:]
                  P.op("pe", lambda: nc.tensor.matmul(R_U, MT, XS[s][:], start=True, stop=False),
                       reads=[b_mtb[s], b_xs[s]], writes=[bU], inc=False)
                  P.op("pe", lambda: nc.tensor.matmul(R_N, tok[:, 0, :], MT, start=True, stop=True),
                       reads=[btok, b_mtb[s]], writes=[bN])
                  yield
                  P.op("act", lambda: A_.copy(AH[s][hs, :], PSC[hs, 384:512]), reads=[bN], writes=[b_ah[s]])
                  hb = HBt[d][hs, j, :]
                  P.op("pe", lambda: nc.tensor.matmul(R_U, AH[s][hs, :], hb, start=False, stop=True),
                       reads=[b_ah[s], b_hs[d][h]], writes=[bU])
                  yield
                  P.op("act", lambda: A_.copy(US[s][:], R_U), reads=[bU], writes=[b_us[s]])
                  P.op("pe", lambda: nc.tensor.matmul(R_Y, rT, hb, start=True, stop=False),
                       reads=[bsct, b_hs[d][h]], writes=[bY], inc=False)
                  P.op("pe", lambda: nc.tensor.matmul(R_Y, SCM[s][:, 128:256], US[s][:], start=False, stop=False),
                       reads=[b_scm[s], b_us[s]], writes=[bY], inc=False)
                  P.op("pe", lambda: nc.tensor.matmul(R_Y, SCM[s][:, 384:512], VTK[s][:], start=False, stop=True),
                       reads=[b_scm[s], b_vtk[s]], writes=[bY], rec_reads=[bsct, b_hs[d][h], b_scm[s], b_us[s], b_vtk[s]])
                  P.op("pe", lambda: nc.tensor.matmul(R_H, tok[:, 1, :], US[s][:], start=True, stop=False),
                       reads=[btok, b_us[s]], writes=[bH], inc=False)
                  P.op("pe", lambda: nc.tensor.matmul(R_H, tok[:, 2, :], VTK[s][:], start=False, stop=True),
                       reads=[btok, b_vtk[s]], writes=[bH], rec_reads=[btok, b_us[s], b_vtk[s]])
                  yield
                  P.op("dve", lambda: V_.tensor_copy(ys[:, h, :], R_Y), reads=[bY], writes=[bys])
                  P.op("dve", lambda: V_.tensor_tensor(TH[s][hs, :], PSA[hs, 192:256], HS[d][hs, j, :], ALU.add),
                       reads=[bH, b_hs[d][h]], writes=[b_th[s]])
                  gam = GAM[hs, d, j, chg:chg + 1]
                  P.op("dve", lambda: V_.tensor_scalar(HS[d][hs, j, :], TH[s][hs, :], gam, None, ALU.mult),
                       reads=[b_th[s], b_gam], writes=[b_hs[d][h]])
                  P.op("act", lambda: A_.activation(out=HBt[d][hs, j, :], in_=TH[s][hs, :], func=AF.Copy, scale=gam),
                       reads=[b_th[s], b_gam, b_hs[d][h]], writes=[b_hs[d][h]])

              def run_pair(gens):
                  live = list(gens)
                  while live:
                      nxt = []
                      for g in live:
                          if stop and stop.startswith("B2:"):
                              sci["steps"] = sci.get("steps", 0) + 1
                              if sci["steps"] > int(stop.split(":")[1]):
                                  raise _Stop()
                          try:
                              next(g)
                              nxt.append(g)
                          except StopIteration:
                              pass
                      live = nxt

              def gn_gen(d, chg, ys, bys):
                  yi = sci["y"] % 2
                  sci["y"] += 1
                  yo, byo = YO[yi], b_yo[yi]
                  P.op("dve", lambda: V_.reduce_sum(GS1[:], ys[:], axis=AX.X), reads=[bys], writes=[b_gs])
                  yield
                  P.op("dve", lambda: V_.tensor_scalar(GS1[:], GS1[:], 1.0 / 64, None, ALU.mult), reads=[b_gs], writes=[b_gs])
                  yield
                  P.op("dve", lambda: V_.tensor_tensor(YC[:], ys[:], GS1[:].unsqueeze(2).to_broadcast([128, NH, 64]), ALU.subtract),
                       reads=[bys, b_gs], writes=[b_yc])
                  yield
                  P.op("act", lambda: A_.activation(out=YQ[:], in_=YC[:], func=AF.Square), reads=[b_yc], writes=[b_yq])
                  yield
                  P.op("dve", lambda: V_.reduce_sum(GS2[:], YQ[:], axis=AX.X), reads=[b_yq], writes=[b_gs])
                  yield
                  P.op("dve", lambda: V_.tensor_scalar(GS2[:], GS2[:], 1.0 / 64, GN_EPS, ALU.mult, ALU.add), reads=[b_gs], writes=[b_gs])
                  yield
                  P.op("act", lambda: A_.activation(out=GS2[:], in_=GS2[:], func=AF.Sqrt), reads=[b_gs], writes=[b_gs])
                  yield
                  P.op("dve", lambda: V_.reciprocal(GS2[:], GS2[:]), reads=[b_gs], writes=[b_gs])
                  yield
                  P.op("dve", lambda: V_.tensor_tensor(YC[:], YC[:], GS2[:].unsqueeze(2).to_broadcast([128, NH, 64]), ALU.mult),
                       reads=[b_yc, b_gs], writes=[b_yc])
                  yield
                  ycf = YC[:].rearrange("p h v -> p (h v)")
                  P.op("pool", lambda: G_.tensor_tensor(ycf, ycf, GNW[:], ALU.mult), reads=[b_yc, b_gn], writes=[b_yc])
                  yield
                  P.op("pool", lambda: G_.tensor_tensor(yo[:], ycf, GNB[:], ALU.add), reads=[b_yc, b_gn], writes=[byo])
                  yield
                  P.dma("sp", YGd[d, chg * 128:(chg + 1) * 128, :], yo[:], reads=[byo], writes=[YGB[d][chg]])
                  yield

              def init_zero(d):
                  P.op("dve", lambda: V_.memset(HS[d][:], 0.0), reads=[b for b in b_hs[d]], writes=[b for b in b_hs[d]])
                  P.op("dve", lambda: V_.memset(HBt[d][:], 0.0), reads=[b for b in b_hs[d]], writes=[b for b in b_hs[d]])

              def init_dram(d):
                  P.dma("sp", SV[:], state0[l, d].rearrange("h v k -> v h k"), writes=[b_sv])
                  for j in range(8):
                      pb = 6 + j % 2
                      P.op("pe", lambda: nc.tensor.transpose(
                          PB[pb][:, 0:64], SV[:, 2 * j:2 * j + 2, :].rearrange("v h k -> v (h k)"), id_f[0:64, 0:64]),
                          reads=[b_sv, b_idf], writes=[PBB[pb]])
                      P.op("dve", lambda: V_.tensor_copy(HS[d][:, j, :], PB[pb][:, 0:64]),
                           reads=[PBB[pb]], writes=[b_hs[d][2 * j], b_hs[d][2 * j + 1]])
                      P.op("act", lambda: A_.copy(HBt[d][:, j, :], PB[pb][:, 0:64]),
                           reads=[PBB[pb], b_hs[d][2 * j], b_hs[d][2 * j + 1]], writes=[b_hs[d][2 * j], b_hs[d][2 * j + 1]])

              def send_state(d):
                  P.dma("sp", SSEND, HS[d][:].rearrange("p j v -> p (j v)"), reads=[b for b in b_hs[d]], writes=[b_ssend])
                  P.coll(SSEND, SGATH, GROUPS, reads=[b_ssend], writes=[b_sgath])

              def init_recv(d):
                  for sl_ in range(2):
                      P.dma("sp", RCV[sl_][:], SGATH[sl_ * 128:(sl_ + 1) * 128, :], reads=[b_sgath], writes=[b_rcv])
                  hsf = HS[d][:].rearrange("p j v -> p (j v)")
                  P.op("dve", lambda: V_.tensor_scalar(hsf, RCV[0][:], MSEL[:, 0:1], None, ALU.mult),
                       reads=[b_rcv, b_msel] + b_hs[d], writes=b_hs[d])
                  P.op("dve", lambda: V_.scalar_tensor_tensor(out=hsf, in0=RCV[1][:], scalar=MSEL[:, 1:2], in1=hsf, op0=ALU.mult, op1=ALU.add),
                       reads=[b_rcv, b_msel] + b_hs[d], writes=b_hs[d])
                  P.op("act", lambda: A_.copy(HBt[d][:].rearrange("p j v -> p (j v)"), hsf), reads=b_hs[d], writes=b_hs[d])

              def run_chunks(ch0, ncs, dirs):
                  for i in range(ncs):
                      for d in dirs:
                          ch = i if d == 0 else ncs - 1 - i
                          chg = ch0 + ch
                          w = chg // 4
                          si = sci["i"] % NSC
                          sci["i"] += 1
                          sct, bsct, vbt, bvbt = SCT[si], b_sct[si], VBT[si], b_vbt[si]
                          P.dma("sp", sct[:], SCd[chg, d].rearrange("j p q t -> p j q t"),
                                reads=[SCB[w][d][j] for j in range(8)], writes=[bsct])
                          P.dma("sp", vbt[:], VBd[chg].rearrange("j p t -> p j t"),
                                reads=[VBB[w][j] for j in range(8)], writes=[bvbt])
                          ysi = (sci["i"]) % 2
                          ys, bys = YS[ysi], b_ys[ysi]
                          for m in range(NH // NS):
                              gl = [group(q, NS * m + q, d, chg, sct, bsct, vbt, bvbt, ys, bys) for q in range(NS)]
                              if m == 0 and sci.get("gn") is not None:
                                  gl.append(sci["gn"])
                                  sci["gn"] = None
                              run_pair(gl)
                          sci["gn"] = gn_gen(d, chg, ys, bys)

              def final_states(pi):
                  for d in range(2):
                      for j in range(8):
                          pb = 6 + j % 2
                          P.op("pe", lambda: nc.tensor.transpose(PB[pb][0:64, 0:128], HS[d][:, j, :], id_f[:]),
                               reads=[b_hs[d][2 * j], b_hs[d][2 * j + 1], b_idf], writes=[PBB[pb]])
                          P.op("dve", lambda: V_.tensor_copy(
                              SV[:, 2 * j:2 * j + 2, :].rearrange("v h k -> v (h k)"), PB[pb][0:64, 0:128]),
                              reads=[PBB[pb]], writes=[b_sv])
                      P.dma("sp", nst[pi, l, d].rearrange("h v k -> v h k"), SV[:], reads=[b_sv], writes=[])

              prompts = [((TS + pi * SEQP) // 128, SEQP // 128, pi) for pi in range(NPR)]
              if pair:
                  RCV = [sb(f"s_rcv{i}", [128, 512], F32, cx) for i in range(2)]; b_rcv = Buf()
                  init_dram(0)
                  run_chunks(0, TS // 128, (0,))
                  send_state(0)
              else:
                  init_dram(0)
                  init_dram(1)
                  run_chunks(0, TS // 128, (0, 1))
              for (ch0, ncs, pi) in prompts:
                  init_zero(0)
                  init_zero(1)
                  run_chunks(ch0, ncs, (0, 1))
                  final_states(pi)
              if pair:
                  init_recv(1)
                  run_chunks(0, TS // 128, (1,))
              if sci.get("gn") is not None:
                  for _ in sci["gn"]:
                      pass
                  sci["gn"] = None
              P.barrier()
          chk("B2")

          with contextlib.ExitStack() as cx:
              YA = sb("c_ya", [128, 4, 1024], F32, cx); b_ya = Buf()
              YBt = sb("c_yb", [128, 4, 1024], F32, cx); b_yb = Buf()
              BTt = sb("c_bt", [128, 8, 512], F32, cx); b_bt = Buf()
              GTt = sb("c_gt", [128, 8, 512], F32, cx); b_gt = Buf()
              OA = sb("c_oa", [128, 8, 512], BF16, cx); b_oa = Buf()
              TMq = [sb(f"c_tm{i}", [128, 512], F32, cx) for i in range(2)]; b_tmq = [Buf(), Buf()]
              for w in range(NW):
                  t0 = w * 512
                  P.dma("sp", YA[:], YGd[0, t0:t0 + 512, :].rearrange("(a p) c -> p a c", p=128),
                        reads=[YGB[0][w * 4 + a] for a in range(4)], writes=[b_ya])
                  P.dma("sp", YBt[:], YGd[1, t0:t0 + 512, :].rearrange("(a p) c -> p a c", p=128),
                        reads=[YGB[1][w * 4 + a] for a in range(4)], writes=[b_yb])
                  P.dma("sp", BTt[:], BTd[:, t0:t0 + 512].rearrange("(j p) t -> p j t", p=128),
                        reads=[BTB[w][j] for j in range(8)], writes=[b_bt])
                  P.dma("sp", GTt[:], GTd[:, t0:t0 + 512].rearrange("(j p) t -> p j t", p=128),
                        reads=[GTB[w][j] for j in range(8)], writes=[b_gt])
                  P.op("pool", lambda: G_.tensor_tensor(YA[:], YA[:], YBt[:], ALU.add), reads=[b_ya, b_yb], writes=[b_ya])
                  for j in range(8):
                      pb = j
                      for a in range(4):
                          P.op("pe", lambda a=a, j=j, pb=pb: nc.tensor.transpose(PB[pb][:, a * 128:(a + 1) * 128], YA[:, a, j * 128:(j + 1) * 128], id_f[:]),
                               reads=[b_ya, b_idf], writes=[PBB[pb]], inc=(a == 3))
                      q = j % 2
                      P.op("dve", lambda j=j, pb=pb, q=q: V_.tensor_tensor(TMq[q][:], PB[pb][:], BTt[:, j, :], ALU.add),
                           reads=[PBB[pb], b_bt], writes=[b_tmq[q]])
                      P.op("pool", lambda j=j, q=q: G_.tensor_tensor(OA[:, j, :], TMq[q][:], GTt[:, j, :], ALU.mult),
                           reads=[b_tmq[q], b_gt], writes=[b_oa])
                  P.dma("sp", OT[0:1024, t0:t0 + 512].rearrange("(j p) t -> p j t", p=128), OA[:], reads=[b_oa], writes=[OTB[0][w]])
              P.barrier()
          chk("B3")

          with contextlib.ExitStack() as cx:
              BIG = sb("d_big", [128, NF, 1024], BF16, cx); b_big = [Buf() for _ in range(NF)]
              H2T = sb("d_h2t", [128, NK, 1024], BF16, cx); b_h2 = Buf()
              WR = [sb(f"d_wr{i}", [128, 8192], BF16, cx) for i in range(2)]; b_wr = [Buf(), Buf()]
              XC = [sb(f"d_xc{i}", [128, 512], F32, cx) for i in range(3)]; b_xc = [Buf() for _ in range(3)]
              XN = [sb(f"d_xn{i}", [128, 512], F32, cx) for i in range(3)]; b_xn = [Buf() for _ in range(3)]
              SQc = [sb(f"d_sq{i}", [128, 512], F32, cx) for i in range(2)]; b_sqc = [Buf(), Buf()]
              RSc = sb("d_rs", [128, 1024], F32, cx); b_rsc = Buf()
              SGc = [sb(f"d_sg{i}", [128, 512], F32, cx) for i in range(2)]; b_sgc = [Buf(), Buf()]
              ci = {"w": 0, "x": 0, "n": 0, "q": 0, "s": 0, "pb": 0}
              wo_v = w_out[l].rearrange("(k p) n -> p k n", p=128)
              wg_v = w_g[l].rearrange("(k p) n -> p k n", p=128)
              wu_v = w_u[l].rearrange("(k p) n -> p k n", p=128)
              wd_v = w_d[l].rearrange("(k p) n -> p k n", p=128)
              for (t0, NT, vs) in STS:
                  nsub = NT // 512
                  for k in range(NK):
                      half = 0 if k < 8 else 1
                      P.dma("sp", BIG[:, k, 0:NT], OT[k * 128:(k + 1) * 128, t0:t0 + NT],
                            reads=[OTB[half][t0 // 512 + s_] for s_ in range(nsub)], writes=[b_big[k]])
                  for dq in range(4):
                      wi = ci["w"] % 2
                      ci["w"] += 1
                      wr = WR[wi][:].rearrange("p (k n) -> p k n", n=512)
                      P.dma("pool", wr, wo_v[:, :, dq * 512:(dq + 1) * 512], writes=[b_wr[wi]])
                      for db in range(4):
                          kd_ = dq * 4 + db
                          for sub in range(nsub):
                              t5 = t0 // 512 + sub
                              pb = 2 + ci["pb"] % 6
                              ci["pb"] += 1
                              for k in range(NK):
                                  P.op("pe", lambda k=k, pb=pb, db=db, sub=sub, wr=wr: nc.tensor.matmul(
                                      PB[pb][:], wr[:, k, db * 128:(db + 1) * 128], BIG[:, k, sub * 512:(sub + 1) * 512],
                                      start=(k == 0), stop=(k == NK - 1)),
                                      reads=[b_wr[wi], b_big[k]], writes=[PBB[pb]], inc=(k == NK - 1),
                                      rec_reads=[b_wr[wi]] + b_big[0:NK])
                              xi = ci["x"] % 3
                              ci["x"] += 1
                              P.dma("sp", XC[xi][:], XSRC[kd_ * 128:(kd_ + 1) * 128, t5 * 512:(t5 + 1) * 512],
                                    reads=[XB[kd_][t5]], writes=[b_xc[xi]])
                              ni = ci["n"] % 3
                              ci["n"] += 1
                              P.op("dve", lambda pb=pb, xi=xi, ni=ni, kd_=kd_: V_.scalar_tensor_tensor(
                                  out=XN[ni][:], in0=PB[pb][:], scalar=modv(l, 32, kd_, vs), in1=XC[xi][:], op0=ALU.mult, op1=ALU.add),
                                  reads=[PBB[pb], b_xc[xi], b_modt], writes=[b_xn[ni]])
                              P.dma("sp", XRES[kd_ * 128:(kd_ + 1) * 128, t5 * 512:(t5 + 1) * 512], XN[ni][:],
                                    reads=[b_xn[ni]], writes=[XB[kd_][t5]])
                              qi = ci["q"] % 2
                              ci["q"] += 1
                              P.op("act", lambda ni=ni, qi=qi: A_.activation(out=SQc[qi][:], in_=XN[ni][:], func=AF.Square),
                                   reads=[b_xn[ni]], writes=[b_sqc[qi]])
                              P.op("pe", lambda qi=qi, sub=sub, kd_=kd_: nc.tensor.matmul(
                                  PB[sub][:], ones_f[:], SQc[qi][:], start=(kd_ == 0), stop=(kd_ == NK - 1)),
                                  reads=[b_sqc[qi], b_ones], writes=[PBB[sub]])
                  for sub in range(nsub):
                      rs = RSc[:, sub * 512:(sub + 1) * 512]
                      P.op("dve", lambda rs=rs, sub=sub: V_.tensor_scalar(rs, PB[sub][:], 1.0 / D, RMS_EPS, ALU.mult, ALU.add),
                           reads=[PBB[sub]], writes=[b_rsc])
                      P.op("act", lambda rs=rs: A_.activation(out=rs, in_=rs, func=AF.Sqrt), reads=[b_rsc], writes=[b_rsc])
                      P.op("dve", lambda rs=rs: V_.reciprocal(rs, rs), reads=[b_rsc], writes=[b_rsc])
                  for k in range(NK):
                      for sub in range(nsub):
                          t5 = t0 // 512 + sub
                          xi = ci["x"] % 3
                          ci["x"] += 1
                          P.dma("sp", XC[xi][:], XRES[k * 128:(k + 1) * 128, t5 * 512:(t5 + 1) * 512],
                                reads=[XB[k][t5]], writes=[b_xc[xi]])
                          qi = ci["q"] % 2
                          ci["q"] += 1
                          P.op("dve", lambda xi=xi, qi=qi, sub=sub: V_.tensor_tensor(SQc[qi][:], XC[xi][:], RSc[:, sub * 512:(sub + 1) * 512], ALU.mult),
                               reads=[b_xc[xi], b_rsc], writes=[b_sqc[qi]])
                          P.op("act", lambda qi=qi, k=k, sub=sub: A_.activation(
                              out=H2T[:, k, sub * 512:(sub + 1) * 512], in_=SQc[qi][:], func=AF.Identity,
                              scale=SCL[:, l, 1, k, vs:vs + 1], bias=modv(l, 48, k, vs)),
                              reads=[b_sqc[qi], b_scl, b_modt], writes=[b_h2])
                  for fp in range(NF // 2):
                      wi = ci["w"] % 2
                      ci["w"] += 1
                      wr = WR[wi][:].rearrange("p (m k n) -> p m k n", m=2, n=256)
                      P.dma("pool", wr[:, 0], wg_v[:, :, fp * 256:(fp + 1) * 256], writes=[b_wr[wi]])
                      P.dma("pool", wr[:, 1], wu_v[:, :, fp * 256:(fp + 1) * 256], join=[b_wr[wi]])
                      for fb in range(2):
                          f = fp * 2 + fb
                          for sub in range(nsub):
                              pg = 2 + ci["pb"] % 6
                              ci["pb"] += 1
                              pu = 2 + ci["pb"] % 6
                              ci["pb"] += 1
                              for m, pb in ((0, pg), (1, pu)):
                                  for k in range(NK):
                                      P.op("pe", lambda k=k, pb=pb, m=m, fb=fb, sub=sub, wr=wr: nc.tensor.matmul(
                                          PB[pb][:], wr[:, m, k, fb * 128:(fb + 1) * 128], H2T[:, k, sub * 512:(sub + 1) * 512],
                                          start=(k == 0), stop=(k == NK - 1)),
                                          reads=[b_wr[wi], b_h2], writes=[PBB[pb]], inc=(k == NK - 1))
                              si = ci["s"] % 2
                              ci["s"] += 1
                              P.op("act", lambda si=si, pg=pg: A_.activation(out=SGc[si][:], in_=PB[pg][:], func=AF.Silu),
                                   reads=[PBB[pg]], writes=[b_sgc[si]])
                              P.op("dve", lambda si=si, pu=pu, f=f, sub=sub: V_.tensor_tensor(
                                  BIG[:, f, sub * 512:(sub + 1) * 512], PB[pu][:], SGc[si][:], ALU.mult),
                                  reads=[PBB[pu], b_sgc[si]], writes=[b_big[f]])
                  for kd_ in range(NK):
                      wi = ci["w"] % 2
                      ci["w"] += 1
                      wr = WR[wi][:, 0:NF * 128].rearrange("p (k n) -> p k n", n=128)
                      P.dma("pool", wr, wd_v[:, :, kd_ * 128:(kd_ + 1) * 128], writes=[b_wr[wi]])
                      for sub in range(nsub):
                          t5 = t0 // 512 + sub
                          pb = 2 + ci["pb"] % 6
                          ci["pb"] += 1
                          for f in range(NF):
                              P.op("pe", lambda f=f, pb=pb, sub=sub, wr=wr: nc.tensor.matmul(
                                  PB[pb][:], wr[:, f, :], BIG[:, f, sub * 512:(sub + 1) * 512], start=(f == 0), stop=(f == NF - 1)),
                                  reads=[b_wr[wi], b_big[f]], writes=[PBB[pb]], inc=(f == NF - 1),
                                  rec_reads=[b_wr[wi]] + b_big)
                          xi = ci["x"] % 3
                          ci["x"] += 1
                          P.dma("sp", XC[xi][:], XRES[kd_ * 128:(kd_ + 1) * 128, t5 * 512:(t5 + 1) * 512],
                                reads=[XB[kd_][t5]], writes=[b_xc[xi]])
                          ni = ci["n"] % 3
                          ci["n"] += 1
                          P.op("dve", lambda pb=pb, xi=xi, ni=ni, kd_=kd_: V_.scalar_tensor_tensor(
                              out=XN[ni][:], in0=PB[pb][:], scalar=modv(l, 80, kd_, vs), in1=XC[xi][:], op0=ALU.mult, op1=ALU.add),
                              reads=[PBB[pb], b_xc[xi], b_modt], writes=[b_xn[ni]])
                          P.dma("sp", XRES[kd_ * 128:(kd_ + 1) * 128, t5 * 512:(t5 + 1) * 512], XN[ni][:],
                                reads=[b_xn[ni]], writes=[XB[kd_][t5]])
              P.barrier()
          chk("C")

    except _Stop:
        P.barrier()
        return nc

    with contextlib.ExitStack() as cx:
        XT_ = sb("f_xt", [128, NK, 512], F32, cx); b_xt = Buf()
        SQ_ = [sb(f"f_sq{i}", [128, 512], F32, cx) for i in range(2)]; b_sq = [Buf(), Buf()]
        RS_ = sb("f_rs", [128, 512], F32, cx); b_rs = Buf()
        YT_ = [sb(f"f_yt{i}", [128, NK, 512], F32, cx) for i in range(2)]; b_yt = [Buf(), Buf()]
        for t5 in range(NT5):
            yi = t5 % 2
            rms_tile((XT_, b_xt, SQ_, b_sq, RS_, b_rs), XRES, XB, t5, YT_[yi], 0, b_yt[yi],
                     lambda k: NFt[:, k:k + 1], lambda k: 0.0, t5 % 2)
            P.dma("sp", yT.rearrange("(k p) t -> p k t", p=128)[:, :, t5 * 512:(t5 + 1) * 512], YT_[yi][:],
                  reads=[b_yt[yi]], writes=[])
        P.barrier()

    P.barrier()
    P.es.close()
    return nc


def _host_layout(inp, L, TS, core_sample, core_prompts, half=None):
    f = lambda a: np.ascontiguousarray(np.asarray(a, dtype=np.float32))
    mir = (half == 1)
    xs = np.asarray(inp["x_sample"][core_sample], np.float32)
    if half is not None:
        tsl = xs.shape[0] // 2
        xs = xs[half * tsl:(half + 1) * tsl]
    xp = [np.asarray(inp["x_prompt"][p], np.float32) for p in core_prompts]
    if mir:
        xs = xs[::-1]
        xp = [a[::-1] for a in xp]
    x = np.concatenate([xs] + xp, axis=0)
    m = {}
    m["xT"] = f(x.T)
    c = np.asarray(inp["c"][core_sample], np.float32)
    cc = np.asarray(inp["c_ctx"], np.float32)
    cv = np.stack([c.reshape(NK, 128).T, cc.reshape(NK, 128).T], axis=-1)
    m["cvec"] = f(cv)
    st = np.asarray(inp["state_rwkv"])[core_sample]
    m["state0"] = f(st[:, ::-1] if mir else st)

    def pT(a, n):
        a = np.asarray(a, np.float32).reshape(L, n, 128)
        return f(a.transpose(2, 0, 1))
    m["bmodT"] = pT(inp["b_mod"], 96)
    m["n1T"] = pT(inp["norm1_g"], NK)
    m["n2T"] = pT(inp["norm2_g"], NK)
    m["nfT"] = f(np.asarray(inp["final_norm_g"], np.float32).reshape(NK, 128).T)
    mu = np.asarray(inp["mu_shift"], np.float32)
    if mir:
        mu = mu[:, [1, 0, 3, 2], :]
    mup = np.zeros((L, 4, 27 * 128), np.float32)
    mup[:, :, :CS] = mu
    m["muT"] = f(mup.reshape(L, 4, 27, 128).transpose(3, 0, 2, 1))
    dsw = (lambda a: a[:, ::-1]) if mir else (lambda a: a)
    m["w0T"] = f(dsw(np.asarray(inp["w0"], np.float32)).reshape(L, 2, 8, 128).transpose(3, 0, 1, 2))
    m["a0T"] = f(dsw(np.asarray(inp["a0"], np.float32)).reshape(L, 2, 8, 128).transpose(3, 0, 1, 2))
    m["w2"] = f(dsw(np.asarray(inp["w2"], np.float32)))
    m["a2"] = f(dsw(np.asarray(inp["a2"], np.float32)))
    m["kkT"] = pT(inp["k_k"], 8)
    m["kaT"] = pT(inp["k_a"], 8)
    m["rkT"] = pT(np.asarray(inp["r_k"]).reshape(L, 1024), 8)
    m["gnw"] = f(np.asarray(inp["gn_w"]).reshape(L, 1024))
    m["gnb"] = f(np.asarray(inp["gn_b"]).reshape(L, 1024))
    m["lng"] = f(np.asarray(inp["gmlp_ln_g"]).reshape(L, 1024))
    m["lnb"] = f(np.asarray(inp["gmlp_ln_b"]).reshape(L, 1024))
    ws = np.asarray(inp["w_spatial"], np.float32)
    bs = np.asarray(inp["b_spatial"], np.float32)
    if mir:
        ws = ws[:, :, ::-1, ::-1]
        bs = bs[:, :, ::-1]
    m["wsT"] = f(ws.transpose(0, 3, 1, 2))
    m["bsp"] = f(bs.reshape(L, 1, 1024))
    if half is not None:
        sel = np.zeros((128, 2), np.float32)
        sel[:, 1 - half] = 1.0
        m["msel"] = sel
    return m


def kernel(**inp):
    L = int(np.asarray(inp["w_in"]).shape[0])
    TSF = int(np.asarray(inp["x_sample"]).shape[1])
    NB = int(np.asarray(inp["x_prompt"]).shape[0])
    NSMP = int(np.asarray(inp["x_sample"]).shape[0])
    ncores = NB // NPR
    pair = (ncores == 2 * NSMP) and (TSF % 1024 == 0)
    TS = TSF // 2 if pair else TSF
    nc = build(L, TS, pair=pair, ncores=ncores)
    shared = {}
    for k_, src in (("w_mod", "w_mod"), ("w_in", "w_in"), ("g2", "g2"),
                    ("w_out", "w_out"), ("w_g", "w_ffn_gate"), ("w_u", "w_ffn_up"), ("w_d", "w_ffn_down")):
        shared[k_] = np.ascontiguousarray(np.asarray(inp[src], dtype=np.float32))
    in_maps = []
    per = ncores // NSMP
    for c in range(ncores):
        s = c // per
        m = _host_layout(inp, L, TSF, s, [NPR * c + i for i in range(NPR)], half=(c % 2 if pair else None))
        m.update(shared)
        in_maps.append(m)
    res = run_bass_kernel_spmd(nc, in_maps, core_ids=list(range(ncores)))
    y_prompt = np.zeros((NB, SEQP, D), np.float32)
    y_sample = np.zeros((NSMP, TSF, D), np.float32)
    new_state = np.zeros((NB, L, 2, NH, HD, HD), np.float32)
    for c in range(ncores):
        r = res.results[c]
        mir = pair and (c % 2 == 1)
        y = np.asarray(r["yT"]).T
        nst_c = np.asarray(r["nst"])
        for i in range(NPR):
            yp = y[TS + i * SEQP:TS + (i + 1) * SEQP]
            y_prompt[NPR * c + i] = yp[::-1] if mir else yp
            new_state[NPR * c + i] = nst_c[i][:, ::-1] if mir else nst_c[i]
        ys = y[:TS]
        if pair:
            h = c % 2
            y_sample[c // 2, h * TS:(h + 1) * TS] = ys[::-1] if mir else ys
        elif c % per == 0:
            y_sample[c // per] = ys
    return (y_prompt, y_sample, new_state)
```
